# Optimizing a Trainium2 kernel written in Bass

```python
import jax
import jax.numpy as jnp
from jax import lax
import numpy as np

D_MODEL = 1024
BATCH = 4
SEQ = 4096
DEPTH = 2
DEC_BATCH = 128
DEC_SEQ = 8
PAST_LEN = 2048
PAGE_SIZE = 128

N_EVEN = (DEPTH + 1) // 2
N_ODD = DEPTH // 2
EPS = 1e-6
F32 = jnp.float32
A_HEADS = 4
A_DK = 128
A_DV = 128
A_CONV = 4
A_CHUNK = 64
A_QK = A_HEADS * A_DK
A_V = A_HEADS * A_DV
A_CONV_DIM = 2 * A_QK + A_V
A_IN = A_CONV_DIM + A_V + 2 * A_HEADS
B_WIDTH = D_MODEL // 2
B_CONV = 3
B_IN = 3 * B_WIDTH
C_HEADS = 8
C_KV_HEADS = 2
C_HD = 64
C_CMP_STRIDE = 16
C_CMP_LEN = 2 * C_CMP_STRIDE
C_CMP_HID = 2 * C_HD
C_SEL_LEN = 64
C_TOPN = 16
C_WINDOW = 512
C_QBLK = 128
C_Q = C_HEADS * C_HD
C_GATES = 3 * C_HEADS
C_IN = C_Q + C_GATES + 3 * 2 * C_KV_HEADS * C_HD
D_GROUPS = 8
D_WIDTH = D_MODEL // 2
D_CHUNK = 128
D_IN = 2 * D_WIDTH
D_FF = 2816
EVEN_IN = A_IN + B_IN
EVEN_OUT = A_V + B_WIDTH
ODD_IN = C_IN + D_IN
ODD_OUT = C_Q + D_WIDTH
BIG = 1e4
NEG = -1e30

kernel_name = 'hybrid_gdn_shortconv_nsa_chunkmlp_step'


def rmsnorm(x, g):
    xf = x.astype(F32)
    y = xf * lax.rsqrt(jnp.mean(xf * xf, axis=-1, keepdims=True) + EPS)
    return (y * g.astype(F32)).astype(x.dtype)


def l2norm(x):
    x = x.astype(F32)
    return x * lax.rsqrt(jnp.sum(x * x, axis=-1, keepdims=True) + EPS)


def swiglu(x, w_gate, w_up, w_down):
    return (jax.nn.silu(x @ w_gate) * (x @ w_up)) @ w_down


def masked_softmax(s, mask):
    s = jnp.where(mask, s, NEG)
    m = jnp.max(s, axis=-1, keepdims=True)
    e = jnp.where(mask, jnp.exp(s - m), 0.0)
    return e / jnp.maximum(jnp.sum(e, axis=-1, keepdims=True), 1e-30)


def alibi_slopes():
    return 2.0 ** (-8.0 * jnp.arange(1, C_HEADS + 1, dtype=F32) / C_HEADS)


def causal_dwconv(x, buf, w):
    T = x.shape[1]
    K = w.shape[0]
    xp = jnp.concatenate([buf.astype(x.dtype), x], axis=1)
    y = sum(xp[:, i:i + T] * w[i] for i in range(K))
    return y, xp[:, T:]


def gated_delta_rule(q, k, v, g, beta, s0):
    B, T, H, _ = q.shape
    C = A_CHUNK if T >= A_CHUNK else T
    pad = (-T) % C
    padt = lambda a: jnp.pad(a, [(0, 0), (0, pad)] + [(0, 0)] * (a.ndim - 2))
    N = (T + pad) // C
    chunks = lambda a: jnp.moveaxis(padt(a).astype(F32).reshape((B, N, C) + a.shape[2:]), 3, 1)
    qc, kc, vc = chunks(q), chunks(k), chunks(v)
    gc = jnp.cumsum(chunks(g), axis=-1)
    bc = chunks(beta)
    tri_incl = jnp.tril(jnp.ones((C, C), bool))
    tri_strict = jnp.tril(jnp.ones((C, C), bool), -1)
    dmask = jnp.where(tri_incl, jnp.exp(jnp.where(tri_incl, gc[..., :, None] - gc[..., None, :], 0.0)), 0.0)
    kb = kc * bc[..., None]
    vb = vc * bc[..., None]
    a_mat = jnp.where(tri_strict, jnp.einsum('bhncd,bhnsd->bhncs', kb, kc) * dmask, 0.0)
    eye = jnp.eye(C, dtype=F32)
    t_mat = lax.linalg.triangular_solve(eye + a_mat, jnp.broadcast_to(eye, a_mat.shape),
                                        left_side=True, lower=True, unit_diagonal=True)
    u = jnp.einsum('bhncs,bhnsd->bhncd', t_mat, vb)
    w = jnp.einsum('bhncs,bhnsd->bhncd', t_mat, kb * jnp.exp(gc)[..., None])
    intra = jnp.where(tri_incl, jnp.einsum('bhncd,bhnsd->bhncs', qc, kc) * dmask, 0.0)
    g_last = gc[..., -1]
    k_tail = kc * jnp.exp(g_last[..., None] - gc)[..., None]
    q_dec = qc * jnp.exp(gc)[..., None]

    def step(s, inp):
        w_n, u_n, qd_n, in_n, kt_n, gl_n = inp
        v_new = u_n - jnp.einsum('bhcd,bhde->bhce', w_n, s)
        o = jnp.einsum('bhcd,bhde->bhce', qd_n, s) + jnp.einsum('bhcs,bhse->bhce', in_n, v_new)
        s = s * jnp.exp(gl_n)[..., None, None] + jnp.einsum('bhcd,bhce->bhde', kt_n, v_new)
        return s, o

    xs = tuple(jnp.moveaxis(a, 2, 0) for a in (w, u, q_dec, intra, k_tail, g_last))
    s_fin, o = lax.scan(step, s0.astype(F32), xs)
    o = jnp.transpose(o, (1, 0, 3, 2, 4)).reshape(B, N * C, H, -1)[:, :T]
    return o, s_fin


def gdn_mixer(p, conv_buf, s0, w_conv, a_log, dt_bias, g_norm):
    B, T, _ = p.shape
    qkv, conv_new = causal_dwconv(p[..., :A_CONV_DIM], conv_buf, w_conv)
    qkv = jax.nn.silu(qkv.astype(F32))
    q = l2norm(qkv[..., :A_QK].reshape(B, T, A_HEADS, A_DK)) * (A_DK ** -0.5)
    k = l2norm(qkv[..., A_QK:2 * A_QK].reshape(B, T, A_HEADS, A_DK))
    v = qkv[..., 2 * A_QK:].reshape(B, T, A_HEADS, A_DV)
    z = p[..., A_CONV_DIM:A_CONV_DIM + A_V].reshape(B, T, A_HEADS, A_DV).astype(F32)
    b = p[..., A_CONV_DIM + A_V:A_CONV_DIM + A_V + A_HEADS].astype(F32)
    a = p[..., A_CONV_DIM + A_V + A_HEADS:].astype(F32)
    beta = jax.nn.sigmoid(b)
    g = -jnp.exp(a_log.astype(F32)) * jax.nn.softplus(a + dt_bias.astype(F32))
    o, s = gated_delta_rule(q, k, v, g, beta, s0)
    o = rmsnorm(o, g_norm) * jax.nn.silu(z)
    return o.reshape(B, T, A_V), s, conv_new


def shortconv_mixer(p, conv_buf, w_conv):
    h = p[..., :B_WIDTH]
    gate_b = p[..., B_WIDTH:2 * B_WIDTH]
    gate_c = p[..., 2 * B_WIDTH:]
    y, conv_new = causal_dwconv(gate_c * h, conv_buf, w_conv)
    return gate_b * y, conv_new


def even_mix(h, s0, conv_a0, conv_b0, w_in, w_out, a_conv, a_log, dt_bias, a_norm, b_conv):
    p = h @ w_in
    o_a, s, conv_a = gdn_mixer(p[..., :A_IN], conv_a0, s0, a_conv, a_log, dt_bias, a_norm)
    o_b, conv_b = shortconv_mixer(p[..., A_IN:], conv_b0, b_conv)
    y = jnp.concatenate([o_a.astype(h.dtype), o_b.astype(h.dtype)], axis=-1) @ w_out
    return y, s, conv_a, conv_b


def compress(rows, pe, w1, w2):
    B, Tk, G, dh = rows.shape
    S = C_CMP_STRIDE
    nh = Tk // S
    halves = rows[:, :nh * S].reshape(B, nh, S, G, dh).astype(F32)
    pa = jnp.einsum('bnsgd,sdh->bngh', halves + pe[:S, None, :], w1[:S])
    pb = jnp.einsum('bnsgd,sdh->bngh', halves + pe[S:, None, :], w1[S:])
    hid = jax.nn.silu(pa[:, :-1] + pb[:, 1:])
    return jnp.einsum('bngh,hd->bgnd', hid, w2)


def cmp_to_sel(nc, ns):
    cs = C_CMP_STRIDE * jnp.arange(nc)[:, None]
    ss = C_SEL_LEN * jnp.arange(ns)[None, :]
    ov = jnp.clip(jnp.minimum(cs + C_CMP_LEN, ss + C_SEL_LEN) - jnp.maximum(cs, ss), 0, None)
    return ov.astype(F32) / C_CMP_LEN


def nsa_attend(q, gates, q_pos, k_cmp, v_cmp, cmp_end, m_cs, k_selb, v_selb, k_win, v_win, w_pos, w_lo):
    B, Tq = q.shape[:2]
    G, R, dh = C_KV_HEADS, C_HEADS // C_KV_HEADS, C_HD
    qg = jnp.transpose(q.reshape(B, Tq, G, R, dh), (0, 2, 1, 3, 4)).astype(F32) * (dh ** -0.5)
    slopes = alibi_slopes().reshape(G, R)
    qp = q_pos.astype(F32)
    mask_c = cmp_end[None, :] <= q_pos[:, None]
    dist_c = qp[:, None] - cmp_end.astype(F32)[None, :]
    s_c = jnp.einsum('bgtrd,bgnd->bgtrn', qg, k_cmp) - slopes[:, None, :, None] * dist_c[None, :, None, :]
    p_c = masked_softmax(s_c, mask_c[None, None, :, None, :])
    o_c = jnp.einsum('bgtrn,bgnd->bgtrd', p_c, v_cmp)
    ns = k_selb.shape[2]
    n_top = min(C_TOPN, ns)
    imp = jnp.einsum('bgtrn,ns->bgts', p_c, m_cs)
    blk = jnp.arange(ns)[None, :]
    cur = (q_pos // C_SEL_LEN)[:, None]
    valid = blk <= cur
    forced = (blk == 0) | (blk == cur) | (blk == cur - 1)
    score = jnp.where(valid, jnp.where(forced, BIG, imp), NEG)
    top_val, top_idx = lax.top_k(score, n_top)
    picked = top_val > 0.5 * NEG
    take = jax.vmap(jax.vmap(lambda blocks, idx: blocks[idx]))
    k_s = take(k_selb, top_idx).reshape(B, G, Tq, n_top * C_SEL_LEN, dh)
    v_s = take(v_selb, top_idx).reshape(B, G, Tq, n_top * C_SEL_LEN, dh)
    pos_s = top_idx[..., None] * C_SEL_LEN + jnp.arange(C_SEL_LEN)
    mask_s = (picked[..., None] & (pos_s <= q_pos[None, None, :, None, None])).reshape(B, G, Tq, 1, -1)
    dist_s = (qp[None, None, :, None, None] - pos_s.astype(F32)).reshape(B, G, Tq, 1, -1)
    s_s = jnp.einsum('bgtrd,bgtkd->bgtrk', qg, k_s.astype(F32)) - slopes[None, :, None, :, None] * dist_s
    p_s = masked_softmax(s_s, mask_s)
    o_s = jnp.einsum('bgtrk,bgtkd->bgtrd', p_s, v_s.astype(F32))
    dist_w = qp[:, None] - w_pos.astype(F32)[None, :]
    mask_w = ((w_pos[None, :] <= q_pos[:, None]) & (q_pos[:, None] - w_pos[None, :] < C_WINDOW)
              & (w_pos[None, :] >= w_lo))
    s_w = jnp.einsum('bgtrd,bgkd->bgtrk', qg, k_win.astype(F32)) - slopes[:, None, :, None] * dist_w[None, :, None, :]
    p_w = masked_softmax(s_w, mask_w[None, None, :, None, :])
    o_w = jnp.einsum('bgtrk,bgkd->bgtrd', p_w, v_win.astype(F32))
    gt = jnp.transpose(gates.reshape(B, Tq, G, R, 3), (0, 2, 1, 3, 4))
    o = gt[..., 0:1] * o_c + gt[..., 1:2] * o_s + gt[..., 2:3] * o_w
    return jnp.transpose(o, (0, 2, 1, 3, 4)).reshape(B, Tq, C_HEADS * dh)


def nsa_core(q, gates, q_start, cmp_all, sel_all, win_all, win_start, pe, w1, w2):
    B, Tq = q.shape[:2]
    Tk = cmp_all.shape[1]
    k_cmp = compress(cmp_all[:, :, 0], pe[0], w1[0], w2[0])
    v_cmp = compress(cmp_all[:, :, 1], pe[1], w1[1], w2[1])
    nc = k_cmp.shape[2]
    cmp_end = C_CMP_STRIDE * jnp.arange(nc) + C_CMP_LEN - 1
    ns = -(-Tk // C_SEL_LEN)
    selp = jnp.pad(sel_all, ((0, 0), (0, ns * C_SEL_LEN - Tk), (0, 0), (0, 0), (0, 0)))
    selp = selp.reshape(B, ns, C_SEL_LEN, 2, C_KV_HEADS, C_HD)
    k_selb = jnp.transpose(selp[:, :, :, 0], (0, 3, 1, 2, 4))
    v_selb = jnp.transpose(selp[:, :, :, 1], (0, 3, 1, 2, 4))
    m_cs = cmp_to_sel(nc, ns)
    k_win = jnp.transpose(win_all[:, :, 0], (0, 2, 1, 3))
    v_win = jnp.transpose(win_all[:, :, 1], (0, 2, 1, 3))
    shared = (k_cmp, v_cmp, cmp_end, m_cs, k_selb, v_selb)
    if Tq <= C_QBLK:
        q_pos = q_start + jnp.arange(Tq)
        w_pos = win_start + jnp.arange(k_win.shape[2])
        return nsa_attend(q, gates, q_pos, *shared, k_win, v_win, w_pos, win_start)
    off = q_start - win_start
    pad_w = ((0, 0), (0, 0), (C_WINDOW, 0), (0, 0))
    kwp = jnp.pad(k_win, pad_w)
    vwp = jnp.pad(v_win, pad_w)
    nkw = C_WINDOW + C_QBLK

    def block(i):
        q0 = i * C_QBLK
        qb = lax.dynamic_slice_in_dim(q, q0, C_QBLK, axis=1)
        gb = lax.dynamic_slice_in_dim(gates, q0, C_QBLK, axis=1)
        kw = lax.dynamic_slice_in_dim(kwp, off + q0, nkw, axis=2)
        vw = lax.dynamic_slice_in_dim(vwp, off + q0, nkw, axis=2)
        q_pos = q_start + q0 + jnp.arange(C_QBLK)
        w_pos = q_start + q0 - C_WINDOW + jnp.arange(nkw)
        return nsa_attend(qb, gb, q_pos, *shared, kw, vw, w_pos, win_start)

    out = lax.map(block, jnp.arange(Tq // C_QBLK))
    return jnp.moveaxis(out, 0, 1).reshape(B, Tq, C_HEADS * C_HD)


def chunk_mlp(p, ws, bs, ln_g, ln_b):
    B, T, _ = p.shape
    z = jax.nn.gelu(p.astype(F32))
    u, v = z[..., :D_WIDTH], z[..., D_WIDTH:]
    mu = jnp.mean(v, axis=-1, keepdims=True)
    var = jnp.mean(jnp.square(v - mu), axis=-1, keepdims=True)
    v = (v - mu) * lax.rsqrt(var + EPS) * ln_g.astype(F32) + ln_b.astype(F32)
    pad = (-T) % D_CHUNK
    nch = (T + pad) // D_CHUNK
    vc = jnp.pad(v, ((0, 0), (0, pad), (0, 0))).reshape(B, nch, D_CHUNK, D_GROUPS, D_WIDTH // D_GROUPS)
    w_causal = jnp.where(jnp.tril(jnp.ones((D_CHUNK, D_CHUNK), bool)), ws.astype(F32), 0.0)
    mixed = jnp.einsum('gts,bnsgc->bntgc', w_causal, vc) + bs.astype(F32).T[None, None, :, :, None]
    mixed = mixed.reshape(B, nch * D_CHUNK, D_WIDTH)[:, :T]
    return u * mixed, v.astype(p.dtype)


def odd_mix(h, past_cmp, past_sel, win_buf, q_start, w_in, w_out, pe, w1, w2, ws, bs, ln_g, ln_b):
    B, T, _ = h.shape
    p = h @ w_in
    pc = p[..., :C_IN]
    q = pc[..., :C_Q].reshape(B, T, C_HEADS, C_HD)
    gates = jax.nn.sigmoid(pc[..., C_Q:C_Q + C_GATES].astype(F32)).reshape(B, T, C_HEADS, 3)
    kv = pc[..., C_Q + C_GATES:].reshape(B, T, 3, 2, C_KV_HEADS, C_HD)
    cmp_new, sel_new, win_new = kv[:, :, 0], kv[:, :, 1], kv[:, :, 2]
    cmp_all = jnp.concatenate([past_cmp.astype(h.dtype), cmp_new], axis=1)
    sel_all = jnp.concatenate([past_sel.astype(h.dtype), sel_new], axis=1)
    win_all = jnp.concatenate([win_buf.astype(h.dtype), win_new], axis=1)
    win_start = q_start - win_buf.shape[1]
    o_c = nsa_core(q, gates, q_start, cmp_all, sel_all, win_all, win_start, pe, w1, w2)
    o_d, v_rows = chunk_mlp(p[..., C_IN:], ws, bs, ln_g, ln_b)
    y = jnp.concatenate([o_c.astype(h.dtype), o_d.astype(h.dtype)], axis=-1) @ w_out
    n_keep = min(C_WINDOW, q_start + T)
    win_keep = win_all[:, win_all.shape[1] - n_keep:]
    return y, cmp_new, sel_new, win_keep, v_rows


def gather_paged(pool, page_table):
    pages = jnp.moveaxis(pool[page_table], 2, 1)
    db, nl, npg, pg = pages.shape[:4]
    return pages.reshape((db, nl, npg * pg) + pages.shape[4:])


def trunk(x, q_start, delta0, conv_a0, conv_b0, past_cmp, past_sel, win_buf,
          norm_g, ffn_gate, ffn_up, ffn_down,
          ev_w_in, ev_w_out, ev_a_conv, ev_a_log, ev_dt_bias, ev_a_norm, ev_b_conv,
          od_w_in, od_w_out, od_cmp_pe, od_cmp_w1, od_cmp_w2, od_d_ws, od_d_bs, od_d_ln_g, od_d_ln_b):
    deltas, convs_a, convs_b = [], [], []
    cmps, sels, wins, dvs = [], [], [], []
    for li in range(DEPTH):
        j = li // 2
        g = norm_g[li]
        x = x + 0.5 * rmsnorm(swiglu(rmsnorm(x, g[0]), ffn_gate[li, 0], ffn_up[li, 0], ffn_down[li, 0]), g[1])
        h = rmsnorm(x, g[2])
        if li % 2 == 0:
            y, s, ca, cb = even_mix(h, delta0[:, j], conv_a0[:, j], conv_b0[:, j], ev_w_in[j], ev_w_out[j],
                                    ev_a_conv[j], ev_a_log[j], ev_dt_bias[j], ev_a_norm[j], ev_b_conv[j])
            deltas.append(s)
            convs_a.append(ca)
            convs_b.append(cb)
        else:
            y, rc, rs, wb, dv = odd_mix(h, past_cmp[:, j], past_sel[:, j], win_buf[:, j], q_start,
                                        od_w_in[j], od_w_out[j], od_cmp_pe[j], od_cmp_w1[j], od_cmp_w2[j],
                                        od_d_ws[j], od_d_bs[j], od_d_ln_g[j], od_d_ln_b[j])
            cmps.append(rc)
            sels.append(rs)
            wins.append(wb)
            dvs.append(dv)
        x = x + rmsnorm(y, g[3])
        x = x + 0.5 * rmsnorm(swiglu(rmsnorm(x, g[4]), ffn_gate[li, 1], ffn_up[li, 1], ffn_down[li, 1]), g[5])
    st = lambda a: jnp.stack(a, axis=1)
    return x, st(deltas), st(convs_a), st(convs_b), st(cmps), st(sels), st(wins), st(dvs)


def setup_inputs(seed: int = 0) -> dict:
    key = jax.random.key(seed)
    ks = iter(jax.random.split(key, 32))
    nrm = lambda shape, scale: jax.random.normal(next(ks), shape, F32) * scale
    n_pages = PAST_LEN // PAGE_SIZE
    n_used = DEC_BATCH * n_pages
    n_pool = n_used + n_used // 4
    win_rows = min(C_WINDOW, PAST_LEN)
    kv_row = (2, C_KV_HEADS, C_HD)
    page_table = jax.random.permutation(next(ks), n_pool)[:n_used].reshape(DEC_BATCH, n_pages).astype(jnp.int32)
    return {
        'x_prompt': nrm((BATCH, SEQ, D_MODEL), 1.0),
        'x_sample': nrm((DEC_BATCH, DEC_SEQ, D_MODEL), 1.0),
        'state_delta': nrm((DEC_BATCH, N_EVEN, A_HEADS, A_DK, A_DV), 0.1),
        'state_conv_a': nrm((DEC_BATCH, N_EVEN, A_CONV - 1, A_CONV_DIM), 1.0),
        'state_conv_b': nrm((DEC_BATCH, N_EVEN, B_CONV - 1, B_WIDTH), 1.0),
        'cache_cmp_kv': nrm((n_pool, N_ODD, PAGE_SIZE) + kv_row, 1.0),
        'cache_sel_kv': nrm((n_pool, N_ODD, PAGE_SIZE) + kv_row, 1.0),
        'cache_win_kv': nrm((DEC_BATCH, N_ODD, win_rows) + kv_row, 1.0),
        'page_table': page_table,
        'norm_g': 1.0 + nrm((DEPTH, 6, D_MODEL), 0.01),
        'ffn_gate': nrm((DEPTH, 2, D_MODEL, D_FF), D_MODEL ** -0.5),
        'ffn_up': nrm((DEPTH, 2, D_MODEL, D_FF), D_MODEL ** -0.5),
        'ffn_down': nrm((DEPTH, 2, D_FF, D_MODEL), D_FF ** -0.5),
        'ev_w_in': nrm((N_EVEN, D_MODEL, EVEN_IN), D_MODEL ** -0.5),
        'ev_w_out': nrm((N_EVEN, EVEN_OUT, D_MODEL), EVEN_OUT ** -0.5),
        'ev_a_conv': nrm((N_EVEN, A_CONV, A_CONV_DIM), A_CONV ** -0.5),
        'ev_a_log': jnp.log(jax.random.uniform(next(ks), (N_EVEN, A_HEADS), F32, 1.0, 16.0)),
        'ev_dt_bias': -4.0 + nrm((N_EVEN, A_HEADS), 0.1),
        'ev_a_norm': 1.0 + nrm((N_EVEN, A_DV), 0.01),
        'ev_b_conv': nrm((N_EVEN, B_CONV, B_WIDTH), B_CONV ** -0.5),
        'od_w_in': nrm((N_ODD, D_MODEL, ODD_IN), D_MODEL ** -0.5),
        'od_w_out': nrm((N_ODD, ODD_OUT, D_MODEL), ODD_OUT ** -0.5),
        'od_cmp_pe': nrm((N_ODD, 2, C_CMP_LEN, C_HD), 0.02),
        'od_cmp_w1': nrm((N_ODD, 2, C_CMP_LEN, C_HD, C_CMP_HID), (C_CMP_LEN * C_HD) ** -0.5),
        'od_cmp_w2': nrm((N_ODD, 2, C_CMP_HID, C_HD), C_CMP_HID ** -0.5),
        'od_d_ws': nrm((N_ODD, D_GROUPS, D_CHUNK, D_CHUNK), D_CHUNK ** -0.5),
        'od_d_bs': 1.0 + nrm((N_ODD, D_GROUPS, D_CHUNK), 0.01),
        'od_d_ln_g': 1.0 + nrm((N_ODD, D_WIDTH), 0.01),
        'od_d_ln_b': nrm((N_ODD, D_WIDTH), 0.01),
    }


def reference(x_prompt, x_sample, state_delta, state_conv_a, state_conv_b, cache_cmp_kv, cache_sel_kv,
              cache_win_kv, page_table, norm_g, ffn_gate, ffn_up, ffn_down,
              ev_w_in, ev_w_out, ev_a_conv, ev_a_log, ev_dt_bias, ev_a_norm, ev_b_conv,
              od_w_in, od_w_out, od_cmp_pe, od_cmp_w1, od_cmp_w2, od_d_ws, od_d_bs, od_d_ln_g, od_d_ln_b):
    weights = (norm_g, ffn_gate, ffn_up, ffn_down,
               ev_w_in, ev_w_out, ev_a_conv, ev_a_log, ev_dt_bias, ev_a_norm, ev_b_conv,
               od_w_in, od_w_out, od_cmp_pe, od_cmp_w1, od_cmp_w2, od_d_ws, od_d_bs, od_d_ln_g, od_d_ln_b)
    dt = x_prompt.dtype
    bp = x_prompt.shape[0]
    no_rows = jnp.zeros((bp, N_ODD, 0, 2, C_KV_HEADS, C_HD), dt)
    (y_prompt, delta_p, conv_a_p, conv_b_p, cmp_p, sel_p, win_p, _) = trunk(
        x_prompt, 0,
        jnp.zeros((bp, N_EVEN, A_HEADS, A_DK, A_DV), F32),
        jnp.zeros((bp, N_EVEN, A_CONV - 1, A_CONV_DIM), dt),
        jnp.zeros((bp, N_EVEN, B_CONV - 1, B_WIDTH), dt),
        no_rows, no_rows, no_rows, *weights)
    past_len = page_table.shape[1] * cache_cmp_kv.shape[2]
    (y_sample, delta_s, conv_a_s, conv_b_s, cmp_s, sel_s, win_s, dv_s) = trunk(
        x_sample, past_len, state_delta, state_conv_a, state_conv_b,
        gather_paged(cache_cmp_kv, page_table), gather_paged(cache_sel_kv, page_table),
        cache_win_kv, *weights)
    return (y_prompt, y_sample, delta_p, delta_s, conv_a_p, conv_a_s, conv_b_p, conv_b_s,
            cmp_p, cmp_s, sel_p, sel_s, win_p, win_s, dv_s)
```

```python
import numpy as np
import concourse.bass as bass
import concourse.mybir as mybir
from concourse.bass_utils import run_bass_kernel_spmd
from contextlib import ExitStack

F32 = mybir.dt.float32
BF16 = mybir.dt.bfloat16
I32 = mybir.dt.int32
AF = mybir.ActivationFunctionType
ALU = mybir.AluOpType
AX = mybir.AxisListType

COMPUTE = ("pe", "act", "dve", "pool")
NSLOT = 12
EPS = 1e-6
D = 1024
DFF = 2816
NM = DFF // 128


class Buf:
    def __init__(self, name, t, nsub=1):
        self.name = name
        self.t = t
        self.nsub = nsub

    def __getitem__(self, idx):
        return self.t[idx]

    def ap(self):
        return self.t.ap()

    def k(self, *subs):
        if not subs:
            return [(self.name, i) for i in range(self.nsub)]
        return [(self.name, s) for s in subs]


class Op:
    __slots__ = ("eng", "fn", "reads", "writes", "deps", "signal", "signo", "dma", "slot", "target", "idx")


class K:
    def __init__(self):
        self.nc = bass.Bass("TRN2", target_bir_lowering=False)
        self.es = ExitStack()
        self.ops = []
        self.eng = {"pe": self.nc.tensor, "act": self.nc.scalar, "dve": self.nc.vector,
                    "pool": self.nc.gpsimd, "sp": self.nc.sync}
        self.scopes = []

    def sbuf(self, name, shape, dtype, nsub=1):
        self.nbuf = getattr(self, "nbuf", 0) + 1
        name = "%s_%d" % (name, self.nbuf)
        t = self.es.enter_context(self.nc.sbuf_tensor(name, list(shape), dtype))
        return Buf(name, t, nsub)

    def capture(self, fn):
        saved = self.ops
        self.ops = []
        fn()
        out = self.ops
        self.ops = saved
        return out

    def barrier(self):
        o = Op()
        o.eng = None
        o.fn = None
        o.reads = []
        o.writes = []
        o.dma = False
        o.signal = False
        o.idx = len(self.ops)
        self.ops.append(o)
        return o

    def scope(self):
        k = self

        class _S:
            def __enter__(s):
                s.es = ExitStack()
                s.old = k.es
                k.es = s.es
                return s

            def __exit__(s, *a):
                k.es = s.old
                s.es.close()
                k.barrier()
        return _S()

    def psum(self, name, shape, dtype, nsub=1):
        t = self.es.enter_context(self.nc.psum_tensor(name, list(shape), dtype))
        return Buf(name, t, nsub)

    def dram(self, name, shape, dtype, kind="Internal", nsub=1):
        t = self.nc.dram_tensor(name, list(shape), dtype, kind=kind)
        return Buf(name, t, nsub)

    def op(self, eng, fn, reads=(), writes=(), dma=False):
        o = Op()
        o.eng = eng
        o.fn = fn
        o.reads = list(reads)
        o.writes = list(writes)
        o.dma = dma
        o.signal = False
        o.idx = len(self.ops)
        self.ops.append(o)
        return o

    def dma(self, q, out, in_, reads, writes, **kw):
        e = self.eng[q]
        return self.op(q, lambda: e.dma_start(out=out, in_=in_, **kw), reads, writes, dma=True)

    def mm(self, out, lhsT, rhs, start, stop, reads, writes, sgc=False):
        nc = self.nc
        if sgc:
            return self.op("pe", lambda: nc.tensor.matmul(out, lhsT=lhsT, rhs=rhs, start=start, stop=stop,
                                                          skip_group_check=True), reads, writes)
        return self.op("pe", lambda: nc.tensor.matmul(out, lhsT=lhsT, rhs=rhs, start=start, stop=stop),
                       reads, writes)

    def tr(self, out, in_, ident, reads, writes):
        nc = self.nc
        return self.op("pe", lambda: nc.tensor.transpose(out, in_, ident), reads, writes)

    def act(self, out, in_, func, reads, writes, **kw):
        nc = self.nc
        return self.op("act", lambda: nc.scalar.activation(out=out, in_=in_, func=func, **kw), reads, writes)

    def tt(self, out, in0, in1, op, reads, writes, eng="dve"):
        e = self.eng[eng]
        return self.op(eng, lambda: e.tensor_tensor(out=out, in0=in0, in1=in1, op=op), reads, writes)

    def ts(self, out, in0, s1, s2, op0, op1, reads, writes, eng="dve"):
        e = self.eng[eng]
        if op1 is None:
            return self.op(eng, lambda: e.tensor_scalar(out=out, in0=in0, scalar1=s1, scalar2=None, op0=op0),
                           reads, writes)
        return self.op(eng, lambda: e.tensor_scalar(out=out, in0=in0, scalar1=s1, scalar2=s2, op0=op0, op1=op1),
                       reads, writes)

    def stt(self, out, in0, scalar, in1, op0, op1, reads, writes):
        nc = self.nc
        return self.op("dve", lambda: nc.vector.scalar_tensor_tensor(out=out, in0=in0, scalar=scalar, in1=in1,
                                                                     op0=op0, op1=op1), reads, writes)

    def cp(self, out, in_, reads, writes, eng="dve"):
        if eng == "act":
            nc = self.nc
            return self.op("act", lambda: nc.scalar.copy(out=out, in_=in_), reads, writes)
        e = self.eng[eng]
        return self.op(eng, lambda: e.tensor_copy(out=out, in_=in_), reads, writes)

    def recip(self, out, in_, reads, writes):
        nc = self.nc
        return self.op("dve", lambda: nc.vector.reciprocal(out=out, in_=in_), reads, writes)

    def memset(self, out, val, writes, eng="dve"):
        e = self.eng[eng]
        return self.op(eng, lambda: e.memset(out, val), (), writes)

    def finish(self, final_wait_keys):
        nc = self.nc
        ops = self.ops
        lastw = {}
        lastr = {}

        def prune(lst, o):
            if o.dma:
                return lst + [o]
            return [x for x in lst if x.dma or x.eng != o.eng] + [o]

        for i_, o in enumerate(ops):
            o.idx = i_
        last_eng = {}
        dmas = []
        for o in ops:
            if o.eng is None:
                o.deps = list(last_eng.values()) + dmas
                for d in o.deps:
                    d.signal = True
                dmas = []
                continue
            if o.dma:
                dmas.append(o)
            else:
                last_eng[o.eng] = o
            deps = {}
            for kk in o.reads:
                for w in lastw.get(kk, ()):
                    deps[w.idx] = w
            for kk in o.writes:
                for w in lastw.get(kk, ()):
                    deps[w.idx] = w
                for r in lastr.get(kk, ()):
                    deps[r.idx] = r
            deps.pop(o.idx, None)
            dl = []
            for d in deps.values():
                if (not d.dma) and (not o.dma) and d.eng == "pe" and o.eng == "pe":
                    continue
                dl.append(d)
            o.deps = dl
            for d in dl:
                d.signal = True
            for kk in o.reads:
                lastr[kk] = prune(lastr.get(kk, []), o)
            for kk in o.writes:
                lastw[kk] = [o]
                lastr[kk] = []
        fin = []
        for kk in final_wait_keys:
            for w in lastw.get(kk, ()):
                w.signal = True
                fin.append(w)
        es = ExitStack()
        sem = {e: es.enter_context(nc.semaphore("s_" + e)) for e in COMPUTE}
        slots = {q: [es.enter_context(nc.semaphore("d_%s%d" % (q, i))) for i in range(NSLOT)]
                 for q in ("sp", "pool")}
        slot_val = {q: [0] * NSLOT for q in ("sp", "pool")}
        ndma = {"sp": 0, "pool": 0}
        cnt = {e: 0 for e in COMPUTE}
        known = {}

        def wait(engname, s, sname, val):
            kk = (engname, sname)
            if known.get(kk, 0) >= val:
                return
            known[kk] = val
            self.eng[engname].wait_ge(s, val)

        for o in ops:
            if o.eng is None:
                for e in ("pe", "act", "dve", "pool", "sp"):
                    for d in o.deps:
                        if d.dma:
                            wait(e, slots[d.eng][d.slot], ("d", d.eng, d.slot), d.target)
                        elif d.eng != e:
                            wait(e, sem[d.eng], ("c", d.eng), d.signo)
                continue
            for d in o.deps:
                if d.dma:
                    wait(o.eng, slots[d.eng][d.slot], ("d", d.eng, d.slot), d.target)
                else:
                    wait(o.eng, sem[d.eng], ("c", d.eng), d.signo)
            if o.dma:
                q = o.eng
                i = ndma[q] % NSLOT
                ndma[q] += 1
                if slot_val[q][i] > 0:
                    wait(q, slots[q][i], ("d", q, i), slot_val[q][i])
                slot_val[q][i] += 16
                o.slot = i
                o.target = slot_val[q][i]
                inst = o.fn()
                inst.then_inc(slots[q][i], 16)
            else:
                inst = o.fn()
                if o.signal:
                    cnt[o.eng] += 1
                    o.signo = cnt[o.eng]
                    inst.then_inc(sem[o.eng], 1)
        for w in fin:
            if w.dma:
                wait("sp", slots[w.eng][w.slot], ("d", w.eng, w.slot), w.target)
            else:
                wait("sp", sem[w.eng], ("c", w.eng), w.signo)
        self.stats = dict(nops=len(ops), cnt=cnt, ndma=ndma)
        es.close()
        self.es.close()
        return nc


def make_consts():
    i = np.arange(128)
    ident = np.eye(128, dtype=np.float32)
    ones = np.ones((128, 128), np.float32)
    U = (i[:, None] <= i[None, :]).astype(np.float32)
    LSN = -(i[None, :] < i[:, None]).astype(np.float32)
    blk = (i[:, None] // 8 == i[None, :] // 8).astype(np.float32)
    seqm = (i[:, None] // 8 == np.arange(16)[None, :]).astype(np.float32)
    seqm = np.concatenate([seqm, np.zeros((128, 112), np.float32)], 1)
    iota = np.broadcast_to(i[:, None].astype(np.float32), (128, 128))
    return np.stack([ident, ones, U, LSN, ones, U * blk, LSN * blk, blk, seqm, iota], 0)


def make_consts2():
    NP = 4224
    p = np.arange(NP)
    kpos = np.stack([np.ones(NP), np.ones(NP), 64.0 * (p // 64), (p % 64) * 1.0], 0).astype(np.float32)
    e = 16 * np.arange(256) + 31
    kposc = np.stack([np.ones(256), np.ones(256), 64.0 * (e // 64), (e % 64) * 1.0], 0).astype(np.float32)
    qpos = np.zeros((4, 2, 4, NP), np.float32)
    for g in range(2):
        for r in range(4):
            sl = 2.0 ** (-(4 * g + r + 1))
            qpos[0, g, r] = -sl * 64.0 * (p // 64)
            qpos[1, g, r] = -sl * (p % 64)
            qpos[2, g, r] = sl
            qpos[3, g, r] = sl
    i = np.arange(128)
    NEGM = -30000.0
    cm = np.where(i[:, None] > i[None, :], NEGM, 0.0)
    wm = np.where(i[:, None] <= i[None, :], NEGM, 0.0)
    amask = np.stack([cm, wm], 0).astype(np.float32)
    cmpmask = np.where(e[:, None] > p[None, :], NEGM, 0.0).astype(np.float32)
    cmpmask[255] = NEGM
    blk = np.arange(64)[None, :]
    cur = (p // 64)[:, None]
    valid = blk <= cur
    forced = (blk == 0) | (blk == cur) | (blk == cur - 1)
    mulc = (valid & ~forced).astype(np.float32)
    addc = np.where(valid, np.where(forced, 1e4, 0.0), -1e30).astype(np.float32)
    topc = np.stack([mulc, addc, valid.astype(np.float32)], 1)
    gexp = (p[None, :] // 64 == np.arange(64)[:, None]).astype(np.float32)
    cs = 16 * np.arange(256)[:, None]
    ss = 64 * np.arange(64)[None, :]
    ov = np.clip(np.minimum(cs + 32, ss + 64) - np.maximum(cs, ss), 0, None)
    mcs = (ov / 32.0).astype(np.float32)
    mcs[255] = 0.0
    return {"c_kpos": kpos, "c_kposc": kposc, "c_qpos": qpos, "c_amask": amask, "c_cmpmask": cmpmask,
            "c_topc": topc, "c_gexp": gexp, "c_mcs": mcs}


C_ID, C_ONES, C_U, C_LSN, C_BLK, C_UB, C_LSNB, C_BLKB, C_SEQM, C_IOTA = range(10)


class Ctx:
    pass


def build(TP=4096, NS=16, stages=("all",), debug_outs=(), debug_ins=(), NPOOL=2560):
    k = K()
    nc = k.nc
    C = Ctx()
    C.k = k
    TS = NS * 8
    assert TS == 128
    TT = 384
    Ttot = TP + TS
    assert Ttot % TT == 0
    NT = Ttot // TT
    C.TP, C.TS, C.TT, C.Ttot, C.NT = TP, TS, TT, Ttot, NT

    k.ext_in = {}

    def ein(name, shape, dt=F32):
        k.ext_in[name] = (tuple(shape), dt)
        return k.dram(name, shape, dt, kind="ExternalInput")

    def eout(name, shape, dt=F32):
        return k.dram(name, shape, dt, kind="ExternalOutput")

    def scratch(name, shape, dt=F32, nsub=1):
        kind = "ExternalOutput" if name in debug_outs else "Internal"
        if name in debug_ins:
            kind = "ExternalInput"
            k.ext_in[name] = (tuple(shape), dt)
        return k.dram(name, shape, dt, kind=kind, nsub=nsub)

    x_all = ein("x_all", [Ttot, D])
    y_all = eout("y_all", [Ttot, D])
    consts = ein("consts", [10, 128, 128])
    norm_g = ein("norm_g", [12, D])
    ffn_gate = ein("ffn_gate", [4, D, DFF])
    ffn_up = ein("ffn_up", [4, D, DFF])
    ffn_down = ein("ffn_down", [4, DFF, D])
    ev_w_in = ein("ev_w_in", [D, 3592])
    ev_w_out = ein("ev_w_out", [D, D])
    ev_a_conv = ein("ev_a_conv", [4, 1536])
    ev_a_log = ein("ev_a_log", [1, 4])
    ev_dt_bias = ein("ev_dt_bias", [1, 4])
    ev_a_norm = ein("ev_a_norm", [1, 128])
    ev_b_conv = ein("ev_b_conv", [3, 512])
    st_delta = ein("st_delta", [NS, 4, 128, 128])
    st_conv_a = ein("st_conv_a", [NS * 3, 1536])
    st_conv_b = ein("st_conv_b", [NS * 2, 512])
    od_w_in = ein("od_w_in", [D, 2328])
    od_w_out = ein("od_w_out", [D, D])
    od_cmp_pe = ein("od_cmp_pe", [2, 32, 64])
    od_cmp_w1 = ein("od_cmp_w1", [2, 32, 64, 128])
    od_cmp_w2 = ein("od_cmp_w2", [2, 128, 64])
    cache_cmp = ein("cache_cmp", [NPOOL * 128, 256])
    cache_sel = ein("cache_sel", [NPOOL * 128, 256])
    page_tab = ein("page_tab", [NS, 16], I32)
    c_kpos = ein("c_kpos", [4, 4224])
    c_kposc = ein("c_kposc", [4, 256])
    c_qpos = ein("c_qpos", [4, 2, 4, 4224])
    c_amask = ein("c_amask", [2, 128, 128])
    c_cmpmask = ein("c_cmpmask", [256, 4224])
    c_topc = ein("c_topc", [4224, 3, 64])
    c_gexp = ein("c_gexp", [64, 4224])
    c_mcs = ein("c_mcs", [256, 64])
    od_d_ws = ein("od_d_ws", [8, 128, 128])
    od_d_bs = ein("od_d_bs", [8, 128])
    od_d_ln_g = ein("od_d_ln_g", [1, 512])
    od_d_ln_b = ein("od_d_ln_b", [1, 512])
    cache_win = ein("cache_win", [NS, 512, 256])
    cmp_p = eout("cmp_p", [TP, 256])
    sel_p = eout("sel_p", [TP, 256])
    win_p = eout("win_p", [512, 256])
    cmp_s = eout("cmp_s", [TS, 256])
    sel_s = eout("sel_s", [TS, 256])
    win_s = k.dram("win_s", [NS, 512, 256], F32, kind="ExternalOutput", nsub=2)
    dv_s = eout("dv_s", [TS, 512])
    delta_p = eout("delta_p", [4, 128, 128])
    delta_s = eout("delta_s", [NS, 4, 128, 128])
    conv_a_p = eout("conv_a_p", [3, 1536])
    conv_a_s = eout("conv_a_s", [NS * 3, 1536])
    conv_b_p = eout("conv_b_p", [2, 512])
    conv_b_s = eout("conv_b_s", [NS * 2, 512])

    xT_d = scratch("xT_d", [D, Ttot], nsub=NT)
    PT0 = scratch("PT0", [29 * 128, Ttot], nsub=NT)
    ba0 = scratch("ba0", [Ttot, 8], nsub=NT)
    OT = scratch("OT", [D, Ttot], BF16, nsub=Ttot // 128)
    FT = scratch("FT", [8 * 128, Ttot], BF16, nsub=NT)
    UTok = scratch("UTok", [Ttot, 512], F32, nsub=Ttot // 128)
    OTok = scratch("OTok", [Ttot, D], F32, nsub=2 * (Ttot // 128))
    KVtok = scratch("KVtok", [Ttot, 768], F32, nsub=Ttot // 128)
    GTok = scratch("GTok", [Ttot, 24], F32, nsub=Ttot // 128)
    VTok = scratch("VTok", [Ttot, 512], F32, nsub=Ttot // 128)

    cst = k.sbuf("cst", [128, 10, 128], F32)
    k.dma("sp", cst[:], consts.ap().rearrange("n p f -> p n f"), consts.k(), cst.k())
    ones_bf = k.sbuf("ones_bf", [128, 128], BF16)
    k.cp(ones_bf[:], cst[:, C_ONES, :], cst.k(), ones_bf.k())
    ident = cst[:, C_ID, :]
    pb = [k.psum("pb%d" % i, [128, 512], F32) for i in range(8)]
    C.cst, C.ones_bf, C.ident, C.pb = cst, ones_bf, ident, pb

    gT = k.sbuf("gT", [128, 12, 8], F32)
    with k.scope():
        stg = k.sbuf("stg_g", [12, D], F32)
        k.dma("sp", stg[:], norm_g.ap(), norm_g.k(), stg.k())
        for c in range(8):
            k.tr(pb[7][:, c * 12:(c + 1) * 12], stg[:, c * 128:(c + 1) * 128], ident[:12, :12],
                 stg.k() + cst.k(), pb[7].k())
        k.cp(gT[:].rearrange("p r c -> p c r"), pb[7][:, 0:96].rearrange("p (c r) -> p c r", r=12),
             pb[7].k(), gT.k())
    gTh = k.sbuf("gTh", [128, 12, 8], F32)
    k.ts(gTh[:], gT[:], 0.5, None, ALU.mult, None, gT.k(), gTh.k())
    C.gT, C.gTh = gT, gTh

    def rms_rstd(src3, W, nchunk, rstd, sqb, scale, keys_r, bank):
        k.act(sqb[:, :nchunk, :W], src3, AF.Square, keys_r, sqb.k())
        for c in range(nchunk):
            k.mm(bank[:, :W], ones_bf[:], sqb[:, c, :W], c == 0, c == nchunk - 1,
                 ones_bf.k() + sqb.k(), bank.k())
        k.act(rstd[:, :W], bank[:, :W], AF.Sqrt, bank.k(), rstd.k(), scale=scale, bias=EPS)
        k.recip(rstd[:, :W], rstd[:, :W], rstd.k(), rstd.k())

    def load_x_from_input(xT, i):
        for j in range(TT // 128):
            t0 = i * TT + j * 128
            xtok = C.xtok
            k.dma("sp", xtok[:], x_all[t0:t0 + 128, :], x_all.k(), xtok.k())
            for hf in range(2):
                bank = pb[5 + hf]
                for c4 in range(4):
                    c = hf * 4 + c4
                    k.tr(bank[:, c4 * 128:(c4 + 1) * 128], xtok[:, c * 128:(c + 1) * 128], ident,
                         xtok.k() + cst.k(), bank.k())
                k.cp(xT[:, hf * 4:(hf + 1) * 4, j * 128:(j + 1) * 128],
                     bank[:].rearrange("p (c t) -> p c t", c=4), bank.k(), xT.k(),
                     eng=("act" if hf else "dve"))

    def load_xT(xT, i):
        k.dma("sp", xT[:], xT_d.ap().rearrange("(c p) t -> p c t", p=128)[:, :, i * TT:(i + 1) * TT],
              xT_d.k(i), xT.k())

    def store_xT(xT, i):
        k.dma("sp", xT_d.ap().rearrange("(c p) t -> p c t", p=128)[:, :, i * TT:(i + 1) * TT], xT[:],
              xT.k(), xT_d.k(i))

    def store_y_out(xT, i):
        ytok = C.ytok
        for j in range(TT // 128):
            t0 = i * TT + j * 128
            for hf in range(2):
                bank = pb[5 + hf]
                for c4 in range(4):
                    c = hf * 4 + c4
                    k.tr(bank[:, c4 * 128:(c4 + 1) * 128], xT[:, c, j * 128:(j + 1) * 128], ident,
                         xT.k() + cst.k(), bank.k())
                k.cp(ytok[:, hf * 512:(hf + 1) * 512], bank[:], bank.k(), ytok.k(), eng=("act" if hf else "dve"))
            k.dma("sp", y_all[t0:t0 + 128, :], ytok[:], ytok.k(), y_all.k())

    def ffn_stage(fidx, gpre, gpost, loader, storer, pre=None):
        with k.scope():
            GM = 6
            NGW = (NM + GM - 1) // GM
            wg = k.sbuf("wg", [128, 8, DFF], BF16, nsub=NGW)
            wu = k.sbuf("wu", [128, 8, DFF], BF16, nsub=NGW)
            wd = k.sbuf("wd", [128, NM, D], BF16)
            for gi in range(NGW):
                c0, c1 = gi * GM * 128, min(DFF, (gi + 1) * GM * 128)
                for c in range(8):
                    k.dma("pool", wg[:, c, c0:c1], ffn_gate[fidx, c * 128:(c + 1) * 128, c0:c1], ffn_gate.k(),
                          wg.k(gi))
                    k.dma("pool", wu[:, c, c0:c1], ffn_up[fidx, c * 128:(c + 1) * 128, c0:c1], ffn_up.k(),
                          wu.k(gi))
            for m in range(NM):
                k.dma("pool", wd[:, m, :], ffn_down[fidx, m * 128:(m + 1) * 128, :], ffn_down.k(), wd.k())
            xTs = [k.sbuf("xT", [128, 8, TT], F32) for _ in range(2)]
            hT = k.sbuf("hT", [128, 8, TT], BF16)
            aT = k.sbuf("aT", [128, NM, TT], BF16)
            sqb = aT
            yT = k.sbuf("yT", [128, 8, TT], F32)
            rstdP = k.sbuf("rstdP", [128, TT], F32)
            rstdE = k.sbuf("rstdE", [128, TT], F32)
            sg = [k.sbuf("sg%d" % i, [128, TT], F32) for i in range(2)]
            if loader is load_x_from_input:
                C.xtok = k.sbuf("xtok", [128, D], F32)
            if storer is store_y_out:
                C.ytok = k.sbuf("ytok", [128, D], F32)
            ysq = Buf(yT.name, yT.t, 1)
            ysq_ap = yT[:].rearrange("p c t -> p (c t)").bitcast(BF16)[:, 0:8 * TT].rearrange("p (c t) -> p c t", c=8)

            def pro_a(i):
                xT = xTs[i % 2]
                loader(xT, i)
                k.act(ysq_ap, xT[:], AF.Square, xT.k(), yT.k())
                for c in range(8):
                    k.mm(pb[7][:, :TT], ones_bf[:], ysq_ap[:, c, :], c == 0, c == 7, ones_bf.k() + yT.k(), pb[7].k())
                k.act(rstdP[:], pb[7][:, :TT], AF.Sqrt, pb[7].k(), rstdP.k(), scale=1.0 / D, bias=EPS)
                k.recip(rstdP[:], rstdP[:], rstdP.k(), rstdP.k())

            def pro_b(i):
                xT = xTs[i % 2]
                for c in range(8):
                    k.stt(hT[:, c, :], xT[:, c, :], gT[:, gpre, c:c + 1], rstdP[:], ALU.mult, ALU.mult,
                          xT.k() + gT.k() + rstdP.k(), hT.k())

            pro_a(0)
            pro_b(0)
            for i in range(NT):
                xT = xTs[i % 2]
                for m in range(NM):
                    bg = pb[m % 2]
                    bu = pb[2 + m % 2]
                    for c in range(8):
                        k.mm(bg[:, :TT], wg[:, c, m * 128:(m + 1) * 128], hT[:, c, :], c == 0, c == 7,
                             wg.k(m // GM) + hT.k(), bg.k())
                    for c in range(8):
                        k.mm(bu[:, :TT], wu[:, c, m * 128:(m + 1) * 128], hT[:, c, :], c == 0, c == 7,
                             wu.k(m // GM) + hT.k(), bu.k())
                    s_ = sg[m % 2]
                    k.act(s_[:], bg[:, :TT], AF.Silu, bg.k(), s_.k())
                    k.tt(aT[:, m, :], s_[:], bu[:, :TT], ALU.mult, s_.k() + bu.k(), aT.k())
                    if m == NM - 5 and i + 1 < NT:
                        pro_a(i + 1)
                if i + 1 < NT:
                    pro_b(i + 1)
                for c in range(8):
                    by = pb[4 + c % 2]
                    for m in range(NM):
                        k.mm(by[:, :TT], wd[:, m, c * 128:(c + 1) * 128], aT[:, m, :], m == 0, m == NM - 1,
                             wd.k() + aT.k(), by.k())
                    k.cp(yT[:, c, :], by[:, :TT], by.k(), yT.k(), eng=("act" if c % 2 else "dve"))
                rms_rstd(yT[:], TT, 8, rstdE, sqb, 1.0 / D, yT.k(), pb[7])
                for c in range(8):
                    k.stt(yT[:, c, :], yT[:, c, :], gTh[:, gpost, c:c + 1], rstdE[:], ALU.mult, ALU.mult,
                          yT.k() + gTh.k() + rstdE.k(), yT.k())
                k.tt(xT[:], xT[:], yT[:], ALU.add, xT.k() + yT.k(), xT.k())
                storer(xT, i)

    def wout_stage(w_out, grow, o_tok=False):
        with k.scope():
            if o_tok:
                otk = k.sbuf("otk", [128, D], F32)
            wo = k.sbuf("wo", [128, 8, D], BF16)
            for c in range(8):
                k.dma("pool", wo[:, c, :], w_out[c * 128:(c + 1) * 128, :], w_out.k(), wo.k())
            oT = k.sbuf("oT", [128, 8, TT], BF16)
            xT = k.sbuf("xT", [128, 8, TT], F32)
            yT = k.sbuf("yT", [128, 8, TT], F32)
            sqb = k.sbuf("sqb", [128, 8, TT], BF16)
            rstd = k.sbuf("rstd", [128, TT], F32)
            for i in range(NT):
                load_xT(xT, i)
                if o_tok:
                    for j in range(TT // 128):
                        t0 = i * TT + j * 128
                        k.dma("sp", otk[:], OTok[t0:t0 + 128, :], OTok.k(), otk.k())
                        for hf in range(2):
                            bank = pb[2 + hf]
                            for c4 in range(4):
                                c = hf * 4 + c4
                                k.tr(bank[:, c4 * 128:(c4 + 1) * 128], otk[:, c * 128:(c + 1) * 128], ident,
                                     otk.k() + cst.k(), bank.k())
                            k.cp(oT[:, hf * 4:(hf + 1) * 4, j * 128:(j + 1) * 128],
                                 bank[:].rearrange("p (c t) -> p c t", c=4), bank.k(), oT.k(),
                                 eng=("act" if hf else "dve"))
                else:
                    k.dma("sp", oT[:], OT.ap().rearrange("(c p) t -> p c t", p=128)[:, :, i * TT:(i + 1) * TT],
                          OT.k(), oT.k())
                for c in range(8):
                    by = pb[4 + c % 2]
                    for kc in range(8):
                        k.mm(by[:, :TT], wo[:, kc, c * 128:(c + 1) * 128], oT[:, kc, :], kc == 0, kc == 7,
                             wo.k() + oT.k(), by.k())
                    k.cp(yT[:, c, :], by[:, :TT], by.k(), yT.k(), eng=("act" if c % 2 else "dve"))
                rms_rstd(yT[:], TT, 8, rstd, sqb, 1.0 / D, yT.k(), pb[7])
                for c in range(8):
                    k.stt(yT[:, c, :], yT[:, c, :], gT[:, grow, c:c + 1], rstd[:], ALU.mult, ALU.mult,
                          yT.k() + gT.k() + rstd.k(), yT.k())
                k.tt(xT[:], xT[:], yT[:], ALU.add, xT.k() + yT.k(), xT.k())
                store_xT(xT, i)

    def win0_stage():
        with k.scope():
            wi = k.sbuf("wi0", [128, 8, 29 * 128], BF16, nsub=4)
            k.memset(wi[:, :, 16 * 128:17 * 128], 0.0, wi.k(2))
            for c in range(8):
                r = slice(c * 128, (c + 1) * 128)
                k.dma("pool", wi[:, c, 0:1024], ev_w_in[r, 0:1024], ev_w_in.k(), wi.k(0))
            for c in range(8):
                r = slice(c * 128, (c + 1) * 128)
                k.dma("pool", wi[:, c, 1024:2048], ev_w_in[r, 1024:2048], ev_w_in.k(), wi.k(1))
                k.dma("pool", wi[:, c, 2048:2056], ev_w_in[r, 2048:2056], ev_w_in.k(), wi.k(2))
            for c in range(8):
                r = slice(c * 128, (c + 1) * 128)
                k.dma("pool", wi[:, c, 17 * 128:29 * 128], ev_w_in[r, 2056:3592], ev_w_in.k(), wi.k(3))
            xT = k.sbuf("xT", [128, 8, TT], F32)
            hT = k.sbuf("hT", [128, 8, TT], BF16)
            sqb = k.sbuf("sqb", [128, 8, TT], BF16)
            rstd = k.sbuf("rstd", [128, TT], F32)
            st = [k.sbuf("pst%d" % i, [128, 4, TT], F32) for i in range(2)]
            bast = k.sbuf("bast", [128, 3, 8], F32)
            PTv = PT0.ap().rearrange("(c p) t -> p c t", p=128)
            for i in range(NT):
                load_xT(xT, i)
                rms_rstd(xT[:], TT, 8, rstd, sqb, 1.0 / D, xT.k(), pb[7])
                for c in range(8):
                    k.stt(hT[:, c, :], xT[:, c, :], gT[:, 2, c:c + 1], rstd[:], ALU.mult, ALU.mult,
                          xT.k() + gT.k() + rstd.k(), hT.k())
                ng = 0
                for j0 in range(0, 29, 4):
                    nj = min(4, 29 - j0)
                    s = st[ng % 2]
                    ng += 1
                    for jj in range(nj):
                        j = j0 + jj
                        bank = pb[j % 4]
                        wkey = wi.k(0 if j < 8 else (1 if j < 16 else (2 if j == 16 else 3)))
                        for c in range(8):
                            k.mm(bank[:, :TT], wi[:, c, j * 128:(j + 1) * 128], hT[:, c, :], c == 0, c == 7,
                                 wkey + hT.k(), bank.k())
                        k.cp(s[:, jj, :], bank[:, :TT], bank.k(), s.k(), eng=("act" if j % 2 else "dve"))
                    k.dma("sp", PTv[:, j0:j0 + nj, i * TT:(i + 1) * TT], s[:, :nj, :], s.k(), PT0.k(i))
                for j in range(TT // 128):
                    for c in range(8):
                        k.mm(pb[6][:, j * 8:(j + 1) * 8], hT[:, c, j * 128:(j + 1) * 128],
                             wi[:, c, 2048:2056], c == 0, c == 7, wi.k(2) + hT.k(), pb[6].k())
                k.cp(bast[:], pb[6][:, 0:24].rearrange("p (j e) -> p j e", e=8), pb[6].k(), bast.k())
                k.dma("sp", ba0.ap().rearrange("(n j p) e -> n p j e", p=128, j=TT // 128)[i], bast[:],
                      bast.k(), ba0.k(i))


    def bc3(ap2, n):
        return ap2.unsqueeze(2).to_broadcast([ap2.shape[0], ap2.shape[1], n])

    def bcm(ap2, n):
        return ap2.unsqueeze(1).to_broadcast([ap2.shape[0], n, ap2.shape[1]])

    def load_colsT(dst, src2d, R, nch, bank):
        with k.scope():
            stg = k.sbuf("stgT", [R, nch * 128], F32)
            k.dma("sp", stg[:], src2d, [], stg.k())
            for c0 in range(0, nch, 4):
                ncc = min(4, nch - c0)
                for cc in range(ncc):
                    k.tr(bank[:, cc * R:(cc + 1) * R], stg[:, (c0 + cc) * 128:(c0 + cc + 1) * 128],
                         ident[:R, :R], stg.k() + cst.k(), bank.k())
                k.cp(dst[:, c0:c0 + ncc, :], bank[:, 0:ncc * R].rearrange("p (c r) -> p c r", r=R),
                     bank.k(), dst.k())

    def store_rowsT(dst2d, src3, R, nch, dst_keys, src_keys):
        with k.scope():
            stg = k.sbuf("stgR", [R, nch * 128], F32)
            for c0 in range(0, nch, 4):
                ncc = min(4, nch - c0)
                bank = pb[(c0 // 4) % 2]
                for cc in range(ncc):
                    k.tr(bank[:R, cc * 128:(cc + 1) * 128], src3[:, c0 + cc, :], ident,
                         src_keys + cst.k(), bank.k())
                k.cp(stg[:, c0 * 128:(c0 + ncc) * 128], bank[:R, 0:ncc * 128], bank.k(), stg.k())
            k.dma("sp", dst2d, stg[:], stg.k(), dst_keys)

    def mix0_stage():
        NCH = Ttot // 128
        NPC = NCH - 1
        with k.scope():
            PTv = PT0.ap().rearrange("(c p) t -> p c t", p=128)
            OTv = OT.ap().rearrange("(c p) t -> p c t", p=128)
            ba = k.sbuf("ba", [128, NCH, 8], F32)
            k.dma("sp", ba[:], ba0.ap().rearrange("(n p) e -> p n e", p=128), ba0.k(), ba.k())
            dtb = k.sbuf("dtb", [128, 4], F32)
            nal = k.sbuf("nal", [128, 4], F32)
            k.dma("sp", dtb[:], ev_dt_bias.ap().partition_broadcast(128).rearrange("p a b -> p (a b)"),
                  [], dtb.k())
            k.dma("sp", nal[:], ev_a_log.ap().partition_broadcast(128).rearrange("p a b -> p (a b)"),
                  [], nal.k())
            k.act(nal[:], nal[:], AF.Exp, nal.k(), nal.k())
            k.ts(nal[:], nal[:], -1.0, None, ALU.mult, None, nal.k(), nal.k())
            beta = k.sbuf("beta", [128, NCH, 4], F32)
            nbeta = k.sbuf("nbeta", [128, NCH, 4], F32)
            g = k.sbuf("g", [128, NCH, 4], F32)
            k.act(beta[:], ba[:, :, 0:4], AF.Sigmoid, ba.k(), beta.k())
            k.ts(nbeta[:], beta[:], -1.0, None, ALU.mult, None, beta.k(), nbeta.k())
            k.tt(g[:], ba[:, :, 4:8], bcm(dtb[:], NCH), ALU.add, ba.k() + dtb.k(), g.k())
            k.act(g[:], g[:], AF.Exp, g.k(), g.k())
            k.act(g[:], g[:], AF.Ln, g.k(), g.k(), bias=1.0)
            k.tt(g[:], g[:], bcm(nal[:], NCH), ALU.mult, g.k() + nal.k(), g.k())
            gcc = k.sbuf("gcc", [128, NCH, 4], F32)
            glt = k.sbuf("glt", [128, NCH, 4], F32)
            gf = g[:].rearrange("p n h -> p (n h)")
            k.mm(pb[0][:, 0:NPC * 4], cst[:, C_U, :], gf[:, 0:NPC * 4], True, True, cst.k() + g.k(), pb[0].k())
            k.mm(pb[0][:, NPC * 4:NCH * 4], cst[:, C_UB, :], gf[:, NPC * 4:NCH * 4], True, True, cst.k() + g.k(), pb[0].k())
            k.cp(gcc[:], pb[0][:, 0:NCH * 4].rearrange("p (n h) -> p n h", h=4), pb[0].k(), gcc.k())
            k.mm(pb[1][:, 0:NPC * 4], cst[:, C_ONES, :], gf[:, 0:NPC * 4], True, True, cst.k() + g.k(), pb[1].k())
            k.mm(pb[1][:, NPC * 4:NCH * 4], cst[:, C_BLKB, :], gf[:, NPC * 4:NCH * 4], True, True, cst.k() + g.k(), pb[1].k())
            k.cp(glt[:], pb[1][:, 0:NCH * 4].rearrange("p (n h) -> p n h", h=4), pb[1].k(), glt.k())
            bge = k.sbuf("bge", [128, NCH, 4], F32)
            etl = k.sbuf("etl", [128, NCH, 4], F32)
            dl = k.sbuf("dl", [128, NCH, 4], F32)
            k.act(bge[:], gcc[:], AF.Exp, gcc.k(), bge.k())
            k.tt(bge[:], bge[:], beta[:], ALU.mult, bge.k() + beta.k(), bge.k())
            k.tt(etl[:], glt[:], gcc[:], ALU.subtract, glt.k() + gcc.k(), etl.k())
            k.act(etl[:], etl[:], AF.Exp, etl.k(), etl.k())
            k.act(dl[:], glt[:], AF.Exp, glt.k(), dl.k())
            gm = k.sbuf("gm", [128, 16, 4], F32)
            dls = k.sbuf("dls", [128, 16, 4], F32)
            k.tt(gm[:], bc3(cst[:, C_SEQM, 0:16], 4), bcm(g[:, NPC, :], 16), ALU.mult, cst.k() + g.k(), gm.k())
            k.mm(pb[2][:, 0:64], cst[:, C_ONES, :], gm[:].rearrange("p s h -> p (s h)"), True, True, cst.k() + gm.k(), pb[2].k())
            k.act(dls[:], pb[2][:, 0:64].rearrange("p (s h) -> p s h", h=4), AF.Exp, pb[2].k(), dls.k())
            wca = k.sbuf("wca", [128, 12, 4], F32)
            load_colsT(wca, ev_a_conv.ap(), 4, 12, pb[3])
            wcb = k.sbuf("wcb", [128, 4, 3], F32)
            load_colsT(wcb, ev_b_conv.ap(), 3, 4, pb[3])
            an = k.sbuf("an", [128, 1, 1], F32)
            load_colsT(an, ev_a_norm.ap(), 1, 1, pb[3])
            hista = k.sbuf("hista", [128, 12, 48], F32)
            load_colsT(hista, st_conv_a.ap(), 48, 12, pb[3])
            histb = k.sbuf("histb", [128, 4, 32], F32)
            load_colsT(histb, st_conv_b.ap(), 32, 4, pb[3])
            Sp = k.sbuf("Sp", [128, 1, 4, 128], F32)
            Ss = k.sbuf("Ss", [128, 16, 4, 128], F32)
            k.memset(Sp[:], 0.0, Sp.k())
            k.dma("sp", Ss[:], st_delta.ap().rearrange("s h k v -> k s h v"), [], Ss.k())
            xpb = k.sbuf("xpb", [128, 12, 131], BF16)
            xpsb = k.sbuf("xpsb", [128, 12, 16, 11], BF16)
            xtmp = k.sbuf("xtmp", [128, 12, 128], F32)
            xlast = k.sbuf("xlast", [128, 12, 3], F32)
            dw = k.sbuf("dw", [128, 12, 4, 128], BF16)
            for c in range(12):
                for i in range(4):
                    k.ts(dw[:, c, i, :], ident, wca[:, c, i:i + 1], None, ALU.mult, None, cst.k() + wca.k(), dw.k())
            qkvs = k.sbuf("qkvs", [128, 12, 128], F32)
            sq = k.sbuf("sq", [128, 8, 128], F32)
            rinv = k.sbuf("rinv", [128, 8, 128], F32)
            HS = []
            for hi_ in range(2):
                Hh = Ctx()
                for nm in ("qn", "kbg", "ktl", "vb", "TTm", "iT", "Ee"):
                    setattr(Hh, nm, k.sbuf(nm + "h", [128, 4, 128], F32))
                HS.append(Hh)
            kn = k.sbuf("kn", [128, 4, 128], F32)
            gb = k.sbuf("gb", [128, 4, 128], F32)
            md = k.sbuf("md", [128, 4, 128], F32)
            mx = k.sbuf("mx", [128, 4, 128], F32)
            Dm = k.sbuf("Dm", [128, 4, 128], F32)
            DTm = k.sbuf("DTm", [128, 4, 128], F32)
            nbm = k.sbuf("nbm", [128, 4, 128], F32)
            Mm = k.sbuf("Mm", [128, 4, 128], F32)
            MT = k.sbuf("MT", [128, 4, 128], F32)
            wTn = k.sbuf("wTn", [128, 4, 128], F32)
            vnT = k.sbuf("vnT", [128, 4, 128], F32)
            vnew = k.sbuf("vnew", [128, 4, 128], F32)
            qd = k.sbuf("qd", [128, 4, 128], F32)
            VM = k.sbuf("VM", [128, 16, 128], F32)
            osq = k.sbuf("osq", [128, 4, 128], F32)
            r2 = k.sbuf("r2", [128, 4, 128], F32)
            zT = k.sbuf("zT", [128, 4, 128], F32)
            o1 = k.sbuf("o1", [128, 4, 128], F32)
            ob = k.sbuf("ob", [128, 4, 128], BF16)
            hb = k.sbuf("hb", [128, 4, 130], F32)
            gcb = k.sbuf("gcb", [128, 4, 130], F32)
            gbb = k.sbuf("gbb", [128, 4, 128], F32)
            mp = k.sbuf("mp", [128, 4, 130], F32)
            ms = k.sbuf("ms", [128, 4, 16, 10], F32)
            yb = k.sbuf("yb", [128, 4, 128], F32)
            yb2 = k.sbuf("yb2", [128, 4, 128], F32)
            ob2 = k.sbuf("ob2", [128, 4, 128], BF16)
            catmp = k.sbuf("catmp", [128, 12, 48], F32)
            cbtmp = k.sbuf("cbtmp", [128, 4, 32], F32)
            U_ = {0: cst[:, C_U, :], 1: cst[:, C_UB, :]}
            LSN_ = {0: cst[:, C_LSN, :], 1: cst[:, C_LSNB, :]}
            b4 = lambda i: pb[i][:].rearrange("p (h t) -> p h t", h=4)

            def chunk_params(n):
                ss_ = 1 if n == NPC else 0
                return ss_, n * 128, ((16, 8) if ss_ else (1, 128)), (Ss if ss_ else Sp)

            def P1(n):
                ss_, t0, (nseq, L), S = chunk_params(n)
                Hn = HS[n % 2]
                qn, kbg, ktl, vb, TTm, iT, Ee = Hn.qn, Hn.kbg, Hn.ktl, Hn.vb, Hn.TTm, Hn.iT, Hn.Ee
                if not ss_:
                    if n == 0:
                        k.memset(xpb[:, :, 0:3], 0.0, xpb.k())
                        k.dma("pool", xpb[:, :, 3:131], PTv[:, 0:12, 0:128], PT0.k(), xpb.k())
                    else:
                        k.dma("pool", xpb[:], PTv[:, 0:12, t0 - 3:t0 + 128], PT0.k(), xpb.k())
                    xk = xpb.k()
                else:
                    k.dma("sp", xtmp[:], PTv[:, 0:12, t0:t0 + 128], PT0.k(), xtmp.k())
                    k.cp(xpsb[:, :, :, 0:3], hista[:].rearrange("p c (s j) -> p c s j", j=3), hista.k(), xpsb.k())
                    k.cp(xpsb[:, :, :, 3:11], xtmp[:].rearrange("p c (s j) -> p c s j", j=8), xtmp.k(), xpsb.k(),
                         eng="act")
                    xk = xpsb.k()
                for c in range(12):
                    bank = pb[c // 4]
                    for i in range(4):
                        if not ss_:
                            o_ = bank[:, (c % 4) * 128:(c % 4 + 1) * 128]
                            r_ = xpb[:, c, i:i + 128]
                        else:
                            o_ = bank[:, (c % 4) * 128:(c % 4 + 1) * 128].rearrange("p (s j) -> p s j", j=8)
                            r_ = xpsb[:, c, :, i:i + 8]
                        k.mm(o_, dw[:, c, i, :], r_, i == 0, i == 3, dw.k() + xk, bank.k())
                for q3 in range(3):
                    k.act(qkvs[:, q3 * 4:q3 * 4 + 4, :], b4(q3), AF.Silu, pb[q3].k(), qkvs.k())
                if n == NPC - 1:
                    k.dma("sp", xlast[:], PTv[:, 0:12, TP - 3:TP], PT0.k(), xlast.k())
                    store_rowsT(conv_a_p.ap(), xlast[:], 3, 12, conv_a_p.k(), xlast.k())
                if ss_:
                    k.cp(catmp[:].rearrange("p c (s j) -> p c s j", j=3),
                         xtmp[:].rearrange("p c (s j) -> p c s j", j=8)[:, :, :, 5:8], xtmp.k(), catmp.k())
                    store_rowsT(conv_a_s.ap(), catmp[:], 48, 12, conv_a_s.k(), catmp.k())
                k.act(sq[:], qkvs[:, 0:8, :], AF.Square, qkvs.k(), sq.k())
                for c in range(8):
                    bank = pb[c // 4]
                    k.mm(bank[:, (c % 4) * 128:(c % 4 + 1) * 128], cst[:, C_ONES, :], sq[:, c, :], True, True,
                         cst.k() + sq.k(), bank.k())
                for hf in range(2):
                    k.act(rinv[:, hf * 4:hf * 4 + 4, :], b4(hf), AF.Sqrt, pb[hf].k(), rinv.k(), bias=EPS)
                k.recip(rinv[:], rinv[:], rinv.k(), rinv.k())
                k.stt(qn[:], qkvs[:, 0:4, :], 128.0 ** -0.5, rinv[:, 0:4, :], ALU.mult, ALU.mult,
                      qkvs.k() + rinv.k(), qn.k())
                k.tt(kn[:], qkvs[:, 4:8, :], rinv[:, 4:8, :], ALU.mult, qkvs.k() + rinv.k(), kn.k())
                for h in range(4):
                    k.tr(pb[2][:, h * 128:(h + 1) * 128], kn[:, h, :], ident, kn.k() + cst.k(), pb[2].k())
                    k.tr(pb[3][:, h * 128:(h + 1) * 128], qkvs[:, 8 + h, :], ident, qkvs.k() + cst.k(), pb[3].k())
                k.tt(kbg[:], b4(2), bc3(bge[:, n, :], 128), ALU.mult, pb[2].k() + bge.k(), kbg.k())
                k.tt(ktl[:], b4(2), bc3(etl[:, n, :], 128), ALU.mult, pb[2].k() + etl.k(), ktl.k())
                k.tt(vb[:], b4(3), bc3(beta[:, n, :], 128), ALU.mult, pb[3].k() + beta.k(), vb.k())
                k.tt(gb[:], bcm(cst[:, C_ONES, :], 4), bc3(g[:, n, :], 128), ALU.mult, cst.k() + g.k(), gb.k())
                for h in range(4):
                    k.mm(pb[4][:, h * 128:(h + 1) * 128], gb[:, h, :], U_[ss_], True, True,
                         gb.k() + cst.k(), pb[4].k())
                k.tt(md[:], b4(4), bc3(gcc[:, n, :], 128), ALU.subtract, pb[4].k() + gcc.k(), md.k())
                k.ts(mx[:], md[:], 0.0, None, ALU.max, None, md.k(), mx.k())
                k.act(Dm[:], mx[:], AF.Exp, mx.k(), Dm.k(), scale=-1.0)
                k.ts(mx[:], md[:], 0.0, None, ALU.min, None, md.k(), mx.k())
                k.act(DTm[:], mx[:], AF.Exp, mx.k(), DTm.k())
                k.act(Ee[:], b4(4), AF.Exp, pb[4].k(), Ee.k())
                for h in range(4):
                    k.mm(pb[0][:, h * 128:(h + 1) * 128], kn[:, h, :], kn[:, h, :], True, True, kn.k(), pb[0].k())
                    k.mm(pb[1][:, h * 128:(h + 1) * 128], kn[:, h, :], qn[:, h, :], True, True,
                         kn.k() + qn.k(), pb[1].k())
                k.tt(nbm[:], bcm(LSN_[ss_], 4), bc3(beta[:, n, :], 128), ALU.mult, cst.k() + beta.k(), nbm.k())
                k.tt(Mm[:], b4(0), Dm[:], ALU.mult, pb[0].k() + Dm.k(), Mm.k())
                k.tt(Mm[:], Mm[:], nbm[:], ALU.mult, Mm.k() + nbm.k(), Mm.k())
                k.tt(iT[:], b4(1), DTm[:], ALU.mult, pb[1].k() + DTm.k(), iT.k())
                k.tt(iT[:], iT[:], bcm(U_[ss_], 4), ALU.mult, iT.k() + cst.k(), iT.k())
                for h in range(4):
                    k.tr(pb[3][:, h * 128:(h + 1) * 128], Mm[:, h, :], ident, Mm.k() + cst.k(), pb[3].k())
                k.cp(MT[:], b4(3), pb[3].k(), MT.k())
                k.tt(TTm[:], MT[:], bcm(ident, 4), ALU.add, MT.k() + cst.k(), TTm.k())
                nit = 2 if ss_ else 6
                for it in range(1, nit + 1):
                    for h in range(4):
                        k.mm(pb[0][:, h * 128:(h + 1) * 128], MT[:, h, :], Mm[:, h, :], True, True,
                             MT.k() + Mm.k(), pb[0].k())
                    if it < nit:
                        for h in range(4):
                            k.mm(pb[1][:, h * 128:(h + 1) * 128], Mm[:, h, :], MT[:, h, :], True, True,
                                 MT.k() + Mm.k(), pb[1].k())
                    k.cp(Mm[:], b4(0), pb[0].k(), Mm.k())
                    if it < nit:
                        k.cp(MT[:], b4(1), pb[1].k(), MT.k(), eng="act")
                    for h in range(4):
                        k.mm(pb[2][:, h * 128:(h + 1) * 128], Mm[:, h, :], TTm[:, h, :], True, True,
                             Mm.k() + TTm.k(), pb[2].k())
                    k.tt(TTm[:], TTm[:], b4(2), ALU.add, TTm.k() + pb[2].k(), TTm.k())
                k.dma("sp", gbb[:], PTv[:, 21:25, t0:t0 + 128], PT0.k(), gbb.k())
                if not ss_:
                    if n == 0:
                        k.memset(hb[:, :, 0:2], 0.0, hb.k())
                        k.memset(gcb[:, :, 0:2], 0.0, gcb.k())
                        k.dma("sp", hb[:, :, 2:130], PTv[:, 17:21, 0:128], PT0.k(), hb.k())
                        k.dma("sp", gcb[:, :, 2:130], PTv[:, 25:29, 0:128], PT0.k(), gcb.k())
                    else:
                        k.dma("sp", hb[:], PTv[:, 17:21, t0 - 2:t0 + 128], PT0.k(), hb.k())
                        k.dma("sp", gcb[:], PTv[:, 25:29, t0 - 2:t0 + 128], PT0.k(), gcb.k())
                    k.tt(mp[:], hb[:], gcb[:], ALU.mult, hb.k() + gcb.k(), mp.k())
                    mv = mp[:].unsqueeze(2)
                    mk = mp.k()
                    if n == NPC - 1:
                        store_rowsT(conv_b_p.ap(), mp[:, :, 128:130], 2, 4, conv_b_p.k(), mp.k())
                else:
                    k.dma("sp", hb[:, :, 0:128], PTv[:, 17:21, t0:t0 + 128], PT0.k(), hb.k())
                    k.dma("sp", gcb[:, :, 0:128], PTv[:, 25:29, t0:t0 + 128], PT0.k(), gcb.k())
                    k.tt(ms[:, :, :, 2:10], hb[:, :, 0:128].rearrange("p c (s j) -> p c s j", j=8),
                         gcb[:, :, 0:128].rearrange("p c (s j) -> p c s j", j=8), ALU.mult,
                         hb.k() + gcb.k(), ms.k())
                    k.cp(ms[:, :, :, 0:2], histb[:].rearrange("p c (s j) -> p c s j", j=2), histb.k(), ms.k())
                    mv = ms[:]
                    mk = ms.k()
                    k.cp(cbtmp[:].rearrange("p c (s j) -> p c s j", j=2), ms[:, :, :, 8:10], ms.k(), cbtmp.k())
                    store_rowsT(conv_b_s.ap(), cbtmp[:], 32, 4, conv_b_s.k(), cbtmp.k())
                y4 = yb[:].rearrange("p c (s j) -> p c s j", j=L)
                y24 = yb2[:].rearrange("p c (s j) -> p c s j", j=L)
                for i in range(3):
                    wv = wcb[:, :, i].unsqueeze(2).unsqueeze(3).to_broadcast([128, 4, nseq, L])
                    k.tt(y4 if i == 0 else y24, mv[:, :, :, i:i + L], wv, ALU.mult, mk + wcb.k(),
                         (yb if i == 0 else yb2).k())
                    if i:
                        k.tt(yb[:], yb[:], yb2[:], ALU.add, yb.k() + yb2.k(), yb.k())
                k.tt(ob2[:], yb[:], gbb[:], ALU.mult, yb.k() + gbb.k(), ob2.k())
                k.dma("sp", OTv[:, 4:8, t0:t0 + 128], ob2[:], ob2.k(), OT.k(n))

            def P2(n):
                ss_, t0, (nseq, L), S = chunk_params(n)
                Hn = HS[n % 2]
                qn, kbg, ktl, vb, TTm, iT, Ee = Hn.qn, Hn.kbg, Hn.ktl, Hn.vb, Hn.TTm, Hn.iT, Hn.Ee
                for h in range(4):
                    k.mm(pb[5][:, h * 128:(h + 1) * 128], kbg[:, h, :], TTm[:, h, :], True, True,
                         kbg.k() + TTm.k(), pb[5].k())
                k.ts(wTn[:], b4(5), -1.0, None, ALU.mult, None, pb[5].k(), wTn.k())
                for h in range(4):
                    o_ = pb[6][:, h * 128:(h + 1) * 128]
                    k.mm(o_, vb[:, h, :], TTm[:, h, :], True, False, vb.k() + TTm.k(), pb[6].k())
                    for j in range(nseq):
                        k.mm(pb[6][:, h * 128 + j * L:h * 128 + (j + 1) * L], S[:, j, h, :],
                             wTn[:, h, j * L:(j + 1) * L], False, j == nseq - 1, S.k() + wTn.k(), pb[6].k())
                k.cp(vnT[:], b4(6), pb[6].k(), vnT.k())
                for h in range(4):
                    k.tr(pb[7][:, h * 128:(h + 1) * 128], vnT[:, h, :], ident, vnT.k() + cst.k(), pb[7].k())
                k.cp(vnew[:], b4(7), pb[7].k(), vnew.k(), eng="act")
                k.tt(qd[:], qn[:], Ee[:], ALU.mult, qn.k() + Ee.k(), qd.k())
                for h in range(4):
                    o_ = pb[5][:, h * 128:(h + 1) * 128]
                    k.mm(o_, vnew[:, h, :], iT[:, h, :], True, False, vnew.k() + iT.k(), pb[5].k())
                    for j in range(nseq):
                        k.mm(pb[5][:, h * 128 + j * L:h * 128 + (j + 1) * L], S[:, j, h, :],
                             qd[:, h, j * L:(j + 1) * L], False, j == nseq - 1, S.k() + qd.k(), pb[5].k())
                if not ss_:
                    for h in range(4):
                        k.mm(pb[6][:, h * 128:(h + 1) * 128], ktl[:, h, :], vnew[:, h, :], True, True,
                             ktl.k() + vnew.k(), pb[6].k())
                    k.tt(Sp[:, 0], Sp[:, 0], bc3(dl[:, n, :], 128), ALU.mult, Sp.k() + dl.k(), Sp.k())
                    k.tt(Sp[:, 0], Sp[:, 0], b4(6), ALU.add, Sp.k() + pb[6].k(), Sp.k())
                else:
                    for h in range(4):
                        k.tt(VM[:], bcm(vnew[:, h, :], 16), bc3(cst[:, C_SEQM, 0:16], 128), ALU.mult,
                             vnew.k() + cst.k(), VM.k())
                        for q4 in range(4):
                            bank = pb[6 + q4 % 2]
                            k.mm(bank[:], ktl[:, h, :], VM[:, q4 * 4:q4 * 4 + 4, :].rearrange("p s v -> p (s v)"), True, True,
                                 ktl.k() + VM.k(), bank.k())
                            sv = Ss[:, q4 * 4:q4 * 4 + 4, h, :]
                            k.tt(sv, sv, bc3(dls[:, q4 * 4:q4 * 4 + 4, h], 128), ALU.mult, Ss.k() + dls.k(), Ss.k())
                            k.tt(sv, sv, bank[:].rearrange("p (s v) -> p s v", v=128), ALU.add,
                                 Ss.k() + bank.k(), Ss.k())
                k.act(osq[:], b4(5), AF.Square, pb[5].k(), osq.k())
                for h in range(4):
                    k.mm(pb[7][:, h * 128:(h + 1) * 128], cst[:, C_ONES, :], osq[:, h, :], True, True,
                         cst.k() + osq.k(), pb[7].k())
                k.act(r2[:], b4(7), AF.Sqrt, pb[7].k(), r2.k(), scale=1.0 / 128, bias=EPS)
                k.recip(r2[:], r2[:], r2.k(), r2.k())
                k.dma("sp", zT[:], PTv[:, 12:16, t0:t0 + 128], PT0.k(), zT.k())
                k.act(zT[:], zT[:], AF.Silu, zT.k(), zT.k())
                k.stt(o1[:], b4(5), an[:, 0, 0:1], r2[:], ALU.mult, ALU.mult, pb[5].k() + an.k() + r2.k(), o1.k())
                k.tt(ob[:], o1[:], zT[:], ALU.mult, o1.k() + zT.k(), ob.k())
                k.dma("sp", OTv[:, 0:4, t0:t0 + 128], ob[:], ob.k(), OT.k(n))

            def zipped(fa, fb):
                la = k.capture(fa)
                lb = k.capture(fb)
                out = []
                for i_ in range(max(len(la), len(lb))):
                    if i_ < len(la):
                        out.append(la[i_])
                    if i_ < len(lb):
                        out.append(lb[i_])
                k.ops.extend(out)

            P1(0)
            for n in range(NCH):
                if n + 1 < NCH:
                    zipped(lambda: P2(n), lambda: P1(n + 1))
                else:
                    P2(n)
            k.dma("sp", delta_p.ap().rearrange("h k v -> k h v"), Sp[:, 0], Sp.k(), delta_p.k())
            k.dma("sp", delta_s.ap().rearrange("s h k v -> k s h v"), Ss[:], Ss.k(), delta_s.k())


    def gelu_tanh(dst, src, t1, keys_src, keys_dst, keys_t1, eng="dve"):
        k.tt(t1, src, src, ALU.mult, keys_src, keys_t1, eng=eng)
        k.ts(t1, t1, 0.044715, 1.0, ALU.mult, ALU.add, keys_t1, keys_t1, eng=eng)
        k.tt(t1, t1, src, ALU.mult, keys_t1 + keys_src, keys_t1, eng=eng)
        k.act(t1, t1, AF.Sigmoid, keys_t1, keys_t1, scale=1.5957691216057308)
        k.tt(dst, src, t1, ALU.mult, keys_src + keys_t1, keys_dst, eng=eng)

    def win1_stage():
        with k.scope():
            wi = k.sbuf("wi1", [128, 8, 2328], BF16)
            for c in range(8):
                r = slice(c * 128, (c + 1) * 128)
                for hc in range(4):
                    k.dma("pool", wi[:, c, hc * 128:hc * 128 + 64], od_w_in[r, hc * 64:hc * 64 + 64],
                          od_w_in.k(), wi.k())
                    k.dma("pool", wi[:, c, hc * 128 + 64:hc * 128 + 128], od_w_in[r, (4 + hc) * 64:(5 + hc) * 64],
                          od_w_in.k(), wi.k())
                k.dma("pool", wi[:, c, 512:2328], od_w_in[r, 512:2328], od_w_in.k(), wi.k())
            lng = k.sbuf("lng", [128, 512], F32)
            lnb = k.sbuf("lnb", [128, 512], F32)
            k.dma("sp", lng[:], od_d_ln_g.ap().partition_broadcast(128).rearrange("p a b -> p (a b)"), [], lng.k())
            k.dma("sp", lnb[:], od_d_ln_b.ap().partition_broadcast(128).rearrange("p a b -> p (a b)"), [], lnb.k())
            xTs = [k.sbuf("xT", [128, 8, TT], F32) for _ in range(2)]
            hTs = [k.sbuf("hT", [128, 8, TT], BF16) for _ in range(2)]
            sqb = k.sbuf("sqb", [128, 8, TT], BF16)
            rstd = k.sbuf("rstd", [128, TT], F32)

            def pro(i):
                xT, hT = xTs[i % 2], hTs[i % 2]
                load_xT(xT, i)
                rms_rstd(xT[:], TT, 8, rstd, sqb, 1.0 / D, xT.k(), pb[7])
                for c in range(8):
                    k.stt(hT[:, c, :], xT[:, c, :], gT[:, 8, c:c + 1], rstd[:], ALU.mult, ALU.mult,
                          xT.k() + gT.k() + rstd.k(), hT.k())
            fst = k.sbuf("fst", [128, 8, TT], BF16)
            ust = k.sbuf("ust", [128, 512], F32)
            ut1 = k.sbuf("ut1", [128, 512], F32)
            kvst = k.sbuf("kvst", [128, 768], F32)
            gst = k.sbuf("gst", [128, 24], F32)
            vst = k.sbuf("vst", [128, 512], F32)
            vt1 = k.sbuf("vt1", [128, 512], F32)
            st1 = k.sbuf("st1", [128, 4], F32)
            FTv = FT.ap().rearrange("(c p) t -> p c t", p=128)
            fchunks = [0, 128, 256, 384, 536, 664, 792, 1048]
            pro(0)
            for i in range(NT):
                hT = hTs[i % 2]
                for j, c0 in enumerate(fchunks):
                    bank = pb[j % 4]
                    for c in range(8):
                        k.mm(bank[:, :TT], wi[:, c, c0:c0 + 128], hT[:, c, :], c == 0, c == 7,
                             wi.k() + hT.k(), bank.k())
                    if j < 4:
                        k.act(fst[:, j, :], bank[:, :TT], AF.Copy, bank.k(), fst.k(), scale=0.125)
                    else:
                        k.cp(fst[:, j, :], bank[:, :TT], bank.k(), fst.k())
                k.dma("sp", FTv[:, :, i * TT:(i + 1) * TT], fst[:], fst.k(), FT.k(i))
                if i + 1 < NT:
                    pro(i + 1)
                for j in range(TT // 128):
                    b = i * (TT // 128) + j
                    t0 = b * 128
                    hs = hT[:, :, j * 128:(j + 1) * 128]
                    for (bank, c0, n) in ((pb[4], 536, 512), (pb[5], 1048, 256), (pb[6], 1816, 512), (pb[5], 512, 24)):
                        off = 256 if c0 == 512 else 0
                        for c in range(8):
                            k.mm(bank[:, off:off + n], hs[:, c, :], wi[:, c, c0:c0 + n], c == 0, c == 7,
                                 wi.k() + hT.k(), bank.k())
                    k.cp(kvst[:, 0:512], pb[4][:, 0:512], pb[4].k(), kvst.k())
                    k.cp(kvst[:, 512:768], pb[5][:, 0:256], pb[5].k(), kvst.k(), eng="act")
                    k.act(gst[:], pb[5][:, 256:280], AF.Sigmoid, pb[5].k(), gst.k())
                    k.dma("sp", KVtok[t0:t0 + 128, :], kvst[:], kvst.k(), KVtok.k(b))
                    k.dma("sp", GTok[t0:t0 + 128, :], gst[:], gst.k(), GTok.k(b))
                    if t0 < TP:
                        k.dma("sp", cmp_p[t0:t0 + 128, :], kvst[:, 0:256], kvst.k(), cmp_p.k())
                        k.dma("sp", sel_p[t0:t0 + 128, :], kvst[:, 256:512], kvst.k(), sel_p.k())
                        if t0 >= TP - 512:
                            w0 = t0 - (TP - 512)
                            k.dma("sp", win_p[w0:w0 + 128, :], kvst[:, 512:768], kvst.k(), win_p.k())
                    else:
                        k.dma("sp", cmp_s.ap(), kvst[:, 0:256], kvst.k(), cmp_s.k())
                        k.dma("sp", sel_s.ap(), kvst[:, 256:512], kvst.k(), sel_s.k())
                        for sq_ in range(NS):
                            k.dma("sp", win_s[sq_, 504:512, :], kvst[sq_ * 8:(sq_ + 1) * 8, 512:768],
                                  kvst.k(), win_s.k(1))
                    for c in range(8):
                        k.mm(pb[3][:, 0:512], hs[:, c, :], wi[:, c, 1304:1816], c == 0, c == 7,
                             wi.k() + hT.k(), pb[3].k())
                    k.cp(ust[:], pb[3][:, 0:512], pb[3].k(), ust.k(), eng="act")
                    gelu_tanh(ust[:], ust[:], ut1[:], ust.k(), ust.k(), ut1.k(), eng="pool")
                    k.dma("sp", UTok[t0:t0 + 128, :], ust[:], ust.k(), UTok.k(b))
                    k.cp(vst[:], pb[6][:, 0:512], pb[6].k(), vst.k())
                    gelu_tanh(vst[:], vst[:], vt1[:], vst.k(), vst.k(), vt1.k())
                    k.op("dve", lambda: nc.vector.reduce_sum(out=st1[:, 0:1], in_=vst[:], axis=AX.X), vst.k(), st1.k())
                    k.ts(st1[:, 0:1], st1[:, 0:1], 1.0 / 512, None, ALU.mult, None, st1.k(), st1.k())
                    k.ts(vst[:], vst[:], st1[:, 0:1], None, ALU.subtract, None, vst.k() + st1.k(), vst.k())
                    k.tt(vt1[:], vst[:], vst[:], ALU.mult, vst.k(), vt1.k())
                    k.op("dve", lambda: nc.vector.reduce_sum(out=st1[:, 1:2], in_=vt1[:], axis=AX.X), vt1.k(), st1.k())
                    k.act(st1[:, 2:3], st1[:, 1:2], AF.Sqrt, st1.k(), st1.k(), scale=1.0 / 512, bias=EPS)
                    k.recip(st1[:, 3:4], st1[:, 2:3], st1.k(), st1.k())
                    k.stt(vst[:], vst[:], st1[:, 3:4], lng[:], ALU.mult, ALU.mult, vst.k() + st1.k() + lng.k(), vst.k())
                    k.tt(vst[:], vst[:], lnb[:], ALU.add, vst.k() + lnb.k(), vst.k())
                    k.dma("sp", VTok[t0:t0 + 128, :], vst[:], vst.k(), VTok.k(b))
                    if t0 >= TP:
                        k.dma("sp", dv_s.ap(), vst[:], vst.k(), dv_s.k())
            wst = k.sbuf("wst", [126, NS, 4, 256], F32)
            k.dma("sp", wst[:], cache_win[:, 8:512, :].rearrange("s (p j) c -> p s j c", j=4), [], wst.k())
            k.dma("sp", win_s[:, 0:504, :].rearrange("s (p j) c -> p s j c", j=4), wst[:], wst.k(), win_s.k(0))


    def cmlp_stage():
        NB = Ttot // 128
        with k.scope():
            wct = k.sbuf("wct", [128, 8, 128], BF16)
            wtmp = k.sbuf("wtmp", [128, 8, 128], F32)
            k.dma("sp", wtmp[:], od_d_ws.ap().rearrange("g t s -> t g s"), [], wtmp.k())
            for g in range(8):
                bank = pb[g // 4]
                k.tr(bank[:, (g % 4) * 128:(g % 4 + 1) * 128], wtmp[:, g, :], ident, wtmp.k() + cst.k(), bank.k())
            for hf in range(2):
                k.tt(wct[:, hf * 4:hf * 4 + 4, :], pb[hf][:].rearrange("p (g t) -> p g t", g=4),
                     bcm(cst[:, C_U, :], 4), ALU.mult, pb[hf].k() + cst.k(), wct.k())
            bsT = k.sbuf("bsT", [128, 1, 8], F32)
            load_colsT(bsT, od_d_bs.ap(), 8, 1, pb[2])
            vt = k.sbuf("vt", [128, 512], F32)
            vtb = k.sbuf("vtb", [128, 512], BF16)
            ut = k.sbuf("ut", [128, 512], F32)
            od = k.sbuf("od", [128, 512], F32)
            for b in range(TP // 128):
                t0 = b * 128
                k.dma("sp", vt[:], VTok[t0:t0 + 128, :], VTok.k(b), vt.k())
                k.dma("sp", ut[:], UTok[t0:t0 + 128, :], UTok.k(b), ut.k())
                k.cp(vtb[:], vt[:], vt.k(), vtb.k())
                for g in range(8):
                    k.mm(pb[3][:, g * 64:(g + 1) * 64], wct[:, g, :], vtb[:, g * 64:(g + 1) * 64], True, True,
                         wct.k() + vtb.k(), pb[3].k())
                k.tt(od[:].rearrange("p (g c) -> p g c", g=8), pb[3][:].rearrange("p (g c) -> p g c", g=8),
                     bc3(bsT[:, 0, :], 64), ALU.add, pb[3].k() + bsT.k(), od.k())
                k.tt(od[:], od[:], ut[:], ALU.mult, od.k() + ut.k(), od.k())
                k.dma("sp", OTok[t0:t0 + 128, 512:1024], od[:], od.k(), OTok.k(2 * b + 1))
            vs = k.sbuf("vs", [8, NS, 512], F32)
            vsb = k.sbuf("vsb", [8, NS, 512], BF16)
            us = k.sbuf("us", [8, NS, 512], F32)
            ods = k.sbuf("ods", [8, NS, 512], F32)
            bS = Ttot // 128 - 1
            k.dma("sp", vs[:], VTok[TP:Ttot, :].rearrange("(s j) c -> j s c", j=8), VTok.k(bS), vs.k())
            k.dma("sp", us[:], UTok[TP:Ttot, :].rearrange("(s j) c -> j s c", j=8), UTok.k(bS), us.k())
            k.cp(vsb[:], vs[:], vs.k(), vsb.k())
            for g in range(8):
                for hf in range(2):
                    bank = pb[4 + hf]
                    k.mm(bank[0:8, :].rearrange("p (s c) -> p s c", c=64), wct[0:8, g, 0:8],
                         vsb[:, hf * 8:hf * 8 + 8, g * 64:(g + 1) * 64], True, True, wct.k() + vsb.k(), bank.k())
                    k.ts(ods[:, hf * 8:hf * 8 + 8, g * 64:(g + 1) * 64],
                         bank[0:8, :].rearrange("p (s c) -> p s c", c=64), bsT[0:8, 0, g:g + 1], None,
                         ALU.add, None, bank.k() + bsT.k(), ods.k())
            k.tt(ods[:], ods[:], us[:], ALU.mult, ods.k() + us.k(), ods.k())
            k.dma("sp", OTok[TP:Ttot, 512:1024].rearrange("(s j) c -> j s c", j=8), ods[:], ods.k(),
                  OTok.k(2 * bS + 1))


    def nsa_common(pad_w1=False):
        A = Ctx()
        A.pad_w1 = pad_w1
        A.kpos = k.sbuf("kposb", [4, 4224], BF16)
        k.dma("pool", A.kpos[:], c_kpos.ap(), [], A.kpos.k())
        A.kposc = k.sbuf("kposcb", [4, 256], BF16)
        k.dma("pool", A.kposc[:], c_kposc.ap(), [], A.kposc.k())
        A.am = k.sbuf("amb", [128, 2, 128], BF16)
        k.dma("pool", A.am[:], c_amask.ap().rearrange("n p f -> p n f"), [], A.am.k())
        A.idb = k.sbuf("idb", [128, 128], BF16)
        k.cp(A.idb[:], ident, cst.k(), A.idb.k())
        A.mcs = k.sbuf("mcsb", [128, 2, 64], BF16)
        k.dma("pool", A.mcs[:], c_mcs.ap().rearrange("(j p) s -> p j s", p=128), [], A.mcs.k())
        A.w1 = k.sbuf("w1dup", [128, 2, 32, 128], BF16)
        for hf in range(2):
            k.dma("pool", A.w1[hf * 64:(hf + 1) * 64], od_cmp_w1.ap().rearrange("k s d h -> d k s h"), [], A.w1.k())
        if pad_w1:
            A.w1p = [k.sbuf("w1p%d" % g, [128, 2, 32, 128], BF16) for g in range(2)]
            for g in range(2):
                k.memset(A.w1p[g][64 * (1 - g):64 * (2 - g)], 0.0, A.w1p[g].k())
                k.dma("pool", A.w1p[g][64 * g:64 * g + 64], od_cmp_w1.ap().rearrange("k s d h -> d k s h"), [],
                      A.w1p[g].k())
        A.w2k = k.sbuf("w2k", [128, 2, 128], BF16)
        k.memset(A.w2k[:], 0.0, A.w2k.k())
        k.dma("pool", A.w2k[:, 0, 0:64], od_cmp_w2[0], [], A.w2k.k())
        k.dma("pool", A.w2k[:, 1, 64:128], od_cmp_w2[0], [], A.w2k.k())
        A.w2v = k.sbuf("w2v", [128, 64], BF16)
        k.dma("pool", A.w2v[:], od_cmp_w2[1], [], A.w2v.k())
        A.cb = k.sbuf("cbias", [128, 2], F32)
        with k.scope():
            pes = k.sbuf("pes", [64, 64], F32)
            k.dma("sp", pes[:], od_cmp_pe.ap().rearrange("k s d -> (k s) d"), [], pes.k())
            k.tr(pb[0][0:64, 0:64], pes[:], ident[:64, :64], pes.k() + cst.k(), pb[0].k())
            peT = k.sbuf("peT", [64, 64], BF16)
            k.cp(peT[:], pb[0][0:64, 0:64], pb[0].k(), peT.k())
            for kind in range(2):
                for s_ in range(32):
                    k.mm(pb[1][:, kind:kind + 1], A.w1[0:64, kind, s_, :], peT[:, kind * 32 + s_:kind * 32 + s_ + 1],
                         s_ == 0, s_ == 31, A.w1.k() + peT.k(), pb[1].k())
            k.cp(A.cb[:], pb[1][:, 0:2], pb[1].k(), A.cb.k())
        A.eT = [k.sbuf("eT%d" % i, [128, 512], BF16) for i in range(2)]
        A.ne = 0
        A.nsb = 0
        return A

    def compress(A, XTk, XTv, nh, KcT, Vc_aug, xkeys):
        Nc = nh - 1
        with k.scope():
            hid = k.sbuf("hidT", [128, 2, 2, 256], BF16)
            bt = k.sbuf("btS", [128, 256], F32)
            sm = k.sbuf("smS", [128, 256], F32)
            k.memset(hid[:], 0.0, hid.k())
            for kind, XT in ((0, XTk), (1, XTv)):
                for g in range(2):
                    for hb, bank in ((0, pb[2]), (16, pb[3])):
                        for s_ in range(16):
                            if A.pad_w1:
                                k.mm(bank[:, 0:nh], A.w1p[g][:, kind, hb + s_, :],
                                     XT[:, s_:s_ + 16 * (nh - 1) + 1:16], s_ == 0, s_ == 15,
                                     A.w1p[g].k() + xkeys, bank.k())
                            else:
                                k.mm(bank[:, 0:nh], A.w1[64 * g:64 * g + 64, kind, hb + s_, :],
                                     XT[64 * g:64 * g + 64, s_:s_ + 16 * (nh - 1) + 1:16], s_ == 0, s_ == 15,
                                     A.w1.k() + xkeys, bank.k())
                    k.cp(bt[:, 0:nh], pb[3][:, 0:nh], pb[3].k(), bt.k())
                    k.tt(sm[:, 0:Nc], pb[2][:, 0:Nc], bt[:, 1:nh], ALU.add, pb[2].k() + bt.k(), sm.k())
                    k.act(hid[:, kind, g, 0:Nc], sm[:, 0:Nc], AF.Silu, sm.k() + A.cb.k(), hid.k(),
                          bias=A.cb[:, kind:kind + 1])
            for g in range(2):
                k.mm(pb[2][:, 0:256], A.w2k[:, g, :], hid[:, 0, g, :], g == 0, g == 1, A.w2k.k() + hid.k(), pb[2].k())
            k.cp(KcT, pb[2][:, 0:256], pb[2].k(), A.kc_keys)
            if getattr(A, "kc_split", None) is not None:
                for g in range(2):
                    k.mm(pb[2][0:64, 256 * 0:256], A.w2k[:, 0, 0:64], hid[:, 0, g, :], True, True,
                         A.w2k.k() + hid.k(), pb[2].k())
                    k.cp(A.kc_split[g][0:64, :], pb[2][0:64, 0:256], pb[2].k(), A.kc_split[g].k())
            for jt in range(2):
                for g in range(2):
                    k.mm(pb[3][:, (jt * 2 + g) * 64:(jt * 2 + g + 1) * 64], hid[:, 1, g, jt * 128:(jt + 1) * 128],
                         A.w2v[:], True, True, A.w2v.k() + hid.k(), pb[3].k())
            k.cp(Vc_aug[:, :, :, 0:64], pb[3][:, 0:256].rearrange("p (j g d) -> p j g d", j=2, g=2),
                 pb[3].k(), Vc_aug.k())

    def att_tile(A, nk, Tq, terms, outs, first, last):
        ncol = 4 * Tq
        bank = pb[A.nsb % 2]
        A.nsb += 1
        o3 = bank[:nk, :ncol].rearrange("p (r t) -> p r t", r=4)
        for i_, (l, r_, keys) in enumerate(terms):
            k.mm(o3, l, r_, i_ == 0, i_ == len(terms) - 1, keys, bank.k())
        eT = A.eT[A.ne % 2]
        A.ne += 1
        k.act(eT[:nk, :ncol], bank[:nk, :ncol], AF.Exp, bank.k(), eT.k())
        att_flush(A)
        A.pend = (nk, Tq, outs, first, last, eT)

    def att_flush(A):
        if getattr(A, "pend", None) is None:
            return
        nk, Tq, outs, first, last, eT = A.pend
        A.pend = None
        for (ob, w, V, vkeys) in outs:
            for r in range(4):
                k.mm(ob[:Tq, r * w:(r + 1) * w], eT[:nk, r * Tq:(r + 1) * Tq], V, first and r == 0, last,
                     eT.k() + vkeys, ob.k(), sgc=True)

    def topk_negsel(A, W, Tq, imp_bank, rden, topc_t, nsT_dst, nsT_keys):
        imp, sc, sc2, m8, m8b, nsl = W.imp, W.sc, W.sc2, W.m8, W.m8b, W.nsl
        i3 = imp_bank[:Tq, 0:256].rearrange("p (r s) -> p r s", r=4)
        k.ts(imp[:Tq], i3[:, 0, :], rden[:Tq, 0:1], None, ALU.mult, None, imp_bank.k() + W.rdk, imp.k())
        for r in range(1, 4):
            k.stt(imp[:Tq], i3[:, r, :], rden[:Tq, r:r + 1], imp[:Tq], ALU.mult, ALU.add,
                  imp_bank.k() + W.rdk + imp.k(), imp.k())
        k.tt(sc[:Tq], imp[:Tq], topc_t[:Tq, 0, :], ALU.mult, imp.k() + W.tck, sc.k())
        k.tt(sc[:Tq], sc[:Tq], topc_t[:Tq, 1, :], ALU.add, sc.k() + W.tck, sc.k())
        k.op("dve", lambda: nc.vector.max(out=m8[:Tq], in_=sc[:Tq]), sc.k(), m8.k())
        k.op("dve", lambda: nc.vector.match_replace(out=sc2[:Tq], in_to_replace=m8[:Tq], in_values=sc[:Tq],
                                                     imm_value=-3.0e38), sc.k() + m8.k(), sc2.k())
        k.op("dve", lambda: nc.vector.max(out=m8b[:Tq], in_=sc2[:Tq]), sc2.k(), m8b.k())
        k.ts(nsl[:Tq], sc[:Tq], m8b[:Tq, 7:8], None, ALU.is_ge, None, sc.k() + m8b.k(), nsl.k())
        k.tt(nsl[:Tq], nsl[:Tq], topc_t[:Tq, 2, :], ALU.mult, nsl.k() + W.tck, nsl.k())
        k.ts(nsl[:Tq], nsl[:Tq], -1.0, 30000.0, ALU.add, ALU.mult, nsl.k(), nsl.k())
        if getattr(W, "defer_tr", False):
            W.pending_tr = (nsl, Tq, nsT_dst, nsT_keys)
            return
        k.tr(pb[7][0:64, 0:Tq], nsl[:Tq, :], ident[:Tq, :Tq], nsl.k() + cst.k(), pb[7].k())
        k.cp(nsT_dst, pb[7][0:64, 0:Tq], pb[7].k(), nsT_keys)

    def topk_finish(W):
        nsl, Tq, nsT_dst, nsT_keys = W.pending_tr
        W.pending_tr = None
        k.tr(pb[7][0:64, 0:Tq], nsl[:Tq, :], ident[:Tq, :Tq], nsl.k() + cst.k(), pb[7].k())
        k.cp(nsT_dst, pb[7][0:64, 0:Tq], pb[7].k(), nsT_keys)

    def combine(W, Tq, ob, gcol, oacc_g, firstbr, okeys):
        att_flush(W.A)
        rd = W.rden
        o3 = ob[:Tq, 0:260].rearrange("p (r w) -> p r w", w=65)
        k.ts(rd[:Tq], o3[:, :, 64], 1e-30, None, ALU.max, None, ob.k(), W.rdk)
        k.recip(rd[:Tq], rd[:Tq], W.rdk, W.rdk)
        k.tt(W.fac[:Tq], rd[:Tq], gcol, ALU.mult, W.rdk + W.gk, W.fac.k())
        if firstbr:
            k.tt(oacc_g, o3[:, :, 0:64], bc3(W.fac[:Tq], 64), ALU.mult, ob.k() + W.fac.k(), okeys)
        else:
            k.tt(W.otmp[:Tq], o3[:, :, 0:64], bc3(W.fac[:Tq], 64), ALU.mult, ob.k() + W.fac.k(), W.otmp.k())
            k.tt(oacc_g, oacc_g, W.otmp[:Tq], ALU.add, okeys + W.otmp.k(), okeys)

    def work_tiles():
        W = Ctx()
        W.imp = k.sbuf("imp", [128, 64], F32)
        W.sc = k.sbuf("sc", [128, 64], F32)
        W.sc2 = k.sbuf("sc2", [128, 64], F32)
        W.m8 = k.sbuf("m8", [128, 8], F32)
        W.m8b = k.sbuf("m8b", [128, 8], F32)
        W.nsls = [k.sbuf("nsl%d" % i, [128, 64], F32) for i in range(2)]
        W.nsl = W.nsls[0]
        W.rden_t = k.sbuf("rden", [128, 4], F32)
        W.rden = W.rden_t
        W.rdk = W.rden_t.k()
        W.fac = k.sbuf("fac", [128, 4], F32)
        W.otmp = k.sbuf("otmp", [128, 4, 64], F32)
        W.gts = [k.sbuf("gt%d" % i, [128, 24], F32) for i in range(2)]
        W.tcs = [k.sbuf("tc%d" % i, [128, 3, 64], F32) for i in range(2)]
        W.qps = [k.sbuf("qp%d" % i, [4, 2, 4, 128], BF16) for i in range(2)]
        W.oaccs = [k.sbuf("oacc%d" % i, [128, 2, 4, 64], F32) for i in range(2)]

        def sel(i):
            W.gt, W.tc, W.qp, W.oacc = W.gts[i], W.tcs[i], W.qps[i], W.oaccs[i]
            W.gk = W.gt.k()
            W.tck = W.tc.k()
        W.sel = sel
        sel(0)
        return W

    def nsa_prompt_stage():
        NQ = TP // 128
        nh = TP // 16
        with k.scope():
            A = nsa_common()
            W = work_tiles()
            W.A = A
            FTv = FT.ap().rearrange("(c p) t -> p c t", p=128)
            QA = [k.sbuf("QA%d" % g, [128, 4, TP], BF16) for g in range(2)]
            KA = [[k.sbuf("KA%d%d" % (kd, g), [128, TP], BF16) for g in range(2)] for kd in range(2)]
            KcA = [k.sbuf("KcA%d" % g, [128, 256], BF16) for g in range(2)]
            for g in range(2):
                k.memset(QA[g][64:128], 0.0, QA[g].k())
                for r in range(4):
                    h_ = 4 * g + r
                    row0 = (h_ % 4) * 128 + (h_ // 4) * 64
                    k.dma("sp", QA[g][0:64, r, :], FT[row0:row0 + 64, 0:TP], FT.k(), QA[g].k())
                k.dma("pool", QA[g][64:68], c_qpos[:, g, :, 0:TP], [], QA[g].k())
                for kd in range(2):
                    k.memset(KA[kd][g][64:128], 0.0, KA[kd][g].k())
                    row0 = (6 + kd) * 128 + 64 * g
                    k.dma("sp", KA[kd][g][0:64, :], FT[row0:row0 + 64, 0:TP], FT.k(), KA[kd][g].k())
                    k.dma("pool", KA[kd][g][64:68, :], c_kpos[:, 0:TP], [], KA[kd][g].k())
                k.memset(KcA[g][64:128], 0.0, KcA[g].k())
                k.dma("pool", KcA[g][64:68, :], c_kposc.ap(), [], KcA[g].k())
            cmk = k.sbuf("cmk", [128, 2, TP], BF16)
            k.dma("pool", cmk[:], c_cmpmask.ap().rearrange("(j p) t -> p j t", p=128)[:, :, 0:TP], [], cmk.k())
            gx = k.sbuf("gx", [128, TP], BF16)
            k.memset(gx[64:128], 0.0, gx.k())
            k.dma("pool", gx[0:64], c_gexp[:, 0:TP], [], gx.k())
            nsT = k.sbuf("nsT", [128, 2, TP], BF16)
            k.memset(nsT[64:128], 0.0, nsT.k())
            KcT = k.sbuf("KcT", [128, 256], BF16)
            Vc = k.sbuf("Vc", [128, 2, 2, 65], BF16)
            k.memset(Vc[:, :, :, 64:65], 1.0, Vc.k())
            with k.scope():
                KX = k.sbuf("KX", [128, 2, TP], BF16)
                k.dma("sp", KX[:], FTv[:, 4:6, 0:TP], FT.k(), KX.k())
                A.kc_keys = KcT.k()
                A.kc_split = KcA
                compress(A, KX[:, 0, :], KX[:, 1, :], nh, KcT[:], Vc, KX.k())
            Vs = k.sbuf("Vs", [128, NQ, 2, 65], BF16)
            Vw = k.sbuf("Vw", [128, NQ, 2, 65], BF16)
            k.memset(Vs[:, :, :, 64:65], 1.0, Vs.k())
            k.memset(Vw[:, :, :, 64:65], 1.0, Vw.k())
            with k.scope():
                kvt = k.sbuf("kvt", [128, 768], F32)
                for j in range(NQ):
                    k.dma("sp", kvt[:], KVtok[j * 128:(j + 1) * 128, :], KVtok.k(j), kvt.k())
                    k.cp(Vs[:, j, :, 0:64], kvt[:, 384:512].rearrange("p (g d) -> p g d", g=2), kvt.k(), Vs.k())
                    k.cp(Vw[:, j, :, 0:64], kvt[:, 640:768].rearrange("p (g d) -> p g d", g=2), kvt.k(), Vw.k(),
                         eng="act")

            def ctx(qi, g):
                W.sel(qi % 2)
                t0 = qi * 128
                return (t0, QA[g][:, :, t0:t0 + 128], QA[g].k(), W.oacc[:, g],
                        W.gt[:].rearrange("p (h b) -> p h b", b=3))

            def do_cmp(qi, g):
                W.sel(qi % 2)
                t0 = qi * 128
                if g == 0:
                    k.dma("sp", W.gt[:], GTok[t0:t0 + 128, :], GTok.k(qi), W.gt.k())
                    k.dma("sp", W.tc[:], c_topc[t0:t0 + 128], [], W.tc.k())
                t0, Qg, qk, oacc_g, g3 = ctx(qi, g)
                ntile = 2 if (16 * 128 + 31) <= t0 + 127 else 1
                for j in range(ntile):
                    terms = [(KcA[g][:, j * 128:(j + 1) * 128], Qg, KcA[g].k() + qk),
                             (A.idb[:], bcm(cmk[:, j, t0:t0 + 128], 4), A.idb.k() + cmk.k())]
                    att_tile(A, 128, 128, terms,
                             [(pb[2], 65, Vc[:, j, g, :], Vc.k()), (pb[3], 64, A.mcs[:, j, :], A.mcs.k())],
                             j == 0, j == ntile - 1)
                combine(W, 128, pb[2], g3[:, 4 * g:4 * g + 4, 0], oacc_g, True, W.oacc.k())
                topk_negsel(A, W, 128, pb[3], W.rden, W.tc, nsT[0:64, g, t0:t0 + 128], nsT.k())

            def do_rest(qi, g):
                t0, Qg, qk, oacc_g, g3 = ctx(qi, g)
                for j in range(qi + 1):
                    terms = [(KA[0][g][:, j * 128:(j + 1) * 128], Qg, KA[0][g].k() + qk),
                             (gx[:, j * 128:(j + 1) * 128], bcm(nsT[:, g, t0:t0 + 128], 4), gx.k() + nsT.k())]
                    if j == qi:
                        terms.append((A.idb[:], bcm(A.am[:, 0, :], 4), A.idb.k() + A.am.k()))
                    att_tile(A, 128, 128, terms, [(pb[4], 65, Vs[:, j, g, :], Vs.k())], j == 0, False)
                j0 = max(0, qi - 4)
                for j in range(j0, qi + 1):
                    terms = [(KA[1][g][:, j * 128:(j + 1) * 128], Qg, KA[1][g].k() + qk)]
                    if j == qi - 4:
                        terms.append((A.idb[:], bcm(A.am[:, 1, :], 4), A.idb.k() + A.am.k()))
                    if j == qi:
                        terms.append((A.idb[:], bcm(A.am[:, 0, :], 4), A.idb.k() + A.am.k()))
                    att_tile(A, 128, 128, terms, [(pb[5], 65, Vw[:, j, g, :], Vw.k())], j == j0, False)
                combine(W, 128, pb[4], g3[:, 4 * g:4 * g + 4, 1], oacc_g, False, W.oacc.k())
                combine(W, 128, pb[5], g3[:, 4 * g:4 * g + 4, 2], oacc_g, False, W.oacc.k())
                if g == 1:
                    k.dma("sp", OTok[t0:t0 + 128, 0:512], W.oacc[:].rearrange("p g r d -> p (g r d)"), W.oacc.k(),
                          OTok.k(2 * qi))

            items = [(qi, g) for qi in range(NQ) for g in range(2)]
            W.defer_tr = True
            W.nsl = W.nsls[0]
            do_cmp(*items[0])
            topk_finish(W)
            for n, it in enumerate(items):
                if n + 1 < len(items):
                    W.nsl = W.nsls[(n + 1) % 2]
                    do_cmp(*items[n + 1])
                do_rest(*it)
                if n + 1 < len(items):
                    topk_finish(W)

    def nsa_sample_stage():
        NPG = 16
        PAST = NPG * 128
        with k.scope():
            A = nsa_common(pad_w1=True)
            W = work_tiles()
            W.A = A
            FTv = FT.ap().rearrange("(c p) t -> p c t", p=128)
            QTs = k.sbuf("QTs", [128, 4, TS], BF16)
            k.dma("sp", QTs[:], FTv[:, 0:4, TP:Ttot], FT.k(), QTs.k())
            k.dma("pool", W.qp[:, :, :, 0:8], c_qpos[:, :, :, PAST:PAST + 8], [], W.qp.k())
            k.dma("sp", W.tc[0:8], c_topc[PAST:PAST + 8], [], W.tc.k())
            cmks = k.sbuf("cmks", [128, 8], BF16)
            k.dma("pool", cmks[:], c_cmpmask[0:128, PAST:PAST + 8], [], cmks.k())
            gxs = k.sbuf("gxs", [64, PAST + 128], BF16)
            k.dma("pool", gxs[:], c_gexp[:, 0:PAST + 128], [], gxs.k())
            pti = k.sbuf("pti", [128, NS * NPG], I32)
            ptf = k.sbuf("ptf", [128, NS * NPG], F32)
            idx = k.sbuf("idx", [128, NS * NPG], I32)
            k.dma("sp", pti[:], page_tab.ap().rearrange("(o s) g -> o (s g)", o=1).partition_broadcast(128)
                  .rearrange("p a b -> p (a b)"), [], pti.k())
            k.cp(ptf[:], pti[:], pti.k(), ptf.k())
            k.ts(ptf[:], ptf[:], 128.0, cst[:, C_IOTA, 0:1], ALU.mult, ALU.add, ptf.k() + cst.k(), ptf.k())
            k.cp(idx[:], ptf[:], ptf.k(), idx.k())
            pgc = k.sbuf("pgc", [128, NPG, 256], F32)
            pgs = k.sbuf("pgs", [128, NPG, 256], F32)
            wst = k.sbuf("wsts", [128, 4, 256], F32)
            nw = k.sbuf("nw", [8, 768], F32)
            XTk = k.sbuf("XTk", [128, PAST], BF16)
            XTv = k.sbuf("XTv", [128, PAST], BF16)
            KsT = k.sbuf("KsT", [128, PAST + 8], BF16)
            KwT = k.sbuf("KwT", [128, 520], BF16)
            Vs = k.sbuf("Vss", [128, NPG + 1, 2, 65], BF16)
            Vw = k.sbuf("Vws", [128, 5, 2, 65], BF16)
            KcT = k.sbuf("KcTs", [128, 256], BF16)
            Vc = k.sbuf("Vcs", [128, 2, 2, 65], BF16)
            nsT = k.sbuf("nsTs", [64, 8], BF16)
            k.memset(Vs[:, :, :, 64:65], 1.0, Vs.k())
            k.memset(Vw[:, :, :, 64:65], 1.0, Vw.k())
            k.memset(Vc[:, :, :, 64:65], 1.0, Vc.k())
            A.kc_keys = KcT.k()
            W.sel(0)
            A.kc_split = None
            QPk = W.qp.k()

            def transposes(src_fn, n, dst, dcol0, skeys, width=128):
                for i0 in range(0, n, 4):
                    nn = min(4, n - i0)
                    bank = pb[6 + (i0 // 4) % 2]
                    for ii in range(nn):
                        k.tr(bank[:, ii * 128:ii * 128 + width], src_fn(i0 + ii), ident[:width, :width],
                             skeys + cst.k(), bank.k())
                    k.cp(dst[:, dcol0 + i0 * 128:dcol0 + i0 * 128 + (nn - 1) * 128 + width],
                         bank[:, 0:(nn - 1) * 128 + width], bank.k(), dst.k(), eng=("act" if (i0 // 4) % 2 else "dve"))

            for sq_ in range(NS):
                r0 = TP + 8 * sq_
                for pg in range(NPG):
                    col = sq_ * NPG + pg
                    k.op("pool", (lambda pg=pg, col=col: nc.gpsimd.indirect_dma_start(
                        out=pgc[:, pg, :], out_offset=None, in_=cache_cmp.ap(),
                        in_offset=bass.IndirectOffsetOnAxis(ap=idx[:, col:col + 1], axis=0))),
                        idx.k(), pgc.k(), dma=True)
                    k.op("pool", (lambda pg=pg, col=col: nc.gpsimd.indirect_dma_start(
                        out=pgs[:, pg, :], out_offset=None, in_=cache_sel.ap(),
                        in_offset=bass.IndirectOffsetOnAxis(ap=idx[:, col:col + 1], axis=0))),
                        idx.k(), pgs.k(), dma=True)
                k.dma("sp", wst[:], cache_win[sq_].rearrange("(j p) c -> p j c", p=128), [], wst.k())
                k.dma("sp", nw[:], KVtok[r0:r0 + 8, :], KVtok.k(Ttot // 128 - 1), nw.k())
                k.dma("sp", W.gt[0:8], GTok[r0:r0 + 8, :], GTok.k(Ttot // 128 - 1), W.gt.k())
                transposes(lambda i: pgc[:, i, 0:128], NPG, XTk, 0, pgc.k())
                transposes(lambda i: pgc[:, i, 128:256], NPG, XTv, 0, pgc.k())
                transposes(lambda i: pgs[:, i, 0:128], NPG, KsT, 0, pgs.k())
                transposes(lambda i: wst[:, i, 0:128], 4, KwT, 0, wst.k())
                transposes(lambda i: nw[:, 256:384], 1, KsT, PAST, nw.k(), width=8)
                transposes(lambda i: nw[:, 512:640], 1, KwT, 512, nw.k(), width=8)
                k.cp(Vs[:, 0:NPG, :, 0:64], pgs[:, :, 128:256].rearrange("p j (g d) -> p j g d", g=2), pgs.k(), Vs.k())
                k.cp(Vs[0:8, NPG, :, 0:64], nw[:, 384:512].rearrange("p (g d) -> p g d", g=2), nw.k(), Vs.k())
                k.cp(Vw[:, 0:4, :, 0:64], wst[:, :, 128:256].rearrange("p j (g d) -> p j g d", g=2), wst.k(), Vw.k())
                k.cp(Vw[0:8, 4, :, 0:64], nw[:, 640:768].rearrange("p (g d) -> p g d", g=2), nw.k(), Vw.k())
                compress(A, XTk[:], XTv[:], PAST // 16, KcT[:], Vc, XTk.k() + XTv.k())
                g3 = W.gt[0:8].rearrange("p (h b) -> p h b", b=3)
                for g in range(2):
                    ps = slice(64 * g, 64 * g + 64)
                    Qg = QTs[ps, :, 8 * sq_:8 * sq_ + 8]
                    QPg = W.qp[:, g, :, 0:8]
                    qk = QTs.k() + QPk
                    oacc_g = W.oacc[0:8, g]
                    terms = [(KcT[ps, 0:128], Qg, KcT.k() + qk),
                             (A.kposc[:, 0:128], QPg, A.kposc.k() + qk),
                             (A.idb[:], bcm(cmks[:], 4), A.idb.k() + cmks.k())]
                    att_tile(A, 128, 8, terms, [(pb[2], 65, Vc[:, 0, g, :], Vc.k()),
                                                (pb[3], 64, A.mcs[:, 0, :], A.mcs.k())], True, True)
                    combine(W, 8, pb[2], g3[:, 4 * g:4 * g + 4, 0], oacc_g, True, W.oacc.k())
                    topk_negsel(A, W, 8, pb[3], W.rden, W.tc, nsT[:], nsT.k())
                    for j in range(NPG + 1):
                        nk = 128 if j < NPG else 8
                        c0 = j * 128
                        terms = [(KsT[ps, c0:c0 + nk], Qg, KsT.k() + qk),
                                 (A.kpos[:, c0:c0 + nk], QPg, A.kpos.k() + qk),
                                 (gxs[:, c0:c0 + nk], bcm(nsT[:], 4), gxs.k() + nsT.k())]
                        if j == NPG:
                            terms.append((A.idb[0:8, 0:8], bcm(A.am[0:8, 0, 0:8], 4), A.idb.k() + A.am.k()))
                        att_tile(A, nk, 8, terms, [(pb[4], 65, Vs[:nk, j, g, :], Vs.k())], j == 0, j == NPG)
                    combine(W, 8, pb[4], g3[:, 4 * g:4 * g + 4, 1], oacc_g, False, W.oacc.k())
                    for j in range(5):
                        nk = 128 if j < 4 else 8
                        c0 = j * 128
                        p0 = PAST - 512 + c0
                        terms = [(KwT[ps, c0:c0 + nk], Qg, KwT.k() + qk),
                                 (A.kpos[:, p0:p0 + nk], QPg, A.kpos.k() + qk)]
                        if j == 0:
                            terms.append((A.idb[:], bcm(A.am[:, 1, 0:8], 4), A.idb.k() + A.am.k()))
                        if j == 4:
                            terms.append((A.idb[0:8, 0:8], bcm(A.am[0:8, 0, 0:8], 4), A.idb.k() + A.am.k()))
                        att_tile(A, nk, 8, terms, [(pb[5], 65, Vw[:nk, j, g, :], Vw.k())], j == 0, j == 4)
                    combine(W, 8, pb[5], g3[:, 4 * g:4 * g + 4, 2], oacc_g, False, W.oacc.k())
                k.dma("sp", OTok[r0:r0 + 8, 0:512], W.oacc[0:8].rearrange("p g r d -> p (g r d)"), W.oacc.k(),
                      OTok.k(2 * (Ttot // 128 - 1)))

    finals = []
    if "s1" in stages or "all" in stages:
        ffn_stage(0, 0, 1, load_x_from_input, store_xT)
        finals += xT_d.k()
    if "s2" in stages or "all" in stages:
        win0_stage()
        finals += PT0.k() + ba0.k()
    if "s3" in stages or "all" in stages:
        mix0_stage()
        finals += OT.k() + delta_p.k() + delta_s.k() + conv_a_p.k() + conv_a_s.k() + conv_b_p.k() + conv_b_s.k()
    if "s5" in stages or "all" in stages:
        wout_stage(ev_w_out, 3)
        ffn_stage(1, 4, 5, load_xT, store_xT)
        ffn_stage(2, 6, 7, load_xT, store_xT)
        finals += xT_d.k()
    if "s7" in stages or "all" in stages:
        win1_stage()
        finals += (FT.k() + UTok.k() + KVtok.k() + GTok.k() + VTok.k() + cmp_p.k() + sel_p.k() + win_p.k()
                   + cmp_s.k() + sel_s.k() + win_s.k() + dv_s.k())
    if "s8p" in stages or "all" in stages:
        nsa_prompt_stage()
        finals += OTok.k()
    if "s8s" in stages or "all" in stages:
        nsa_sample_stage()
        finals += OTok.k()
    if "s8d" in stages or "all" in stages:
        cmlp_stage()
        finals += OTok.k()
    if "s9" in stages or "all" in stages:
        wout_stage(od_w_out, 9, o_tok=True)
        ffn_stage(3, 10, 11, load_xT, store_y_out)
        finals += y_all.k()
    k.finish(finals)
    return k


_CACHE = {}


def kernel(**inputs):
    f32 = lambda a: np.ascontiguousarray(np.asarray(a), dtype=np.float32)
    xp = f32(inputs["x_prompt"])
    xs = f32(inputs["x_sample"])
    B, TP, _ = xp.shape
    NSEQ = xs.shape[0]
    ncore = 8
    NS = NSEQ // ncore
    cc = f32(inputs["cache_cmp_kv"])
    NPOOL = cc.shape[0]
    if "k" not in _CACHE:
        _CACHE["k"] = build(TP=TP, NS=NS, NPOOL=NPOOL)
    k = _CACHE["k"]
    shared = {
        "consts": make_consts(),
        "norm_g": f32(inputs["norm_g"]).reshape(12, D),
        "ffn_gate": f32(inputs["ffn_gate"]).reshape(4, D, DFF),
        "ffn_up": f32(inputs["ffn_up"]).reshape(4, D, DFF),
        "ffn_down": f32(inputs["ffn_down"]).reshape(4, DFF, D),
        "ev_w_in": f32(inputs["ev_w_in"])[0],
        "ev_w_out": f32(inputs["ev_w_out"])[0],
        "ev_a_conv": f32(inputs["ev_a_conv"])[0],
        "ev_a_log": f32(inputs["ev_a_log"]),
        "ev_dt_bias": f32(inputs["ev_dt_bias"]),
        "ev_a_norm": f32(inputs["ev_a_norm"]),
        "ev_b_conv": f32(inputs["ev_b_conv"])[0],
        "od_w_in": f32(inputs["od_w_in"])[0],
        "od_w_out": f32(inputs["od_w_out"])[0],
        "od_cmp_pe": f32(inputs["od_cmp_pe"])[0],
        "od_cmp_w1": f32(inputs["od_cmp_w1"])[0],
        "od_cmp_w2": f32(inputs["od_cmp_w2"])[0],
        "od_d_ws": f32(inputs["od_d_ws"])[0],
        "od_d_bs": f32(inputs["od_d_bs"])[0],
        "od_d_ln_g": f32(inputs["od_d_ln_g"]),
        "od_d_ln_b": f32(inputs["od_d_ln_b"]),
        "cache_cmp": cc.reshape(NPOOL * 128, 256),
        "cache_sel": f32(inputs["cache_sel_kv"]).reshape(NPOOL * 128, 256),
    }
    shared.update(make_consts2())
    sd = f32(inputs["state_delta"])
    sca = f32(inputs["state_conv_a"])
    scb = f32(inputs["state_conv_b"])
    cw = f32(inputs["cache_win_kv"])
    pt = np.ascontiguousarray(np.asarray(inputs["page_table"]), dtype=np.int32)
    in_maps = []
    for c in range(ncore):
        sl = slice(c * NS, (c + 1) * NS)
        m = dict(shared)
        m["x_all"] = np.concatenate([xp[c % B], xs[sl].reshape(NS * 8, D)], 0)
        m["st_delta"] = sd[sl, 0]
        m["st_conv_a"] = sca[sl, 0].reshape(NS * 3, 1536)
        m["st_conv_b"] = scb[sl, 0].reshape(NS * 2, 512)
        m["cache_win"] = cw[sl, 0].reshape(NS, 512, 256)
        m["page_tab"] = pt[sl]
        in_maps.append(m)
    res = run_bass_kernel_spmd(k.nc, in_maps, core_ids=list(range(ncore)))
    R = res.results
    cat = lambda name, shp: np.concatenate([R[c][name].reshape(shp) for c in range(ncore)], 0)
    stk = lambda name, shp: np.stack([R[c][name].reshape(shp) for c in range(B)], 0)
    y_prompt = np.stack([R[c]["y_all"][:TP] for c in range(B)], 0)
    y_sample = np.concatenate([R[c]["y_all"][TP:].reshape(NS, 8, D) for c in range(ncore)], 0)
    return (y_prompt, y_sample,
            stk("delta_p", (1, 4, 128, 128)), cat("delta_s", (NS, 1, 4, 128, 128)),
            stk("conv_a_p", (1, 3, 1536)), cat("conv_a_s", (NS, 1, 3, 1536)),
            stk("conv_b_p", (1, 2, 512)), cat("conv_b_s", (NS, 1, 2, 512)),
            stk("cmp_p", (1, TP, 2, 2, 64)), cat("cmp_s", (NS, 1, 8, 2, 2, 64)),
            stk("sel_p", (1, TP, 2, 2, 64)), cat("sel_s", (NS, 1, 8, 2, 2, 64)),
            stk("win_p", (1, 512, 2, 2, 64)), cat("win_s", (NS, 1, 512, 2, 2, 64)),
            cat("dv_s", (NS, 1, 8, 512)))
```

```python
import numpy as np
import concourse.bass as bass
import concourse.mybir as mybir
from concourse.bass_utils import run_bass_kernel_spmd
from contextlib import ExitStack

F32 = mybir.dt.float32
BF16 = mybir.dt.bfloat16
I32 = mybir.dt.int32
AF = mybir.ActivationFunctionType
ALU = mybir.AluOpType
AX = mybir.AxisListType

COMPUTE = ("pe", "act", "dve", "pool")
NSLOT = 12
EPS = 1e-6
D = 1024
DFF = 2816
NM = DFF // 128


class Buf:
    def __init__(self, name, t, nsub=1):
        self.name = name
        self.t = t
        self.nsub = nsub

    def __getitem__(self, idx):
        return self.t[idx]

    def ap(self):
        return self.t.ap()

    def k(self, *subs):
        if not subs:
            return [(self.name, i) for i in range(self.nsub)]
        return [(self.name, s) for s in subs]


class Op:
    __slots__ = ("eng", "fn", "reads", "writes", "deps", "signal", "signo", "dma", "slot", "target", "idx")


class K:
    def __init__(self):
        self.nc = bass.Bass("TRN2", target_bir_lowering=False)
        self.es = ExitStack()
        self.ops = []
        self.eng = {"pe": self.nc.tensor, "act": self.nc.scalar, "dve": self.nc.vector,
                    "pool": self.nc.gpsimd, "sp": self.nc.sync}
        self.scopes = []

    def sbuf(self, name, shape, dtype, nsub=1):
        self.nbuf = getattr(self, "nbuf", 0) + 1
        name = "%s_%d" % (name, self.nbuf)
        t = self.es.enter_context(self.nc.sbuf_tensor(name, list(shape), dtype))
        return Buf(name, t, nsub)

    def capture(self, fn):
        saved = self.ops
        self.ops = []
        fn()
        out = self.ops
        self.ops = saved
        return out

    def barrier(self):
        o = Op()
        o.eng = None
        o.fn = None
        o.reads = []
        o.writes = []
        o.dma = False
        o.signal = False
        o.idx = len(self.ops)
        self.ops.append(o)
        return o

    def scope(self):
        k = self

        class _S:
            def __enter__(s):
                s.es = ExitStack()
                s.old = k.es
                k.es = s.es
                return s

            def __exit__(s, *a):
                k.es = s.old
                s.es.close()
                k.barrier()
        return _S()

    def psum(self, name, shape, dtype, nsub=1):
        t = self.es.enter_context(self.nc.psum_tensor(name, list(shape), dtype))
        return Buf(name, t, nsub)

    def dram(self, name, shape, dtype, kind="Internal", nsub=1):
        t = self.nc.dram_tensor(name, list(shape), dtype, kind=kind)
        return Buf(name, t, nsub)

    def op(self, eng, fn, reads=(), writes=(), dma=False):
        o = Op()
        o.eng = eng
        o.fn = fn
        o.reads = list(reads)
        o.writes = list(writes)
        o.dma = dma
        o.signal = False
        o.idx = len(self.ops)
        self.ops.append(o)
        return o

    def dma(self, q, out, in_, reads, writes, **kw):
        e = self.eng[q]
        return self.op(q, lambda: e.dma_start(out=out, in_=in_, **kw), reads, writes, dma=True)

    def mm(self, out, lhsT, rhs, start, stop, reads, writes, sgc=False):
        nc = self.nc
        if sgc:
            return self.op("pe", lambda: nc.tensor.matmul(out, lhsT=lhsT, rhs=rhs, start=start, stop=stop,
                                                          skip_group_check=True), reads, writes)
        return self.op("pe", lambda: nc.tensor.matmul(out, lhsT=lhsT, rhs=rhs, start=start, stop=stop),
                       reads, writes)

    def tr(self, out, in_, ident, reads, writes):
        nc = self.nc
        return self.op("pe", lambda: nc.tensor.transpose(out, in_, ident), reads, writes)

    def act(self, out, in_, func, reads, writes, **kw):
        nc = self.nc
        return self.op("act", lambda: nc.scalar.activation(out=out, in_=in_, func=func, **kw), reads, writes)

    def tt(self, out, in0, in1, op, reads, writes, eng="dve"):
        e = self.eng[eng]
        return self.op(eng, lambda: e.tensor_tensor(out=out, in0=in0, in1=in1, op=op), reads, writes)

    def ts(self, out, in0, s1, s2, op0, op1, reads, writes, eng="dve"):
        e = self.eng[eng]
        if op1 is None:
            return self.op(eng, lambda: e.tensor_scalar(out=out, in0=in0, scalar1=s1, scalar2=None, op0=op0),
                           reads, writes)
        return self.op(eng, lambda: e.tensor_scalar(out=out, in0=in0, scalar1=s1, scalar2=s2, op0=op0, op1=op1),
                       reads, writes)

    def stt(self, out, in0, scalar, in1, op0, op1, reads, writes):
        nc = self.nc
        return self.op("dve", lambda: nc.vector.scalar_tensor_tensor(out=out, in0=in0, scalar=scalar, in1=in1,
                                                                     op0=op0, op1=op1), reads, writes)

    def cp(self, out, in_, reads, writes, eng="dve"):
        if eng == "act":
            nc = self.nc
            return self.op("act", lambda: nc.scalar.copy(out=out, in_=in_), reads, writes)
        e = self.eng[eng]
        return self.op(eng, lambda: e.tensor_copy(out=out, in_=in_), reads, writes)

    def recip(self, out, in_, reads, writes):
        nc = self.nc
        return self.op("dve", lambda: nc.vector.reciprocal(out=out, in_=in_), reads, writes)

    def memset(self, out, val, writes, eng="dve"):
        e = self.eng[eng]
        return self.op(eng, lambda: e.memset(out, val), (), writes)

    def finish(self, final_wait_keys):
        nc = self.nc
        ops = self.ops
        lastw = {}
        lastr = {}

        def prune(lst, o):
            if o.dma:
                return lst + [o]
            return [x for x in lst if x.dma or x.eng != o.eng] + [o]

        for i_, o in enumerate(ops):
            o.idx = i_
        last_eng = {}
        dmas = []
        for o in ops:
            if o.eng is None:
                o.deps = list(last_eng.values()) + dmas
                for d in o.deps:
                    d.signal = True
                dmas = []
                continue
            if o.dma:
                dmas.append(o)
            else:
                last_eng[o.eng] = o
            deps = {}
            for kk in o.reads:
                for w in lastw.get(kk, ()):
                    deps[w.idx] = w
            for kk in o.writes:
                for w in lastw.get(kk, ()):
                    deps[w.idx] = w
                for r in lastr.get(kk, ()):
                    deps[r.idx] = r
            deps.pop(o.idx, None)
            dl = []
            for d in deps.values():
                if (not d.dma) and (not o.dma) and d.eng == "pe" and o.eng == "pe":
                    continue
                dl.append(d)
            o.deps = dl
            for d in dl:
                d.signal = True
            for kk in o.reads:
                lastr[kk] = prune(lastr.get(kk, []), o)
            for kk in o.writes:
                lastw[kk] = [o]
                lastr[kk] = []
        fin = []
        for kk in final_wait_keys:
            for w in lastw.get(kk, ()):
                w.signal = True
                fin.append(w)
        es = ExitStack()
        sem = {e: es.enter_context(nc.semaphore("s_" + e)) for e in COMPUTE}
        slots = {q: [es.enter_context(nc.semaphore("d_%s%d" % (q, i))) for i in range(NSLOT)]
                 for q in ("sp", "pool")}
        slot_val = {q: [0] * NSLOT for q in ("sp", "pool")}
        ndma = {"sp": 0, "pool": 0}
        cnt = {e: 0 for e in COMPUTE}
        known = {}

        def wait(engname, s, sname, val):
            kk = (engname, sname)
            if known.get(kk, 0) >= val:
                return
            known[kk] = val
            self.eng[engname].wait_ge(s, val)

        for o in ops:
            if o.eng is None:
                for e in ("pe", "act", "dve", "pool", "sp"):
                    for d in o.deps:
                        if d.dma:
                            wait(e, slots[d.eng][d.slot], ("d", d.eng, d.slot), d.target)
                        elif d.eng != e:
                            wait(e, sem[d.eng], ("c", d.eng), d.signo)
                continue
            for d in o.deps:
                if d.dma:
                    wait(o.eng, slots[d.eng][d.slot], ("d", d.eng, d.slot), d.target)
                else:
                    wait(o.eng, sem[d.eng], ("c", d.eng), d.signo)
            if o.dma:
                q = o.eng
                i = ndma[q] % NSLOT
                ndma[q] += 1
                if slot_val[q][i] > 0:
                    wait(q, slots[q][i], ("d", q, i), slot_val[q][i])
                slot_val[q][i] += 16
                o.slot = i
                o.target = slot_val[q][i]
                inst = o.fn()
                inst.then_inc(slots[q][i], 16)
            else:
                inst = o.fn()
                if o.signal:
                    cnt[o.eng] += 1
                    o.signo = cnt[o.eng]
                    inst.then_inc(sem[o.eng], 1)
        for w in fin:
            if w.dma:
                wait("sp", slots[w.eng][w.slot], ("d", w.eng, w.slot), w.target)
            else:
                wait("sp", sem[w.eng], ("c", w.eng), w.signo)
        self.stats = dict(nops=len(ops), cnt=cnt, ndma=ndma)
        es.close()
        self.es.close()
        return nc


def make_consts():
    i = np.arange(128)
    ident = np.eye(128, dtype=np.float32)
    ones = np.ones((128, 128), np.float32)
    U = (i[:, None] <= i[None, :]).astype(np.float32)
    LSN = -(i[None, :] < i[:, None]).astype(np.float32)
    blk = (i[:, None] // 8 == i[None, :] // 8).astype(np.float32)
    seqm = (i[:, None] // 8 == np.arange(16)[None, :]).astype(np.float32)
    seqm = np.concatenate([seqm, np.zeros((128, 112), np.float32)], 1)
    iota = np.broadcast_to(i[:, None].astype(np.float32), (128, 128))
    return np.stack([ident, ones, U, LSN, ones, U * blk, LSN * blk, blk, seqm, iota], 0)


def make_consts2():
    NP = 4224
    p = np.arange(NP)
    kpos = np.stack([np.ones(NP), np.ones(NP), 64.0 * (p // 64), (p % 64) * 1.0], 0).astype(np.float32)
    e = 16 * np.arange(256) + 31
    kposc = np.stack([np.ones(256), np.ones(256), 64.0 * (e // 64), (e % 64) * 1.0], 0).astype(np.float32)
    qpos = np.zeros((4, 2, 4, NP), np.float32)
    for g in range(2):
        for r in range(4):
            sl = 2.0 ** (-(4 * g + r + 1))
            qpos[0, g, r] = -sl * 64.0 * (p // 64)
            qpos[1, g, r] = -sl * (p % 64)
            qpos[2, g, r] = sl
            qpos[3, g, r] = sl
    i = np.arange(128)
    NEGM = -30000.0
    cm = np.where(i[:, None] > i[None, :], NEGM, 0.0)
    wm = np.where(i[:, None] <= i[None, :], NEGM, 0.0)
    amask = np.stack([cm, wm], 0).astype(np.float32)
    cmpmask = np.where(e[:, None] > p[None, :], NEGM, 0.0).astype(np.float32)
    cmpmask[255] = NEGM
    blk = np.arange(64)[None, :]
    cur = (p // 64)[:, None]
    valid = blk <= cur
    forced = (blk == 0) | (blk == cur) | (blk == cur - 1)
    mulc = (valid & ~forced).astype(np.float32)
    addc = np.where(valid, np.where(forced, 1e4, 0.0), -1e30).astype(np.float32)
    topc = np.stack([mulc, addc, valid.astype(np.float32)], 1)
    gexp = (p[None, :] // 64 == np.arange(64)[:, None]).astype(np.float32)
    cs = 16 * np.arange(256)[:, None]
    ss = 64 * np.arange(64)[None, :]
    ov = np.clip(np.minimum(cs + 32, ss + 64) - np.maximum(cs, ss), 0, None)
    mcs = (ov / 32.0).astype(np.float32)
    mcs[255] = 0.0
    return {"c_kpos": kpos, "c_kposc": kposc, "c_qpos": qpos, "c_amask": amask, "c_cmpmask": cmpmask,
            "c_topc": topc, "c_gexp": gexp, "c_mcs": mcs}


C_ID, C_ONES, C_U, C_LSN, C_BLK, C_UB, C_LSNB, C_BLKB, C_SEQM, C_IOTA = range(10)


class Ctx:
    pass


def build(TP=4096, NS=16, stages=("all",), debug_outs=(), debug_ins=(), NPOOL=2560):
    k = K()
    nc = k.nc
    C = Ctx()
    C.k = k
    TS = NS * 8
    assert TS == 128
    TT = 384
    Ttot = TP + TS
    assert Ttot % TT == 0
    NT = Ttot // TT
    C.TP, C.TS, C.TT, C.Ttot, C.NT = TP, TS, TT, Ttot, NT

    k.ext_in = {}

    def ein(name, shape, dt=F32):
        k.ext_in[name] = (tuple(shape), dt)
        return k.dram(name, shape, dt, kind="ExternalInput")

    def eout(name, shape, dt=F32):
        return k.dram(name, shape, dt, kind="ExternalOutput")

    def scratch(name, shape, dt=F32, nsub=1):
        kind = "ExternalOutput" if name in debug_outs else "Internal"
        if name in debug_ins:
            kind = "ExternalInput"
            k.ext_in[name] = (tuple(shape), dt)
        return k.dram(name, shape, dt, kind=kind, nsub=nsub)

    x_all = ein("x_all", [Ttot, D])
    y_all = eout("y_all", [Ttot, D])
    consts = ein("consts", [10, 128, 128])
    norm_g = ein("norm_g", [12, D])
    ffn_gate = ein("ffn_gate", [4, D, DFF])
    ffn_up = ein("ffn_up", [4, D, DFF])
    ffn_down = ein("ffn_down", [4, DFF, D])
    ev_w_in = ein("ev_w_in", [D, 3592])
    ev_w_out = ein("ev_w_out", [D, D])
    ev_a_conv = ein("ev_a_conv", [4, 1536])
    ev_a_log = ein("ev_a_log", [1, 4])
    ev_dt_bias = ein("ev_dt_bias", [1, 4])
    ev_a_norm = ein("ev_a_norm", [1, 128])
    ev_b_conv = ein("ev_b_conv", [3, 512])
    st_delta = ein("st_delta", [NS, 4, 128, 128])
    st_conv_a = ein("st_conv_a", [NS * 3, 1536])
    st_conv_b = ein("st_conv_b", [NS * 2, 512])
    od_w_in = ein("od_w_in", [D, 2328])
    od_w_out = ein("od_w_out", [D, D])
    od_cmp_pe = ein("od_cmp_pe", [2, 32, 64])
    od_cmp_w1 = ein("od_cmp_w1", [2, 32, 64, 128])
    od_cmp_w2 = ein("od_cmp_w2", [2, 128, 64])
    cache_cmp = ein("cache_cmp", [NPOOL * 128, 256])
    cache_sel = ein("cache_sel", [NPOOL * 128, 256])
    page_tab = ein("page_tab", [NS, 16], I32)
    c_kpos = ein("c_kpos", [4, 4224])
    c_kposc = ein("c_kposc", [4, 256])
    c_qpos = ein("c_qpos", [4, 2, 4, 4224])
    c_amask = ein("c_amask", [2, 128, 128])
    c_cmpmask = ein("c_cmpmask", [256, 4224])
    c_topc = ein("c_topc", [4224, 3, 64])
    c_gexp = ein("c_gexp", [64, 4224])
    c_mcs = ein("c_mcs", [256, 64])
    od_d_ws = ein("od_d_ws", [8, 128, 128])
    od_d_bs = ein("od_d_bs", [8, 128])
    od_d_ln_g = ein("od_d_ln_g", [1, 512])
    od_d_ln_b = ein("od_d_ln_b", [1, 512])
    cache_win = ein("cache_win", [NS, 512, 256])
    cmp_p = eout("cmp_p", [TP, 256])
    sel_p = eout("sel_p", [TP, 256])
    win_p = eout("win_p", [512, 256])
    cmp_s = eout("cmp_s", [TS, 256])
    sel_s = eout("sel_s", [TS, 256])
    win_s = k.dram("win_s", [NS, 512, 256], F32, kind="ExternalOutput", nsub=2)
    dv_s = eout("dv_s", [TS, 512])
    delta_p = eout("delta_p", [4, 128, 128])
    delta_s = eout("delta_s", [NS, 4, 128, 128])
    conv_a_p = eout("conv_a_p", [3, 1536])
    conv_a_s = eout("conv_a_s", [NS * 3, 1536])
    conv_b_p = eout("conv_b_p", [2, 512])
    conv_b_s = eout("conv_b_s", [NS * 2, 512])

    xT_d = scratch("xT_d", [D, Ttot], nsub=NT)
    PT0 = scratch("PT0", [29 * 128, Ttot], nsub=NT)
    ba0 = scratch("ba0", [Ttot, 8], nsub=NT)
    OT = scratch("OT", [D, Ttot], BF16, nsub=Ttot // 128)
    FT = scratch("FT", [8 * 128, Ttot], BF16, nsub=NT)
    UTok = scratch("UTok", [Ttot, 512], F32, nsub=Ttot // 128)
    OTok = scratch("OTok", [Ttot, D], F32, nsub=2 * (Ttot // 128))
    KVtok = scratch("KVtok", [Ttot, 768], F32, nsub=Ttot // 128)
    GTok = scratch("GTok", [Ttot, 24], F32, nsub=Ttot // 128)
    VTok = scratch("VTok", [Ttot, 512], F32, nsub=Ttot // 128)

    cst = k.sbuf("cst", [128, 10, 128], F32)
    k.dma("sp", cst[:], consts.ap().rearrange("n p f -> p n f"), consts.k(), cst.k())
    ones_bf = k.sbuf("ones_bf", [128, 128], BF16)
    k.cp(ones_bf[:], cst[:, C_ONES, :], cst.k(), ones_bf.k())
    ident = cst[:, C_ID, :]
    pb = [k.psum("pb%d" % i, [128, 512], F32) for i in range(8)]
    C.cst, C.ones_bf, C.ident, C.pb = cst, ones_bf, ident, pb

    gT = k.sbuf("gT", [128, 12, 8], F32)
    with k.scope():
        stg = k.sbuf("stg_g", [12, D], F32)
        k.dma("sp", stg[:], norm_g.ap(), norm_g.k(), stg.k())
        for c in range(8):
            k.tr(pb[7][:, c * 12:(c + 1) * 12], stg[:, c * 128:(c + 1) * 128], ident[:12, :12],
                 stg.k() + cst.k(), pb[7].k())
        k.cp(gT[:].rearrange("p r c -> p c r"), pb[7][:, 0:96].rearrange("p (c r) -> p c r", r=12),
             pb[7].k(), gT.k())
    gTh = k.sbuf("gTh", [128, 12, 8], F32)
    k.ts(gTh[:], gT[:], 0.5, None, ALU.mult, None, gT.k(), gTh.k())
    C.gT, C.gTh = gT, gTh

    def rms_rstd(src3, W, nchunk, rstd, sqb, scale, keys_r, bank):
        k.act(sqb[:, :nchunk, :W], src3, AF.Square, keys_r, sqb.k())
        for c in range(nchunk):
            k.mm(bank[:, :W], ones_bf[:], sqb[:, c, :W], c == 0, c == nchunk - 1,
                 ones_bf.k() + sqb.k(), bank.k())
        k.act(rstd[:, :W], bank[:, :W], AF.Sqrt, bank.k(), rstd.k(), scale=scale, bias=EPS)
        k.recip(rstd[:, :W], rstd[:, :W], rstd.k(), rstd.k())

    def load_x_from_input(xT, i):
        for j in range(TT // 128):
            t0 = i * TT + j * 128
            xtok = C.xtok
            k.dma("sp", xtok[:], x_all[t0:t0 + 128, :], x_all.k(), xtok.k())
            for hf in range(2):
                bank = pb[5 + hf]
                for c4 in range(4):
                    c = hf * 4 + c4
                    k.tr(bank[:, c4 * 128:(c4 + 1) * 128], xtok[:, c * 128:(c + 1) * 128], ident,
                         xtok.k() + cst.k(), bank.k())
                k.cp(xT[:, hf * 4:(hf + 1) * 4, j * 128:(j + 1) * 128],
                     bank[:].rearrange("p (c t) -> p c t", c=4), bank.k(), xT.k(),
                     eng=("act" if hf else "dve"))

    def load_xT(xT, i):
        k.dma("sp", xT[:], xT_d.ap().rearrange("(c p) t -> p c t", p=128)[:, :, i * TT:(i + 1) * TT],
              xT_d.k(i), xT.k())

    def store_xT(xT, i):
        k.dma("sp", xT_d.ap().rearrange("(c p) t -> p c t", p=128)[:, :, i * TT:(i + 1) * TT], xT[:],
              xT.k(), xT_d.k(i))

    def store_y_out(xT, i):
        ytok = C.ytok
        for j in range(TT // 128):
            t0 = i * TT + j * 128
            for hf in range(2):
                bank = pb[5 + hf]
                for c4 in range(4):
                    c = hf * 4 + c4
                    k.tr(bank[:, c4 * 128:(c4 + 1) * 128], xT[:, c, j * 128:(j + 1) * 128], ident,
                         xT.k() + cst.k(), bank.k())
                k.cp(ytok[:, hf * 512:(hf + 1) * 512], bank[:], bank.k(), ytok.k(), eng=("act" if hf else "dve"))
            k.dma("sp", y_all[t0:t0 + 128, :], ytok[:], ytok.k(), y_all.k())

    def ffn_stage(fidx, gpre, gpost, loader, storer, pre=None):
        with k.scope():
            GM = 6
            NGW = (NM + GM - 1) // GM
            wg = k.sbuf("wg", [128, 8, DFF], BF16, nsub=NGW)
            wu = k.sbuf("wu", [128, 8, DFF], BF16, nsub=NGW)
            wd = k.sbuf("wd", [128, NM, D], BF16)
            for gi in range(NGW):
                c0, c1 = gi * GM * 128, min(DFF, (gi + 1) * GM * 128)
                for c in range(8):
                    k.dma("pool", wg[:, c, c0:c1], ffn_gate[fidx, c * 128:(c + 1) * 128, c0:c1], ffn_gate.k(),
                          wg.k(gi))
                    k.dma("pool", wu[:, c, c0:c1], ffn_up[fidx, c * 128:(c + 1) * 128, c0:c1], ffn_up.k(),
                          wu.k(gi))
            for m in range(NM):
                k.dma("pool", wd[:, m, :], ffn_down[fidx, m * 128:(m + 1) * 128, :], ffn_down.k(), wd.k())
            xTs = [k.sbuf("xT", [128, 8, TT], F32) for _ in range(2)]
            hT = k.sbuf("hT", [128, 8, TT], BF16)
            aT = k.sbuf("aT", [128, NM, TT], BF16)
            sqb = aT
            yT = k.sbuf("yT", [128, 8, TT], F32)
            rstdP = k.sbuf("rstdP", [128, TT], F32)
            rstdE = k.sbuf("rstdE", [128, TT], F32)
            sg = [k.sbuf("sg%d" % i, [128, TT], F32) for i in range(2)]
            if loader is load_x_from_input:
                C.xtok = k.sbuf("xtok", [128, D], F32)
            if storer is store_y_out:
                C.ytok = k.sbuf("ytok", [128, D], F32)
            ysq = Buf(yT.name, yT.t, 1)
            ysq_ap = yT[:].rearrange("p c t -> p (c t)").bitcast(BF16)[:, 0:8 * TT].rearrange("p (c t) -> p c t", c=8)

            def pro_a(i):
                xT = xTs[i % 2]
                loader(xT, i)
                k.act(ysq_ap, xT[:], AF.Square, xT.k(), yT.k())
                for c in range(8):
                    k.mm(pb[7][:, :TT], ones_bf[:], ysq_ap[:, c, :], c == 0, c == 7, ones_bf.k() + yT.k(), pb[7].k())
                k.act(rstdP[:], pb[7][:, :TT], AF.Sqrt, pb[7].k(), rstdP.k(), scale=1.0 / D, bias=EPS)
                k.recip(rstdP[:], rstdP[:], rstdP.k(), rstdP.k())

            def pro_b(i):
                xT = xTs[i % 2]
                for c in range(8):
                    k.stt(hT[:, c, :], xT[:, c, :], gT[:, gpre, c:c + 1], rstdP[:], ALU.mult, ALU.mult,
                          xT.k() + gT.k() + rstdP.k(), hT.k())

            pro_a(0)
            pro_b(0)
            for i in range(NT):
                xT = xTs[i % 2]
                for m in range(NM):
                    bg = pb[m % 2]
                    bu = pb[2 + m % 2]
                    for c in range(8):
                        k.mm(bg[:, :TT], wg[:, c, m * 128:(m + 1) * 128], hT[:, c, :], c == 0, c == 7,
                             wg.k(m // GM) + hT.k(), bg.k())
                    for c in range(8):
                        k.mm(bu[:, :TT], wu[:, c, m * 128:(m + 1) * 128], hT[:, c, :], c == 0, c == 7,
                             wu.k(m // GM) + hT.k(), bu.k())
                    s_ = sg[m % 2]
                    k.act(s_[:], bg[:, :TT], AF.Silu, bg.k(), s_.k())
                    k.tt(aT[:, m, :], s_[:], bu[:, :TT], ALU.mult, s_.k() + bu.k(), aT.k())
                    if m == NM - 5 and i + 1 < NT:
                        pro_a(i + 1)
                if i + 1 < NT:
                    pro_b(i + 1)
                for c in range(8):
                    by = pb[4 + c % 2]
                    for m in range(NM):
                        k.mm(by[:, :TT], wd[:, m, c * 128:(c + 1) * 128], aT[:, m, :], m == 0, m == NM - 1,
                             wd.k() + aT.k(), by.k())
                    k.cp(yT[:, c, :], by[:, :TT], by.k(), yT.k(), eng=("act" if c % 2 else "dve"))
                rms_rstd(yT[:], TT, 8, rstdE, sqb, 1.0 / D, yT.k(), pb[7])
                for c in range(8):
                    k.stt(yT[:, c, :], yT[:, c, :], gTh[:, gpost, c:c + 1], rstdE[:], ALU.mult, ALU.mult,
                          yT.k() + gTh.k() + rstdE.k(), yT.k())
                k.tt(xT[:], xT[:], yT[:], ALU.add, xT.k() + yT.k(), xT.k())
                storer(xT, i)

    def wout_stage(w_out, grow, o_tok=False):
        with k.scope():
            if o_tok:
                otk = k.sbuf("otk", [128, D], F32)
            wo = k.sbuf("wo", [128, 8, D], BF16)
            for c in range(8):
                k.dma("pool", wo[:, c, :], w_out[c * 128:(c + 1) * 128, :], w_out.k(), wo.k())
            oT = k.sbuf("oT", [128, 8, TT], BF16)
            xT = k.sbuf("xT", [128, 8, TT], F32)
            yT = k.sbuf("yT", [128, 8, TT], F32)
            sqb = k.sbuf("sqb", [128, 8, TT], BF16)
            rstd = k.sbuf("rstd", [128, TT], F32)
            for i in range(NT):
                load_xT(xT, i)
                if o_tok:
                    for j in range(TT // 128):
                        t0 = i * TT + j * 128
                        k.dma("sp", otk[:], OTok[t0:t0 + 128, :], OTok.k(), otk.k())
                        for hf in range(2):
                            bank = pb[2 + hf]
                            for c4 in range(4):
                                c = hf * 4 + c4
                                k.tr(bank[:, c4 * 128:(c4 + 1) * 128], otk[:, c * 128:(c + 1) * 128], ident,
                                     otk.k() + cst.k(), bank.k())
                            k.cp(oT[:, hf * 4:(hf + 1) * 4, j * 128:(j + 1) * 128],
                                 bank[:].rearrange("p (c t) -> p c t", c=4), bank.k(), oT.k(),
                                 eng=("act" if hf else "dve"))
                else:
                    k.dma("sp", oT[:], OT.ap().rearrange("(c p) t -> p c t", p=128)[:, :, i * TT:(i + 1) * TT],
                          OT.k(), oT.k())
                for c in range(8):
                    by = pb[4 + c % 2]
                    for kc in range(8):
                        k.mm(by[:, :TT], wo[:, kc, c * 128:(c + 1) * 128], oT[:, kc, :], kc == 0, kc == 7,
                             wo.k() + oT.k(), by.k())
                    k.cp(yT[:, c, :], by[:, :TT], by.k(), yT.k(), eng=("act" if c % 2 else "dve"))
                rms_rstd(yT[:], TT, 8, rstd, sqb, 1.0 / D, yT.k(), pb[7])
                for c in range(8):
                    k.stt(yT[:, c, :], yT[:, c, :], gT[:, grow, c:c + 1], rstd[:], ALU.mult, ALU.mult,
                          yT.k() + gT.k() + rstd.k(), yT.k())
                k.tt(xT[:], xT[:], yT[:], ALU.add, xT.k() + yT.k(), xT.k())
                store_xT(xT, i)

    def win0_stage():
        with k.scope():
            wi = k.sbuf("wi0", [128, 8, 29 * 128], BF16, nsub=4)
            k.memset(wi[:, :, 16 * 128:17 * 128], 0.0, wi.k(2))
            for c in range(8):
                r = slice(c * 128, (c + 1) * 128)
                k.dma("pool", wi[:, c, 0:1024], ev_w_in[r, 0:1024], ev_w_in.k(), wi.k(0))
            for c in range(8):
                r = slice(c * 128, (c + 1) * 128)
                k.dma("pool", wi[:, c, 1024:2048], ev_w_in[r, 1024:2048], ev_w_in.k(), wi.k(1))
                k.dma("pool", wi[:, c, 2048:2056], ev_w_in[r, 2048:2056], ev_w_in.k(), wi.k(2))
            for c in range(8):
                r = slice(c * 128, (c + 1) * 128)
                k.dma("pool", wi[:, c, 17 * 128:29 * 128], ev_w_in[r, 2056:3592], ev_w_in.k(), wi.k(3))
            xT = k.sbuf("xT", [128, 8, TT], F32)
            hT = k.sbuf("hT", [128, 8, TT], BF16)
            sqb = k.sbuf("sqb", [128, 8, TT], BF16)
            rstd = k.sbuf("rstd", [128, TT], F32)
            st = [k.sbuf("pst%d" % i, [128, 4, TT], F32) for i in range(2)]
            bast = k.sbuf("bast", [128, 3, 8], F32)
            PTv = PT0.ap().rearrange("(c p) t -> p c t", p=128)
            for i in range(NT):
                load_xT(xT, i)
                rms_rstd(xT[:], TT, 8, rstd, sqb, 1.0 / D, xT.k(), pb[7])
                for c in range(8):
                    k.stt(hT[:, c, :], xT[:, c, :], gT[:, 2, c:c + 1], rstd[:], ALU.mult, ALU.mult,
                          xT.k() + gT.k() + rstd.k(), hT.k())
                ng = 0
                for j0 in range(0, 29, 4):
                    nj = min(4, 29 - j0)
                    s = st[ng % 2]
                    ng += 1
                    for jj in range(nj):
                        j = j0 + jj
                        bank = pb[j % 4]
                        wkey = wi.k(0 if j < 8 else (1 if j < 16 else (2 if j == 16 else 3)))
                        for c in range(8):
                            k.mm(bank[:, :TT], wi[:, c, j * 128:(j + 1) * 128], hT[:, c, :], c == 0, c == 7,
                                 wkey + hT.k(), bank.k())
                        k.cp(s[:, jj, :], bank[:, :TT], bank.k(), s.k(), eng=("act" if j % 2 else "dve"))
                    k.dma("sp", PTv[:, j0:j0 + nj, i * TT:(i + 1) * TT], s[:, :nj, :], s.k(), PT0.k(i))
                for j in range(TT // 128):
                    for c in range(8):
                        k.mm(pb[6][:, j * 8:(j + 1) * 8], hT[:, c, j * 128:(j + 1) * 128],
                             wi[:, c, 2048:2056], c == 0, c == 7, wi.k(2) + hT.k(), pb[6].k())
                k.cp(bast[:], pb[6][:, 0:24].rearrange("p (j e) -> p j e", e=8), pb[6].k(), bast.k())
                k.dma("sp", ba0.ap().rearrange("(n j p) e -> n p j e", p=128, j=TT // 128)[i], bast[:],
                      bast.k(), ba0.k(i))


    def bc3(ap2, n):
        return ap2.unsqueeze(2).to_broadcast([ap2.shape[0], ap2.shape[1], n])

    def bcm(ap2, n):
        return ap2.unsqueeze(1).to_broadcast([ap2.shape[0], n, ap2.shape[1]])

    def load_colsT(dst, src2d, R, nch, bank):
        with k.scope():
            stg = k.sbuf("stgT", [R, nch * 128], F32)
            k.dma("sp", stg[:], src2d, [], stg.k())
            for c0 in range(0, nch, 4):
                ncc = min(4, nch - c0)
                for cc in range(ncc):
                    k.tr(bank[:, cc * R:(cc + 1) * R], stg[:, (c0 + cc) * 128:(c0 + cc + 1) * 128],
                         ident[:R, :R], stg.k() + cst.k(), bank.k())
                k.cp(dst[:, c0:c0 + ncc, :], bank[:, 0:ncc * R].rearrange("p (c r) -> p c r", r=R),
                     bank.k(), dst.k())

    def store_rowsT(dst2d, src3, R, nch, dst_keys, src_keys):
        with k.scope():
            stg = k.sbuf("stgR", [R, nch * 128], F32)
            for c0 in range(0, nch, 4):
                ncc = min(4, nch - c0)
                bank = pb[(c0 // 4) % 2]
                for cc in range(ncc):
                    k.tr(bank[:R, cc * 128:(cc + 1) * 128], src3[:, c0 + cc, :], ident,
                         src_keys + cst.k(), bank.k())
                k.cp(stg[:, c0 * 128:(c0 + ncc) * 128], bank[:R, 0:ncc * 128], bank.k(), stg.k())
            k.dma("sp", dst2d, stg[:], stg.k(), dst_keys)

    def mix0_stage():
        NCH = Ttot // 128
        NPC = NCH - 1
        with k.scope():
            PTv = PT0.ap().rearrange("(c p) t -> p c t", p=128)
            OTv = OT.ap().rearrange("(c p) t -> p c t", p=128)
            ba = k.sbuf("ba", [128, NCH, 8], F32)
            k.dma("sp", ba[:], ba0.ap().rearrange("(n p) e -> p n e", p=128), ba0.k(), ba.k())
            dtb = k.sbuf("dtb", [128, 4], F32)
            nal = k.sbuf("nal", [128, 4], F32)
            k.dma("sp", dtb[:], ev_dt_bias.ap().partition_broadcast(128).rearrange("p a b -> p (a b)"),
                  [], dtb.k())
            k.dma("sp", nal[:], ev_a_log.ap().partition_broadcast(128).rearrange("p a b -> p (a b)"),
                  [], nal.k())
            k.act(nal[:], nal[:], AF.Exp, nal.k(), nal.k())
            k.ts(nal[:], nal[:], -1.0, None, ALU.mult, None, nal.k(), nal.k())
            beta = k.sbuf("beta", [128, NCH, 4], F32)
            nbeta = k.sbuf("nbeta", [128, NCH, 4], F32)
            g = k.sbuf("g", [128, NCH, 4], F32)
            k.act(beta[:], ba[:, :, 0:4], AF.Sigmoid, ba.k(), beta.k())
            k.ts(nbeta[:], beta[:], -1.0, None, ALU.mult, None, beta.k(), nbeta.k())
            k.tt(g[:], ba[:, :, 4:8], bcm(dtb[:], NCH), ALU.add, ba.k() + dtb.k(), g.k())
            k.act(g[:], g[:], AF.Exp, g.k(), g.k())
            k.act(g[:], g[:], AF.Ln, g.k(), g.k(), bias=1.0)
            k.tt(g[:], g[:], bcm(nal[:], NCH), ALU.mult, g.k() + nal.k(), g.k())
            gcc = k.sbuf("gcc", [128, NCH, 4], F32)
            glt = k.sbuf("glt", [128, NCH, 4], F32)
            gf = g[:].rearrange("p n h -> p (n h)")
            k.mm(pb[0][:, 0:NPC * 4], cst[:, C_U, :], gf[:, 0:NPC * 4], True, True, cst.k() + g.k(), pb[0].k())
            k.mm(pb[0][:, NPC * 4:NCH * 4], cst[:, C_UB, :], gf[:, NPC * 4:NCH * 4], True, True, cst.k() + g.k(), pb[0].k())
            k.cp(gcc[:], pb[0][:, 0:NCH * 4].rearrange("p (n h) -> p n h", h=4), pb[0].k(), gcc.k())
            k.mm(pb[1][:, 0:NPC * 4], cst[:, C_ONES, :], gf[:, 0:NPC * 4], True, True, cst.k() + g.k(), pb[1].k())
            k.mm(pb[1][:, NPC * 4:NCH * 4], cst[:, C_BLKB, :], gf[:, NPC * 4:NCH * 4], True, True, cst.k() + g.k(), pb[1].k())
            k.cp(glt[:], pb[1][:, 0:NCH * 4].rearrange("p (n h) -> p n h", h=4), pb[1].k(), glt.k())
            bge = k.sbuf("bge", [128, NCH, 4], F32)
            etl = k.sbuf("etl", [128, NCH, 4], F32)
            dl = k.sbuf("dl", [128, NCH, 4], F32)
            k.act(bge[:], gcc[:], AF.Exp, gcc.k(), bge.k())
            k.tt(bge[:], bge[:], beta[:], ALU.mult, bge.k() + beta.k(), bge.k())
            k.tt(etl[:], glt[:], gcc[:], ALU.subtract, glt.k() + gcc.k(), etl.k())
            k.act(etl[:], etl[:], AF.Exp, etl.k(), etl.k())
            k.act(dl[:], glt[:], AF.Exp, glt.k(), dl.k())
            gm = k.sbuf("gm", [128, 16, 4], F32)
            dls = k.sbuf("dls", [128, 16, 4], F32)
            k.tt(gm[:], bc3(cst[:, C_SEQM, 0:16], 4), bcm(g[:, NPC, :], 16), ALU.mult, cst.k() + g.k(), gm.k())
            k.mm(pb[2][:, 0:64], cst[:, C_ONES, :], gm[:].rearrange("p s h -> p (s h)"), True, True, cst.k() + gm.k(), pb[2].k())
            k.act(dls[:], pb[2][:, 0:64].rearrange("p (s h) -> p s h", h=4), AF.Exp, pb[2].k(), dls.k())
            wca = k.sbuf("wca", [128, 12, 4], F32)
            load_colsT(wca, ev_a_conv.ap(), 4, 12, pb[3])
            wcb = k.sbuf("wcb", [128, 4, 3], F32)
            load_colsT(wcb, ev_b_conv.ap(), 3, 4, pb[3])
            an = k.sbuf("an", [128, 1, 1], F32)
            load_colsT(an, ev_a_norm.ap(), 1, 1, pb[3])
            hista = k.sbuf("hista", [128, 12, 48], F32)
            load_colsT(hista, st_conv_a.ap(), 48, 12, pb[3])
            histb = k.sbuf("histb", [128, 4, 32], F32)
            load_colsT(histb, st_conv_b.ap(), 32, 4, pb[3])
            Sp = k.sbuf("Sp", [128, 1, 4, 128], F32)
            Ss = k.sbuf("Ss", [128, 16, 4, 128], F32)
            k.memset(Sp[:], 0.0, Sp.k())
            k.dma("sp", Ss[:], st_delta.ap().rearrange("s h k v -> k s h v"), [], Ss.k())
            xpb = k.sbuf("xpb", [128, 12, 131], BF16)
            xpsb = k.sbuf("xpsb", [128, 12, 16, 11], BF16)
            xtmp = k.sbuf("xtmp", [128, 12, 128], F32)
            xlast = k.sbuf("xlast", [128, 12, 3], F32)
            dw = k.sbuf("dw", [128, 12, 4, 128], BF16)
            for c in range(12):
                for i in range(4):
                    k.ts(dw[:, c, i, :], ident, wca[:, c, i:i + 1], None, ALU.mult, None, cst.k() + wca.k(), dw.k())
            qkvs = k.sbuf("qkvs", [128, 12, 128], F32)
            sq = k.sbuf("sq", [128, 8, 128], F32)
            rinv = k.sbuf("rinv", [128, 8, 128], F32)
            HS = []
            for hi_ in range(2):
                Hh = Ctx()
                for nm in ("qn", "kbg", "ktl", "vb", "TTm", "iT", "Ee"):
                    setattr(Hh, nm, k.sbuf(nm + "h", [128, 4, 128], F32))
                HS.append(Hh)
            kn = k.sbuf("kn", [128, 4, 128], F32)
            gb = k.sbuf("gb", [128, 4, 128], F32)
            md = k.sbuf("md", [128, 4, 128], F32)
            mx = k.sbuf("mx", [128, 4, 128], F32)
            Dm = k.sbuf("Dm", [128, 4, 128], F32)
            DTm = k.sbuf("DTm", [128, 4, 128], F32)
            nbm = k.sbuf("nbm", [128, 4, 128], F32)
            Mm = k.sbuf("Mm", [128, 4, 128], F32)
            Mb = k.sbuf("Mb", [128, 4, 128], BF16)
            MTb = k.sbuf("MTb", [128, 4, 128], BF16)
            TTb = k.sbuf("TTb", [128, 4, 128], BF16)
            MT = k.sbuf("MT", [128, 4, 128], F32)
            wTn = k.sbuf("wTn", [128, 4, 128], F32)
            vnT = k.sbuf("vnT", [128, 4, 128], F32)
            vnew = k.sbuf("vnew", [128, 4, 128], F32)
            qd = k.sbuf("qd", [128, 4, 128], F32)
            VM = k.sbuf("VM", [128, 16, 128], F32)
            osq = k.sbuf("osq", [128, 4, 128], F32)
            r2 = k.sbuf("r2", [128, 4, 128], F32)
            zT = k.sbuf("zT", [128, 4, 128], F32)
            o1 = k.sbuf("o1", [128, 4, 128], F32)
            ob = k.sbuf("ob", [128, 4, 128], BF16)
            hb = k.sbuf("hb", [128, 4, 130], F32)
            gcb = k.sbuf("gcb", [128, 4, 130], F32)
            gbb = k.sbuf("gbb", [128, 4, 128], F32)
            mp = k.sbuf("mp", [128, 4, 130], F32)
            ms = k.sbuf("ms", [128, 4, 16, 10], F32)
            yb = k.sbuf("yb", [128, 4, 128], F32)
            yb2 = k.sbuf("yb2", [128, 4, 128], F32)
            ob2 = k.sbuf("ob2", [128, 4, 128], BF16)
            catmp = k.sbuf("catmp", [128, 12, 48], F32)
            cbtmp = k.sbuf("cbtmp", [128, 4, 32], F32)
            U_ = {0: cst[:, C_U, :], 1: cst[:, C_UB, :]}
            LSN_ = {0: cst[:, C_LSN, :], 1: cst[:, C_LSNB, :]}
            b4 = lambda i: pb[i][:].rearrange("p (h t) -> p h t", h=4)

            def chunk_params(n):
                ss_ = 1 if n == NPC else 0
                return ss_, n * 128, ((16, 8) if ss_ else (1, 128)), (Ss if ss_ else Sp)

            def P1(n):
                ss_, t0, (nseq, L), S = chunk_params(n)
                Hn = HS[n % 2]
                qn, kbg, ktl, vb, TTm, iT, Ee = Hn.qn, Hn.kbg, Hn.ktl, Hn.vb, Hn.TTm, Hn.iT, Hn.Ee
                if not ss_:
                    if n == 0:
                        k.memset(xpb[:, :, 0:3], 0.0, xpb.k())
                        k.dma("pool", xpb[:, :, 3:131], PTv[:, 0:12, 0:128], PT0.k(), xpb.k())
                    else:
                        k.dma("pool", xpb[:], PTv[:, 0:12, t0 - 3:t0 + 128], PT0.k(), xpb.k())
                    xk = xpb.k()
                else:
                    k.dma("sp", xtmp[:], PTv[:, 0:12, t0:t0 + 128], PT0.k(), xtmp.k())
                    k.cp(xpsb[:, :, :, 0:3], hista[:].rearrange("p c (s j) -> p c s j", j=3), hista.k(), xpsb.k())
                    k.cp(xpsb[:, :, :, 3:11], xtmp[:].rearrange("p c (s j) -> p c s j", j=8), xtmp.k(), xpsb.k(),
                         eng="act")
                    xk = xpsb.k()
                for c in range(12):
                    bank = pb[c // 4]
                    for i in range(4):
                        if not ss_:
                            o_ = bank[:, (c % 4) * 128:(c % 4 + 1) * 128]
                            r_ = xpb[:, c, i:i + 128]
                        else:
                            o_ = bank[:, (c % 4) * 128:(c % 4 + 1) * 128].rearrange("p (s j) -> p s j", j=8)
                            r_ = xpsb[:, c, :, i:i + 8]
                        k.mm(o_, dw[:, c, i, :], r_, i == 0, i == 3, dw.k() + xk, bank.k())
                for q3 in range(3):
                    k.act(qkvs[:, q3 * 4:q3 * 4 + 4, :], b4(q3), AF.Silu, pb[q3].k(), qkvs.k())
                if n == NPC - 1:
                    k.dma("sp", xlast[:], PTv[:, 0:12, TP - 3:TP], PT0.k(), xlast.k())
                    store_rowsT(conv_a_p.ap(), xlast[:], 3, 12, conv_a_p.k(), xlast.k())
                if ss_:
                    k.cp(catmp[:].rearrange("p c (s j) -> p c s j", j=3),
                         xtmp[:].rearrange("p c (s j) -> p c s j", j=8)[:, :, :, 5:8], xtmp.k(), catmp.k())
                    store_rowsT(conv_a_s.ap(), catmp[:], 48, 12, conv_a_s.k(), catmp.k())
                k.act(sq[:], qkvs[:, 0:8, :], AF.Square, qkvs.k(), sq.k())
                for c in range(8):
                    bank = pb[c // 4]
                    k.mm(bank[:, (c % 4) * 128:(c % 4 + 1) * 128], cst[:, C_ONES, :], sq[:, c, :], True, True,
                         cst.k() + sq.k(), bank.k())
                for hf in range(2):
                    k.act(rinv[:, hf * 4:hf * 4 + 4, :], b4(hf), AF.Sqrt, pb[hf].k(), rinv.k(), bias=EPS)
                k.recip(rinv[:], rinv[:], rinv.k(), rinv.k())
                k.stt(qn[:], qkvs[:, 0:4, :], 128.0 ** -0.5, rinv[:, 0:4, :], ALU.mult, ALU.mult,
                      qkvs.k() + rinv.k(), qn.k())
                k.tt(kn[:], qkvs[:, 4:8, :], rinv[:, 4:8, :], ALU.mult, qkvs.k() + rinv.k(), kn.k())
                for h in range(4):
                    k.tr(pb[2][:, h * 128:(h + 1) * 128], kn[:, h, :], ident, kn.k() + cst.k(), pb[2].k())
                    k.tr(pb[3][:, h * 128:(h + 1) * 128], qkvs[:, 8 + h, :], ident, qkvs.k() + cst.k(), pb[3].k())
                k.tt(kbg[:], b4(2), bc3(bge[:, n, :], 128), ALU.mult, pb[2].k() + bge.k(), kbg.k())
                k.tt(ktl[:], b4(2), bc3(etl[:, n, :], 128), ALU.mult, pb[2].k() + etl.k(), ktl.k())
                k.tt(vb[:], b4(3), bc3(beta[:, n, :], 128), ALU.mult, pb[3].k() + beta.k(), vb.k())
                k.tt(gb[:], bcm(cst[:, C_ONES, :], 4), bc3(g[:, n, :], 128), ALU.mult, cst.k() + g.k(), gb.k())
                for h in range(4):
                    k.mm(pb[4][:, h * 128:(h + 1) * 128], gb[:, h, :], U_[ss_], True, True,
                         gb.k() + cst.k(), pb[4].k())
                k.tt(md[:], b4(4), bc3(gcc[:, n, :], 128), ALU.subtract, pb[4].k() + gcc.k(), md.k())
                k.ts(mx[:], md[:], 0.0, None, ALU.max, None, md.k(), mx.k())
                k.act(Dm[:], mx[:], AF.Exp, mx.k(), Dm.k(), scale=-1.0)
                k.ts(mx[:], md[:], 0.0, None, ALU.min, None, md.k(), mx.k())
                k.act(DTm[:], mx[:], AF.Exp, mx.k(), DTm.k())
                k.act(Ee[:], b4(4), AF.Exp, pb[4].k(), Ee.k())
                for h in range(4):
                    k.mm(pb[0][:, h * 128:(h + 1) * 128], kn[:, h, :], kn[:, h, :], True, True, kn.k(), pb[0].k())
                    k.mm(pb[1][:, h * 128:(h + 1) * 128], kn[:, h, :], qn[:, h, :], True, True,
                         kn.k() + qn.k(), pb[1].k())
                k.tt(nbm[:], bcm(LSN_[ss_], 4), bc3(beta[:, n, :], 128), ALU.mult, cst.k() + beta.k(), nbm.k())
                k.tt(Mm[:], b4(0), Dm[:], ALU.mult, pb[0].k() + Dm.k(), Mm.k())
                k.tt(Mm[:], Mm[:], nbm[:], ALU.mult, Mm.k() + nbm.k(), Mm.k())
                k.tt(iT[:], b4(1), DTm[:], ALU.mult, pb[1].k() + DTm.k(), iT.k())
                k.tt(iT[:], iT[:], bcm(U_[ss_], 4), ALU.mult, iT.k() + cst.k(), iT.k())
                for h in range(4):
                    k.tr(pb[3][:, h * 128:(h + 1) * 128], Mm[:, h, :], ident, Mm.k() + cst.k(), pb[3].k())
                k.cp(MT[:], b4(3), pb[3].k(), MT.k())
                k.tt(TTm[:], MT[:], bcm(ident, 4), ALU.add, MT.k() + cst.k(), TTm.k())
                k.cp(Mb[:], Mm[:], Mm.k(), Mb.k(), eng="pool")
                k.cp(MTb[:], b4(3), pb[3].k(), MTb.k(), eng="act")
                k.cp(TTb[:], TTm[:], TTm.k(), TTb.k(), eng="act")
                nit = 2 if ss_ else 6
                for it in range(1, nit + 1):
                    for h in range(4):
                        k.mm(pb[0][:, h * 128:(h + 1) * 128], MTb[:, h, :], Mb[:, h, :], True, True,
                             MTb.k() + Mb.k(), pb[0].k())
                    if it < nit:
                        for h in range(4):
                            k.mm(pb[1][:, h * 128:(h + 1) * 128], Mb[:, h, :], MTb[:, h, :], True, True,
                                 MTb.k() + Mb.k(), pb[1].k())
                    k.cp(Mb[:], b4(0), pb[0].k(), Mb.k())
                    if it < nit:
                        k.cp(MTb[:], b4(1), pb[1].k(), MTb.k(), eng="act")
                    for h in range(4):
                        k.mm(pb[2][:, h * 128:(h + 1) * 128], Mb[:, h, :], TTb[:, h, :], True, True,
                             Mb.k() + TTb.k(), pb[2].k())
                    k.tt(TTm[:], TTm[:], b4(2), ALU.add, TTm.k() + pb[2].k(), TTm.k())
                    if it < nit:
                        k.cp(TTb[:], TTm[:], TTm.k(), TTb.k(), eng="act")
                k.dma("sp", gbb[:], PTv[:, 21:25, t0:t0 + 128], PT0.k(), gbb.k())
                if not ss_:
                    if n == 0:
                        k.memset(hb[:, :, 0:2], 0.0, hb.k())
                        k.memset(gcb[:, :, 0:2], 0.0, gcb.k())
                        k.dma("sp", hb[:, :, 2:130], PTv[:, 17:21, 0:128], PT0.k(), hb.k())
                        k.dma("sp", gcb[:, :, 2:130], PTv[:, 25:29, 0:128], PT0.k(), gcb.k())
                    else:
                        k.dma("sp", hb[:], PTv[:, 17:21, t0 - 2:t0 + 128], PT0.k(), hb.k())
                        k.dma("sp", gcb[:], PTv[:, 25:29, t0 - 2:t0 + 128], PT0.k(), gcb.k())
                    k.tt(mp[:], hb[:], gcb[:], ALU.mult, hb.k() + gcb.k(), mp.k())
                    mv = mp[:].unsqueeze(2)
                    mk = mp.k()
                    if n == NPC - 1:
                        store_rowsT(conv_b_p.ap(), mp[:, :, 128:130], 2, 4, conv_b_p.k(), mp.k())
                else:
                    k.dma("sp", hb[:, :, 0:128], PTv[:, 17:21, t0:t0 + 128], PT0.k(), hb.k())
                    k.dma("sp", gcb[:, :, 0:128], PTv[:, 25:29, t0:t0 + 128], PT0.k(), gcb.k())
                    k.tt(ms[:, :, :, 2:10], hb[:, :, 0:128].rearrange("p c (s j) -> p c s j", j=8),
                         gcb[:, :, 0:128].rearrange("p c (s j) -> p c s j", j=8), ALU.mult,
                         hb.k() + gcb.k(), ms.k())
                    k.cp(ms[:, :, :, 0:2], histb[:].rearrange("p c (s j) -> p c s j", j=2), histb.k(), ms.k())
                    mv = ms[:]
                    mk = ms.k()
                    k.cp(cbtmp[:].rearrange("p c (s j) -> p c s j", j=2), ms[:, :, :, 8:10], ms.k(), cbtmp.k())
                    store_rowsT(conv_b_s.ap(), cbtmp[:], 32, 4, conv_b_s.k(), cbtmp.k())
                y4 = yb[:].rearrange("p c (s j) -> p c s j", j=L)
                y24 = yb2[:].rearrange("p c (s j) -> p c s j", j=L)
                for i in range(3):
                    wv = wcb[:, :, i].unsqueeze(2).unsqueeze(3).to_broadcast([128, 4, nseq, L])
                    k.tt(y4 if i == 0 else y24, mv[:, :, :, i:i + L], wv, ALU.mult, mk + wcb.k(),
                         (yb if i == 0 else yb2).k())
                    if i:
                        k.tt(yb[:], yb[:], yb2[:], ALU.add, yb.k() + yb2.k(), yb.k())
                k.tt(ob2[:], yb[:], gbb[:], ALU.mult, yb.k() + gbb.k(), ob2.k())
                k.dma("sp", OTv[:, 4:8, t0:t0 + 128], ob2[:], ob2.k(), OT.k(n))

            def P2(n):
                ss_, t0, (nseq, L), S = chunk_params(n)
                Hn = HS[n % 2]
                qn, kbg, ktl, vb, TTm, iT, Ee = Hn.qn, Hn.kbg, Hn.ktl, Hn.vb, Hn.TTm, Hn.iT, Hn.Ee
                for h in range(4):
                    k.mm(pb[5][:, h * 128:(h + 1) * 128], kbg[:, h, :], TTm[:, h, :], True, True,
                         kbg.k() + TTm.k(), pb[5].k())
                k.ts(wTn[:], b4(5), -1.0, None, ALU.mult, None, pb[5].k(), wTn.k())
                for h in range(4):
                    o_ = pb[6][:, h * 128:(h + 1) * 128]
                    k.mm(o_, vb[:, h, :], TTm[:, h, :], True, False, vb.k() + TTm.k(), pb[6].k())
                    for j in range(nseq):
                        k.mm(pb[6][:, h * 128 + j * L:h * 128 + (j + 1) * L], S[:, j, h, :],
                             wTn[:, h, j * L:(j + 1) * L], False, j == nseq - 1, S.k() + wTn.k(), pb[6].k())
                k.cp(vnT[:], b4(6), pb[6].k(), vnT.k())
                for h in range(4):
                    k.tr(pb[7][:, h * 128:(h + 1) * 128], vnT[:, h, :], ident, vnT.k() + cst.k(), pb[7].k())
                k.cp(vnew[:], b4(7), pb[7].k(), vnew.k(), eng="act")
                k.tt(qd[:], qn[:], Ee[:], ALU.mult, qn.k() + Ee.k(), qd.k())
                for h in range(4):
                    o_ = pb[5][:, h * 128:(h + 1) * 128]
                    k.mm(o_, vnew[:, h, :], iT[:, h, :], True, False, vnew.k() + iT.k(), pb[5].k())
                    for j in range(nseq):
                        k.mm(pb[5][:, h * 128 + j * L:h * 128 + (j + 1) * L], S[:, j, h, :],
                             qd[:, h, j * L:(j + 1) * L], False, j == nseq - 1, S.k() + qd.k(), pb[5].k())
                if not ss_:
                    for h in range(4):
                        k.mm(pb[6][:, h * 128:(h + 1) * 128], ktl[:, h, :], vnew[:, h, :], True, True,
                             ktl.k() + vnew.k(), pb[6].k())
                    k.tt(Sp[:, 0], Sp[:, 0], bc3(dl[:, n, :], 128), ALU.mult, Sp.k() + dl.k(), Sp.k())
                    k.tt(Sp[:, 0], Sp[:, 0], b4(6), ALU.add, Sp.k() + pb[6].k(), Sp.k())
                else:
                    for h in range(4):
                        k.tt(VM[:], bcm(vnew[:, h, :], 16), bc3(cst[:, C_SEQM, 0:16], 128), ALU.mult,
                             vnew.k() + cst.k(), VM.k())
                        for q4 in range(4):
                            bank = pb[6 + q4 % 2]
                            k.mm(bank[:], ktl[:, h, :], VM[:, q4 * 4:q4 * 4 + 4, :].rearrange("p s v -> p (s v)"), True, True,
                                 ktl.k() + VM.k(), bank.k())
                            sv = Ss[:, q4 * 4:q4 * 4 + 4, h, :]
                            k.tt(sv, sv, bc3(dls[:, q4 * 4:q4 * 4 + 4, h], 128), ALU.mult, Ss.k() + dls.k(), Ss.k())
                            k.tt(sv, sv, bank[:].rearrange("p (s v) -> p s v", v=128), ALU.add,
                                 Ss.k() + bank.k(), Ss.k())
                k.act(osq[:], b4(5), AF.Square, pb[5].k(), osq.k())
                for h in range(4):
                    k.mm(pb[7][:, h * 128:(h + 1) * 128], cst[:, C_ONES, :], osq[:, h, :], True, True,
                         cst.k() + osq.k(), pb[7].k())
                k.act(r2[:], b4(7), AF.Sqrt, pb[7].k(), r2.k(), scale=1.0 / 128, bias=EPS)
                k.recip(r2[:], r2[:], r2.k(), r2.k())
                k.dma("sp", zT[:], PTv[:, 12:16, t0:t0 + 128], PT0.k(), zT.k())
                k.act(zT[:], zT[:], AF.Silu, zT.k(), zT.k())
                k.stt(o1[:], b4(5), an[:, 0, 0:1], r2[:], ALU.mult, ALU.mult, pb[5].k() + an.k() + r2.k(), o1.k())
                k.tt(ob[:], o1[:], zT[:], ALU.mult, o1.k() + zT.k(), ob.k())
                k.dma("sp", OTv[:, 0:4, t0:t0 + 128], ob[:], ob.k(), OT.k(n))

            def zipped(fa, fb):
                la = k.capture(fa)
                lb = k.capture(fb)
                out = []
                for i_ in range(max(len(la), len(lb))):
                    if i_ < len(la):
                        out.append(la[i_])
                    if i_ < len(lb):
                        out.append(lb[i_])
                k.ops.extend(out)

            P1(0)
            for n in range(NCH):
                if n + 1 < NCH:
                    zipped(lambda: P2(n), lambda: P1(n + 1))
                else:
                    P2(n)
            k.dma("sp", delta_p.ap().rearrange("h k v -> k h v"), Sp[:, 0], Sp.k(), delta_p.k())
            k.dma("sp", delta_s.ap().rearrange("s h k v -> k s h v"), Ss[:], Ss.k(), delta_s.k())


    def gelu_tanh(dst, src, t1, keys_src, keys_dst, keys_t1, eng="dve"):
        k.tt(t1, src, src, ALU.mult, keys_src, keys_t1, eng=eng)
        k.ts(t1, t1, 0.044715, 1.0, ALU.mult, ALU.add, keys_t1, keys_t1, eng=eng)
        k.tt(t1, t1, src, ALU.mult, keys_t1 + keys_src, keys_t1, eng=eng)
        k.act(t1, t1, AF.Sigmoid, keys_t1, keys_t1, scale=1.5957691216057308)
        k.tt(dst, src, t1, ALU.mult, keys_src + keys_t1, keys_dst, eng=eng)

    def win1_stage():
        with k.scope():
            wi = k.sbuf("wi1", [128, 8, 2328], BF16)
            for c in range(8):
                r = slice(c * 128, (c + 1) * 128)
                for hc in range(4):
                    k.dma("pool", wi[:, c, hc * 128:hc * 128 + 64], od_w_in[r, hc * 64:hc * 64 + 64],
                          od_w_in.k(), wi.k())
                    k.dma("pool", wi[:, c, hc * 128 + 64:hc * 128 + 128], od_w_in[r, (4 + hc) * 64:(5 + hc) * 64],
                          od_w_in.k(), wi.k())
                k.dma("pool", wi[:, c, 512:2328], od_w_in[r, 512:2328], od_w_in.k(), wi.k())
            lng = k.sbuf("lng", [128, 512], F32)
            lnb = k.sbuf("lnb", [128, 512], F32)
            k.dma("sp", lng[:], od_d_ln_g.ap().partition_broadcast(128).rearrange("p a b -> p (a b)"), [], lng.k())
            k.dma("sp", lnb[:], od_d_ln_b.ap().partition_broadcast(128).rearrange("p a b -> p (a b)"), [], lnb.k())
            xTs = [k.sbuf("xT", [128, 8, TT], F32) for _ in range(2)]
            hTs = [k.sbuf("hT", [128, 8, TT], BF16) for _ in range(2)]
            sqb = k.sbuf("sqb", [128, 8, TT], BF16)
            rstd = k.sbuf("rstd", [128, TT], F32)

            def pro(i):
                xT, hT = xTs[i % 2], hTs[i % 2]
                load_xT(xT, i)
                rms_rstd(xT[:], TT, 8, rstd, sqb, 1.0 / D, xT.k(), pb[7])
                for c in range(8):
                    k.stt(hT[:, c, :], xT[:, c, :], gT[:, 8, c:c + 1], rstd[:], ALU.mult, ALU.mult,
                          xT.k() + gT.k() + rstd.k(), hT.k())
            fst = k.sbuf("fst", [128, 8, TT], BF16)
            ust = k.sbuf("ust", [128, 512], F32)
            ut1 = k.sbuf("ut1", [128, 512], F32)
            kvst = k.sbuf("kvst", [128, 768], F32)
            gst = k.sbuf("gst", [128, 24], F32)
            vst = k.sbuf("vst", [128, 512], F32)
            vt1 = k.sbuf("vt1", [128, 512], F32)
            st1 = k.sbuf("st1", [128, 4], F32)
            FTv = FT.ap().rearrange("(c p) t -> p c t", p=128)
            fchunks = [0, 128, 256, 384, 536, 664, 792, 1048]
            pro(0)
            for i in range(NT):
                hT = hTs[i % 2]
                for j, c0 in enumerate(fchunks):
                    bank = pb[j % 4]
                    for c in range(8):
                        k.mm(bank[:, :TT], wi[:, c, c0:c0 + 128], hT[:, c, :], c == 0, c == 7,
                             wi.k() + hT.k(), bank.k())
                    if j < 4:
                        k.act(fst[:, j, :], bank[:, :TT], AF.Copy, bank.k(), fst.k(), scale=0.125)
                    else:
                        k.cp(fst[:, j, :], bank[:, :TT], bank.k(), fst.k())
                k.dma("sp", FTv[:, :, i * TT:(i + 1) * TT], fst[:], fst.k(), FT.k(i))
                if i + 1 < NT:
                    pro(i + 1)
                for j in range(TT // 128):
                    b = i * (TT // 128) + j
                    t0 = b * 128
                    hs = hT[:, :, j * 128:(j + 1) * 128]
                    for (bank, c0, n) in ((pb[4], 536, 512), (pb[5], 1048, 256), (pb[6], 1816, 512), (pb[5], 512, 24)):
                        off = 256 if c0 == 512 else 0
                        for c in range(8):
                            k.mm(bank[:, off:off + n], hs[:, c, :], wi[:, c, c0:c0 + n], c == 0, c == 7,
                                 wi.k() + hT.k(), bank.k())
                    k.cp(kvst[:, 0:512], pb[4][:, 0:512], pb[4].k(), kvst.k())
                    k.cp(kvst[:, 512:768], pb[5][:, 0:256], pb[5].k(), kvst.k(), eng="act")
                    k.act(gst[:], pb[5][:, 256:280], AF.Sigmoid, pb[5].k(), gst.k())
                    k.dma("sp", KVtok[t0:t0 + 128, :], kvst[:], kvst.k(), KVtok.k(b))
                    k.dma("sp", GTok[t0:t0 + 128, :], gst[:], gst.k(), GTok.k(b))
                    if t0 < TP:
                        k.dma("sp", cmp_p[t0:t0 + 128, :], kvst[:, 0:256], kvst.k(), cmp_p.k())
                        k.dma("sp", sel_p[t0:t0 + 128, :], kvst[:, 256:512], kvst.k(), sel_p.k())
                        if t0 >= TP - 512:
                            w0 = t0 - (TP - 512)
                            k.dma("sp", win_p[w0:w0 + 128, :], kvst[:, 512:768], kvst.k(), win_p.k())
                    else:
                        k.dma("sp", cmp_s.ap(), kvst[:, 0:256], kvst.k(), cmp_s.k())
                        k.dma("sp", sel_s.ap(), kvst[:, 256:512], kvst.k(), sel_s.k())
                        for sq_ in range(NS):
                            k.dma("sp", win_s[sq_, 504:512, :], kvst[sq_ * 8:(sq_ + 1) * 8, 512:768],
                                  kvst.k(), win_s.k(1))
                    for c in range(8):
                        k.mm(pb[3][:, 0:512], hs[:, c, :], wi[:, c, 1304:1816], c == 0, c == 7,
                             wi.k() + hT.k(), pb[3].k())
                    k.cp(ust[:], pb[3][:, 0:512], pb[3].k(), ust.k(), eng="act")
                    gelu_tanh(ust[:], ust[:], ut1[:], ust.k(), ust.k(), ut1.k(), eng="pool")
                    k.dma("sp", UTok[t0:t0 + 128, :], ust[:], ust.k(), UTok.k(b))
                    k.cp(vst[:], pb[6][:, 0:512], pb[6].k(), vst.k())
                    gelu_tanh(vst[:], vst[:], vt1[:], vst.k(), vst.k(), vt1.k())
                    k.op("dve", lambda: nc.vector.reduce_sum(out=st1[:, 0:1], in_=vst[:], axis=AX.X), vst.k(), st1.k())
                    k.ts(st1[:, 0:1], st1[:, 0:1], 1.0 / 512, None, ALU.mult, None, st1.k(), st1.k())
                    k.ts(vst[:], vst[:], st1[:, 0:1], None, ALU.subtract, None, vst.k() + st1.k(), vst.k())
                    k.tt(vt1[:], vst[:], vst[:], ALU.mult, vst.k(), vt1.k())
                    k.op("dve", lambda: nc.vector.reduce_sum(out=st1[:, 1:2], in_=vt1[:], axis=AX.X), vt1.k(), st1.k())
                    k.act(st1[:, 2:3], st1[:, 1:2], AF.Sqrt, st1.k(), st1.k(), scale=1.0 / 512, bias=EPS)
                    k.recip(st1[:, 3:4], st1[:, 2:3], st1.k(), st1.k())
                    k.stt(vst[:], vst[:], st1[:, 3:4], lng[:], ALU.mult, ALU.mult, vst.k() + st1.k() + lng.k(), vst.k())
                    k.tt(vst[:], vst[:], lnb[:], ALU.add, vst.k() + lnb.k(), vst.k())
                    k.dma("sp", VTok[t0:t0 + 128, :], vst[:], vst.k(), VTok.k(b))
                    if t0 >= TP:
                        k.dma("sp", dv_s.ap(), vst[:], vst.k(), dv_s.k())
            wst = k.sbuf("wst", [126, NS, 4, 256], F32)
            k.dma("sp", wst[:], cache_win[:, 8:512, :].rearrange("s (p j) c -> p s j c", j=4), [], wst.k())
            k.dma("sp", win_s[:, 0:504, :].rearrange("s (p j) c -> p s j c", j=4), wst[:], wst.k(), win_s.k(0))


    def cmlp_stage():
        NB = Ttot // 128
        with k.scope():
            wct = k.sbuf("wct", [128, 8, 128], BF16)
            wtmp = k.sbuf("wtmp", [128, 8, 128], F32)
            k.dma("sp", wtmp[:], od_d_ws.ap().rearrange("g t s -> t g s"), [], wtmp.k())
            for g in range(8):
                bank = pb[g // 4]
                k.tr(bank[:, (g % 4) * 128:(g % 4 + 1) * 128], wtmp[:, g, :], ident, wtmp.k() + cst.k(), bank.k())
            for hf in range(2):
                k.tt(wct[:, hf * 4:hf * 4 + 4, :], pb[hf][:].rearrange("p (g t) -> p g t", g=4),
                     bcm(cst[:, C_U, :], 4), ALU.mult, pb[hf].k() + cst.k(), wct.k())
            bsT = k.sbuf("bsT", [128, 1, 8], F32)
            load_colsT(bsT, od_d_bs.ap(), 8, 1, pb[2])
            vt = k.sbuf("vt", [128, 512], F32)
            vtb = k.sbuf("vtb", [128, 512], BF16)
            ut = k.sbuf("ut", [128, 512], F32)
            od = k.sbuf("od", [128, 512], F32)
            for b in range(TP // 128):
                t0 = b * 128
                k.dma("sp", vt[:], VTok[t0:t0 + 128, :], VTok.k(b), vt.k())
                k.dma("sp", ut[:], UTok[t0:t0 + 128, :], UTok.k(b), ut.k())
                k.cp(vtb[:], vt[:], vt.k(), vtb.k())
                for g in range(8):
                    k.mm(pb[3][:, g * 64:(g + 1) * 64], wct[:, g, :], vtb[:, g * 64:(g + 1) * 64], True, True,
                         wct.k() + vtb.k(), pb[3].k())
                k.tt(od[:].rearrange("p (g c) -> p g c", g=8), pb[3][:].rearrange("p (g c) -> p g c", g=8),
                     bc3(bsT[:, 0, :], 64), ALU.add, pb[3].k() + bsT.k(), od.k())
                k.tt(od[:], od[:], ut[:], ALU.mult, od.k() + ut.k(), od.k())
                k.dma("sp", OTok[t0:t0 + 128, 512:1024], od[:], od.k(), OTok.k(2 * b + 1))
            vs = k.sbuf("vs", [8, NS, 512], F32)
            vsb = k.sbuf("vsb", [8, NS, 512], BF16)
            us = k.sbuf("us", [8, NS, 512], F32)
            ods = k.sbuf("ods", [8, NS, 512], F32)
            bS = Ttot // 128 - 1
            k.dma("sp", vs[:], VTok[TP:Ttot, :].rearrange("(s j) c -> j s c", j=8), VTok.k(bS), vs.k())
            k.dma("sp", us[:], UTok[TP:Ttot, :].rearrange("(s j) c -> j s c", j=8), UTok.k(bS), us.k())
            k.cp(vsb[:], vs[:], vs.k(), vsb.k())
            for g in range(8):
                for hf in range(2):
                    bank = pb[4 + hf]
                    k.mm(bank[0:8, :].rearrange("p (s c) -> p s c", c=64), wct[0:8, g, 0:8],
                         vsb[:, hf * 8:hf * 8 + 8, g * 64:(g + 1) * 64], True, True, wct.k() + vsb.k(), bank.k())
                    k.ts(ods[:, hf * 8:hf * 8 + 8, g * 64:(g + 1) * 64],
                         bank[0:8, :].rearrange("p (s c) -> p s c", c=64), bsT[0:8, 0, g:g + 1], None,
                         ALU.add, None, bank.k() + bsT.k(), ods.k())
            k.tt(ods[:], ods[:], us[:], ALU.mult, ods.k() + us.k(), ods.k())
            k.dma("sp", OTok[TP:Ttot, 512:1024].rearrange("(s j) c -> j s c", j=8), ods[:], ods.k(),
                  OTok.k(2 * bS + 1))


    def nsa_common(pad_w1=False):
        A = Ctx()
        A.pad_w1 = pad_w1
        A.kpos = k.sbuf("kposb", [4, 4224], BF16)
        k.dma("pool", A.kpos[:], c_kpos.ap(), [], A.kpos.k())
        A.kposc = k.sbuf("kposcb", [4, 256], BF16)
        k.dma("pool", A.kposc[:], c_kposc.ap(), [], A.kposc.k())
        A.am = k.sbuf("amb", [128, 2, 128], BF16)
        k.dma("pool", A.am[:], c_amask.ap().rearrange("n p f -> p n f"), [], A.am.k())
        A.idb = k.sbuf("idb", [128, 128], BF16)
        k.cp(A.idb[:], ident, cst.k(), A.idb.k())
        A.mcs = k.sbuf("mcsb", [128, 2, 64], BF16)
        k.dma("pool", A.mcs[:], c_mcs.ap().rearrange("(j p) s -> p j s", p=128), [], A.mcs.k())
        A.w1 = k.sbuf("w1dup", [128, 2, 32, 128], BF16)
        for hf in range(2):
            k.dma("pool", A.w1[hf * 64:(hf + 1) * 64], od_cmp_w1.ap().rearrange("k s d h -> d k s h"), [], A.w1.k())
        if pad_w1:
            A.w1p = [k.sbuf("w1p%d" % g, [128, 2, 32, 128], BF16) for g in range(2)]
            for g in range(2):
                k.memset(A.w1p[g][64 * (1 - g):64 * (2 - g)], 0.0, A.w1p[g].k())
                k.dma("pool", A.w1p[g][64 * g:64 * g + 64], od_cmp_w1.ap().rearrange("k s d h -> d k s h"), [],
                      A.w1p[g].k())
        A.w2k = k.sbuf("w2k", [128, 2, 128], BF16)
        k.memset(A.w2k[:], 0.0, A.w2k.k())
        k.dma("pool", A.w2k[:, 0, 0:64], od_cmp_w2[0], [], A.w2k.k())
        k.dma("pool", A.w2k[:, 1, 64:128], od_cmp_w2[0], [], A.w2k.k())
        A.w2v = k.sbuf("w2v", [128, 64], BF16)
        k.dma("pool", A.w2v[:], od_cmp_w2[1], [], A.w2v.k())
        A.cb = k.sbuf("cbias", [128, 2], F32)
        with k.scope():
            pes = k.sbuf("pes", [64, 64], F32)
            k.dma("sp", pes[:], od_cmp_pe.ap().rearrange("k s d -> (k s) d"), [], pes.k())
            k.tr(pb[0][0:64, 0:64], pes[:], ident[:64, :64], pes.k() + cst.k(), pb[0].k())
            peT = k.sbuf("peT", [64, 64], BF16)
            k.cp(peT[:], pb[0][0:64, 0:64], pb[0].k(), peT.k())
            for kind in range(2):
                for s_ in range(32):
                    k.mm(pb[1][:, kind:kind + 1], A.w1[0:64, kind, s_, :], peT[:, kind * 32 + s_:kind * 32 + s_ + 1],
                         s_ == 0, s_ == 31, A.w1.k() + peT.k(), pb[1].k())
            k.cp(A.cb[:], pb[1][:, 0:2], pb[1].k(), A.cb.k())
        A.eT = [k.sbuf("eT%d" % i, [128, 512], BF16) for i in range(2)]
        A.ne = 0
        A.nsb = 0
        return A

    def compress(A, XTk, XTv, nh, KcT, Vc_aug, xkeys):
        Nc = nh - 1
        with k.scope():
            hid = k.sbuf("hidT", [128, 2, 2, 256], BF16)
            bt = k.sbuf("btS", [128, 256], F32)
            sm = k.sbuf("smS", [128, 256], F32)
            k.memset(hid[:], 0.0, hid.k())
            for kind, XT in ((0, XTk), (1, XTv)):
                for g in range(2):
                    for hb, bank in ((0, pb[2]), (16, pb[3])):
                        for s_ in range(16):
                            if A.pad_w1:
                                k.mm(bank[:, 0:nh], A.w1p[g][:, kind, hb + s_, :],
                                     XT[:, s_:s_ + 16 * (nh - 1) + 1:16], s_ == 0, s_ == 15,
                                     A.w1p[g].k() + xkeys, bank.k())
                            else:
                                k.mm(bank[:, 0:nh], A.w1[64 * g:64 * g + 64, kind, hb + s_, :],
                                     XT[64 * g:64 * g + 64, s_:s_ + 16 * (nh - 1) + 1:16], s_ == 0, s_ == 15,
                                     A.w1.k() + xkeys, bank.k())
                    k.cp(bt[:, 0:nh], pb[3][:, 0:nh], pb[3].k(), bt.k())
                    k.tt(sm[:, 0:Nc], pb[2][:, 0:Nc], bt[:, 1:nh], ALU.add, pb[2].k() + bt.k(), sm.k())
                    k.act(hid[:, kind, g, 0:Nc], sm[:, 0:Nc], AF.Silu, sm.k() + A.cb.k(), hid.k(),
                          bias=A.cb[:, kind:kind + 1])
            for g in range(2):
                k.mm(pb[2][:, 0:256], A.w2k[:, g, :], hid[:, 0, g, :], g == 0, g == 1, A.w2k.k() + hid.k(), pb[2].k())
            k.cp(KcT, pb[2][:, 0:256], pb[2].k(), A.kc_keys)
            if getattr(A, "kc_split", None) is not None:
                for g in range(2):
                    k.mm(pb[2][0:64, 256 * 0:256], A.w2k[:, 0, 0:64], hid[:, 0, g, :], True, True,
                         A.w2k.k() + hid.k(), pb[2].k())
                    k.cp(A.kc_split[g][0:64, :], pb[2][0:64, 0:256], pb[2].k(), A.kc_split[g].k())
            for jt in range(2):
                for g in range(2):
                    k.mm(pb[3][:, (jt * 2 + g) * 64:(jt * 2 + g + 1) * 64], hid[:, 1, g, jt * 128:(jt + 1) * 128],
                         A.w2v[:], True, True, A.w2v.k() + hid.k(), pb[3].k())
            k.cp(Vc_aug[:, :, :, 0:64], pb[3][:, 0:256].rearrange("p (j g d) -> p j g d", j=2, g=2),
                 pb[3].k(), Vc_aug.k())

    def att_tile(A, nk, Tq, terms, outs, first, last):
        ncol = 4 * Tq
        bank = pb[A.nsb % 2]
        A.nsb += 1
        o3 = bank[:nk, :ncol].rearrange("p (r t) -> p r t", r=4)
        for i_, (l, r_, keys) in enumerate(terms):
            k.mm(o3, l, r_, i_ == 0, i_ == len(terms) - 1, keys, bank.k())
        eT = A.eT[A.ne % 2]
        A.ne += 1
        k.act(eT[:nk, :ncol], bank[:nk, :ncol], AF.Exp, bank.k(), eT.k())
        att_flush(A)
        A.pend = (nk, Tq, outs, first, last, eT)

    def att_flush(A):
        if getattr(A, "pend", None) is None:
            return
        nk, Tq, outs, first, last, eT = A.pend
        A.pend = None
        for (ob, w, V, vkeys) in outs:
            for r in range(4):
                k.mm(ob[:Tq, r * w:(r + 1) * w], eT[:nk, r * Tq:(r + 1) * Tq], V, first and r == 0, last,
                     eT.k() + vkeys, ob.k(), sgc=True)

    def topk_negsel(A, W, Tq, imp_bank, rden, topc_t, nsT_dst, nsT_keys):
        imp, sc, sc2, m8, m8b, nsl = W.imp, W.sc, W.sc2, W.m8, W.m8b, W.nsl
        i3 = imp_bank[:Tq, 0:256].rearrange("p (r s) -> p r s", r=4)
        k.ts(imp[:Tq], i3[:, 0, :], rden[:Tq, 0:1], None, ALU.mult, None, imp_bank.k() + W.rdk, imp.k())
        for r in range(1, 4):
            k.stt(imp[:Tq], i3[:, r, :], rden[:Tq, r:r + 1], imp[:Tq], ALU.mult, ALU.add,
                  imp_bank.k() + W.rdk + imp.k(), imp.k())
        k.tt(sc[:Tq], imp[:Tq], topc_t[:Tq, 0, :], ALU.mult, imp.k() + W.tck, sc.k())
        k.tt(sc[:Tq], sc[:Tq], topc_t[:Tq, 1, :], ALU.add, sc.k() + W.tck, sc.k())
        k.op("dve", lambda: nc.vector.max(out=m8[:Tq], in_=sc[:Tq]), sc.k(), m8.k())
        k.op("dve", lambda: nc.vector.match_replace(out=sc2[:Tq], in_to_replace=m8[:Tq], in_values=sc[:Tq],
                                                     imm_value=-3.0e38), sc.k() + m8.k(), sc2.k())
        k.op("dve", lambda: nc.vector.max(out=m8b[:Tq], in_=sc2[:Tq]), sc2.k(), m8b.k())
        k.ts(nsl[:Tq], sc[:Tq], m8b[:Tq, 7:8], None, ALU.is_ge, None, sc.k() + m8b.k(), nsl.k())
        k.tt(nsl[:Tq], nsl[:Tq], topc_t[:Tq, 2, :], ALU.mult, nsl.k() + W.tck, nsl.k())
        k.ts(nsl[:Tq], nsl[:Tq], -1.0, 30000.0, ALU.add, ALU.mult, nsl.k(), nsl.k())
        if getattr(W, "defer_tr", False):
            W.pending_tr = (nsl, Tq, nsT_dst, nsT_keys)
            return
        k.tr(pb[7][0:64, 0:Tq], nsl[:Tq, :], ident[:Tq, :Tq], nsl.k() + cst.k(), pb[7].k())
        k.cp(nsT_dst, pb[7][0:64, 0:Tq], pb[7].k(), nsT_keys)

    def topk_finish(W):
        nsl, Tq, nsT_dst, nsT_keys = W.pending_tr
        W.pending_tr = None
        k.tr(pb[7][0:64, 0:Tq], nsl[:Tq, :], ident[:Tq, :Tq], nsl.k() + cst.k(), pb[7].k())
        k.cp(nsT_dst, pb[7][0:64, 0:Tq], pb[7].k(), nsT_keys)

    def combine(W, Tq, ob, gcol, oacc_g, firstbr, okeys):
        att_flush(W.A)
        rd = W.rden
        o3 = ob[:Tq, 0:260].rearrange("p (r w) -> p r w", w=65)
        k.ts(rd[:Tq], o3[:, :, 64], 1e-30, None, ALU.max, None, ob.k(), W.rdk)
        k.recip(rd[:Tq], rd[:Tq], W.rdk, W.rdk)
        k.tt(W.fac[:Tq], rd[:Tq], gcol, ALU.mult, W.rdk + W.gk, W.fac.k())
        if firstbr:
            k.tt(oacc_g, o3[:, :, 0:64], bc3(W.fac[:Tq], 64), ALU.mult, ob.k() + W.fac.k(), okeys)
        else:
            k.tt(W.otmp[:Tq], o3[:, :, 0:64], bc3(W.fac[:Tq], 64), ALU.mult, ob.k() + W.fac.k(), W.otmp.k())
            k.tt(oacc_g, oacc_g, W.otmp[:Tq], ALU.add, okeys + W.otmp.k(), okeys)

    def work_tiles():
        W = Ctx()
        W.imp = k.sbuf("imp", [128, 64], F32)
        W.sc = k.sbuf("sc", [128, 64], F32)
        W.sc2 = k.sbuf("sc2", [128, 64], F32)
        W.m8 = k.sbuf("m8", [128, 8], F32)
        W.m8b = k.sbuf("m8b", [128, 8], F32)
        W.nsls = [k.sbuf("nsl%d" % i, [128, 64], F32) for i in range(2)]
        W.nsl = W.nsls[0]
        W.rden_t = k.sbuf("rden", [128, 4], F32)
        W.rden = W.rden_t
        W.rdk = W.rden_t.k()
        W.fac = k.sbuf("fac", [128, 4], F32)
        W.otmp = k.sbuf("otmp", [128, 4, 64], F32)
        W.gts = [k.sbuf("gt%d" % i, [128, 24], F32) for i in range(2)]
        W.tcs = [k.sbuf("tc%d" % i, [128, 3, 64], F32) for i in range(2)]
        W.qps = [k.sbuf("qp%d" % i, [4, 2, 4, 128], BF16) for i in range(2)]
        W.oaccs = [k.sbuf("oacc%d" % i, [128, 2, 4, 64], F32) for i in range(2)]

        def sel(i):
            W.gt, W.tc, W.qp, W.oacc = W.gts[i], W.tcs[i], W.qps[i], W.oaccs[i]
            W.gk = W.gt.k()
            W.tck = W.tc.k()
        W.sel = sel
        sel(0)
        return W

    def nsa_prompt_stage():
        NQ = TP // 128
        nh = TP // 16
        with k.scope():
            A = nsa_common()
            W = work_tiles()
            W.A = A
            FTv = FT.ap().rearrange("(c p) t -> p c t", p=128)
            QA = [k.sbuf("QA%d" % g, [128, 4, TP], BF16) for g in range(2)]
            KA = [[k.sbuf("KA%d%d" % (kd, g), [128, TP], BF16) for g in range(2)] for kd in range(2)]
            KcA = [k.sbuf("KcA%d" % g, [128, 256], BF16) for g in range(2)]
            for g in range(2):
                k.memset(QA[g][64:128], 0.0, QA[g].k())
                for r in range(4):
                    h_ = 4 * g + r
                    row0 = (h_ % 4) * 128 + (h_ // 4) * 64
                    k.dma("sp", QA[g][0:64, r, :], FT[row0:row0 + 64, 0:TP], FT.k(), QA[g].k())
                k.dma("pool", QA[g][64:68], c_qpos[:, g, :, 0:TP], [], QA[g].k())
                for kd in range(2):
                    k.memset(KA[kd][g][64:128], 0.0, KA[kd][g].k())
                    row0 = (6 + kd) * 128 + 64 * g
                    k.dma("sp", KA[kd][g][0:64, :], FT[row0:row0 + 64, 0:TP], FT.k(), KA[kd][g].k())
                    k.dma("pool", KA[kd][g][64:68, :], c_kpos[:, 0:TP], [], KA[kd][g].k())
                k.memset(KcA[g][64:128], 0.0, KcA[g].k())
                k.dma("pool", KcA[g][64:68, :], c_kposc.ap(), [], KcA[g].k())
            cmk = k.sbuf("cmk", [128, 2, TP], BF16)
            k.dma("pool", cmk[:], c_cmpmask.ap().rearrange("(j p) t -> p j t", p=128)[:, :, 0:TP], [], cmk.k())
            gx = k.sbuf("gx", [128, TP], BF16)
            k.memset(gx[64:128], 0.0, gx.k())
            k.dma("pool", gx[0:64], c_gexp[:, 0:TP], [], gx.k())
            nsT = k.sbuf("nsT", [128, 2, TP], BF16)
            k.memset(nsT[64:128], 0.0, nsT.k())
            KcT = k.sbuf("KcT", [128, 256], BF16)
            Vc = k.sbuf("Vc", [128, 2, 2, 65], BF16)
            k.memset(Vc[:, :, :, 64:65], 1.0, Vc.k())
            with k.scope():
                KX = k.sbuf("KX", [128, 2, TP], BF16)
                k.dma("sp", KX[:], FTv[:, 4:6, 0:TP], FT.k(), KX.k())
                A.kc_keys = KcT.k()
                A.kc_split = KcA
                compress(A, KX[:, 0, :], KX[:, 1, :], nh, KcT[:], Vc, KX.k())
            Vs = k.sbuf("Vs", [128, NQ, 2, 65], BF16)
            Vw = k.sbuf("Vw", [128, NQ, 2, 65], BF16)
            k.memset(Vs[:, :, :, 64:65], 1.0, Vs.k())
            k.memset(Vw[:, :, :, 64:65], 1.0, Vw.k())
            with k.scope():
                kvt = k.sbuf("kvt", [128, 768], F32)
                for j in range(NQ):
                    k.dma("sp", kvt[:], KVtok[j * 128:(j + 1) * 128, :], KVtok.k(j), kvt.k())
                    k.cp(Vs[:, j, :, 0:64], kvt[:, 384:512].rearrange("p (g d) -> p g d", g=2), kvt.k(), Vs.k())
                    k.cp(Vw[:, j, :, 0:64], kvt[:, 640:768].rearrange("p (g d) -> p g d", g=2), kvt.k(), Vw.k(),
                         eng="act")

            def ctx(qi, g):
                W.sel(qi % 2)
                t0 = qi * 128
                return (t0, QA[g][:, :, t0:t0 + 128], QA[g].k(), W.oacc[:, g],
                        W.gt[:].rearrange("p (h b) -> p h b", b=3))

            def do_cmp(qi, g):
                W.sel(qi % 2)
                t0 = qi * 128
                if g == 0:
                    k.dma("sp", W.gt[:], GTok[t0:t0 + 128, :], GTok.k(qi), W.gt.k())
                    k.dma("sp", W.tc[:], c_topc[t0:t0 + 128], [], W.tc.k())
                t0, Qg, qk, oacc_g, g3 = ctx(qi, g)
                ntile = 2 if (16 * 128 + 31) <= t0 + 127 else 1
                for j in range(ntile):
                    terms = [(KcA[g][:, j * 128:(j + 1) * 128], Qg, KcA[g].k() + qk),
                             (A.idb[:], bcm(cmk[:, j, t0:t0 + 128], 4), A.idb.k() + cmk.k())]
                    att_tile(A, 128, 128, terms,
                             [(pb[2], 65, Vc[:, j, g, :], Vc.k()), (pb[3], 64, A.mcs[:, j, :], A.mcs.k())],
                             j == 0, j == ntile - 1)
                combine(W, 128, pb[2], g3[:, 4 * g:4 * g + 4, 0], oacc_g, True, W.oacc.k())
                topk_negsel(A, W, 128, pb[3], W.rden, W.tc, nsT[0:64, g, t0:t0 + 128], nsT.k())

            def do_rest(qi, g):
                t0, Qg, qk, oacc_g, g3 = ctx(qi, g)
                for j in range(qi + 1):
                    terms = [(KA[0][g][:, j * 128:(j + 1) * 128], Qg, KA[0][g].k() + qk),
                             (gx[:, j * 128:(j + 1) * 128], bcm(nsT[:, g, t0:t0 + 128], 4), gx.k() + nsT.k())]
                    if j == qi:
                        terms.append((A.idb[:], bcm(A.am[:, 0, :], 4), A.idb.k() + A.am.k()))
                    att_tile(A, 128, 128, terms, [(pb[4], 65, Vs[:, j, g, :], Vs.k())], j == 0, False)
                j0 = max(0, qi - 4)
                for j in range(j0, qi + 1):
                    terms = [(KA[1][g][:, j * 128:(j + 1) * 128], Qg, KA[1][g].k() + qk)]
                    if j == qi - 4:
                        terms.append((A.idb[:], bcm(A.am[:, 1, :], 4), A.idb.k() + A.am.k()))
                    if j == qi:
                        terms.append((A.idb[:], bcm(A.am[:, 0, :], 4), A.idb.k() + A.am.k()))
                    att_tile(A, 128, 128, terms, [(pb[5], 65, Vw[:, j, g, :], Vw.k())], j == j0, False)
                combine(W, 128, pb[4], g3[:, 4 * g:4 * g + 4, 1], oacc_g, False, W.oacc.k())
                combine(W, 128, pb[5], g3[:, 4 * g:4 * g + 4, 2], oacc_g, False, W.oacc.k())
                if g == 1:
                    k.dma("sp", OTok[t0:t0 + 128, 0:512], W.oacc[:].rearrange("p g r d -> p (g r d)"), W.oacc.k(),
                          OTok.k(2 * qi))

            items = [(qi, g) for qi in range(NQ) for g in range(2)]
            W.defer_tr = True
            W.nsl = W.nsls[0]
            do_cmp(*items[0])
            topk_finish(W)
            for n, it in enumerate(items):
                if n + 1 < len(items):
                    W.nsl = W.nsls[(n + 1) % 2]
                    do_cmp(*items[n + 1])
                do_rest(*it)
                if n + 1 < len(items):
                    topk_finish(W)

    def nsa_sample_stage():
        NPG = 16
        PAST = NPG * 128
        with k.scope():
            A = nsa_common(pad_w1=True)
            W = work_tiles()
            W.A = A
            FTv = FT.ap().rearrange("(c p) t -> p c t", p=128)
            QTs = k.sbuf("QTs", [128, 4, TS], BF16)
            k.dma("sp", QTs[:], FTv[:, 0:4, TP:Ttot], FT.k(), QTs.k())
            k.dma("pool", W.qp[:, :, :, 0:8], c_qpos[:, :, :, PAST:PAST + 8], [], W.qp.k())
            k.dma("sp", W.tc[0:8], c_topc[PAST:PAST + 8], [], W.tc.k())
            cmks = k.sbuf("cmks", [128, 8], BF16)
            k.dma("pool", cmks[:], c_cmpmask[0:128, PAST:PAST + 8], [], cmks.k())
            gxs = k.sbuf("gxs", [64, PAST + 128], BF16)
            k.dma("pool", gxs[:], c_gexp[:, 0:PAST + 128], [], gxs.k())
            pti = k.sbuf("pti", [128, NS * NPG], I32)
            ptf = k.sbuf("ptf", [128, NS * NPG], F32)
            idx = k.sbuf("idx", [128, NS * NPG], I32)
            k.dma("sp", pti[:], page_tab.ap().rearrange("(o s) g -> o (s g)", o=1).partition_broadcast(128)
                  .rearrange("p a b -> p (a b)"), [], pti.k())
            k.cp(ptf[:], pti[:], pti.k(), ptf.k())
            k.ts(ptf[:], ptf[:], 128.0, cst[:, C_IOTA, 0:1], ALU.mult, ALU.add, ptf.k() + cst.k(), ptf.k())
            k.cp(idx[:], ptf[:], ptf.k(), idx.k())
            pgc = k.sbuf("pgc", [128, NPG, 256], F32)
            pgs = k.sbuf("pgs", [128, NPG, 256], F32)
            wst = k.sbuf("wsts", [128, 4, 256], F32)
            nw = k.sbuf("nw", [8, 768], F32)
            XTk = k.sbuf("XTk", [128, PAST], BF16)
            XTv = k.sbuf("XTv", [128, PAST], BF16)
            KsT = k.sbuf("KsT", [128, PAST + 8], BF16)
            KwT = k.sbuf("KwT", [128, 520], BF16)
            Vs = k.sbuf("Vss", [128, NPG + 1, 2, 65], BF16)
            Vw = k.sbuf("Vws", [128, 5, 2, 65], BF16)
            KcT = k.sbuf("KcTs", [128, 256], BF16)
            Vc = k.sbuf("Vcs", [128, 2, 2, 65], BF16)
            nsT = k.sbuf("nsTs", [64, 8], BF16)
            k.memset(Vs[:, :, :, 64:65], 1.0, Vs.k())
            k.memset(Vw[:, :, :, 64:65], 1.0, Vw.k())
            k.memset(Vc[:, :, :, 64:65], 1.0, Vc.k())
            A.kc_keys = KcT.k()
            W.sel(0)
            A.kc_split = None
            QPk = W.qp.k()

            def transposes(src_fn, n, dst, dcol0, skeys, width=128):
                for i0 in range(0, n, 4):
                    nn = min(4, n - i0)
                    bank = pb[6 + (i0 // 4) % 2]
                    for ii in range(nn):
                        k.tr(bank[:, ii * 128:ii * 128 + width], src_fn(i0 + ii), ident[:width, :width],
                             skeys + cst.k(), bank.k())
                    k.cp(dst[:, dcol0 + i0 * 128:dcol0 + i0 * 128 + (nn - 1) * 128 + width],
                         bank[:, 0:(nn - 1) * 128 + width], bank.k(), dst.k(), eng=("act" if (i0 // 4) % 2 else "dve"))

            for sq_ in range(NS):
                r0 = TP + 8 * sq_
                for pg in range(NPG):
                    col = sq_ * NPG + pg
                    k.op("pool", (lambda pg=pg, col=col: nc.gpsimd.indirect_dma_start(
                        out=pgc[:, pg, :], out_offset=None, in_=cache_cmp.ap(),
                        in_offset=bass.IndirectOffsetOnAxis(ap=idx[:, col:col + 1], axis=0))),
                        idx.k(), pgc.k(), dma=True)
                    k.op("pool", (lambda pg=pg, col=col: nc.gpsimd.indirect_dma_start(
                        out=pgs[:, pg, :], out_offset=None, in_=cache_sel.ap(),
                        in_offset=bass.IndirectOffsetOnAxis(ap=idx[:, col:col + 1], axis=0))),
                        idx.k(), pgs.k(), dma=True)
                k.dma("sp", wst[:], cache_win[sq_].rearrange("(j p) c -> p j c", p=128), [], wst.k())
                k.dma("sp", nw[:], KVtok[r0:r0 + 8, :], KVtok.k(Ttot // 128 - 1), nw.k())
                k.dma("sp", W.gt[0:8], GTok[r0:r0 + 8, :], GTok.k(Ttot // 128 - 1), W.gt.k())
                transposes(lambda i: pgc[:, i, 0:128], NPG, XTk, 0, pgc.k())
                transposes(lambda i: pgc[:, i, 128:256], NPG, XTv, 0, pgc.k())
                transposes(lambda i: pgs[:, i, 0:128], NPG, KsT, 0, pgs.k())
                transposes(lambda i: wst[:, i, 0:128], 4, KwT, 0, wst.k())
                transposes(lambda i: nw[:, 256:384], 1, KsT, PAST, nw.k(), width=8)
                transposes(lambda i: nw[:, 512:640], 1, KwT, 512, nw.k(), width=8)
                k.cp(Vs[:, 0:NPG, :, 0:64], pgs[:, :, 128:256].rearrange("p j (g d) -> p j g d", g=2), pgs.k(), Vs.k())
                k.cp(Vs[0:8, NPG, :, 0:64], nw[:, 384:512].rearrange("p (g d) -> p g d", g=2), nw.k(), Vs.k())
                k.cp(Vw[:, 0:4, :, 0:64], wst[:, :, 128:256].rearrange("p j (g d) -> p j g d", g=2), wst.k(), Vw.k())
                k.cp(Vw[0:8, 4, :, 0:64], nw[:, 640:768].rearrange("p (g d) -> p g d", g=2), nw.k(), Vw.k())
                compress(A, XTk[:], XTv[:], PAST // 16, KcT[:], Vc, XTk.k() + XTv.k())
                g3 = W.gt[0:8].rearrange("p (h b) -> p h b", b=3)
                for g in range(2):
                    ps = slice(64 * g, 64 * g + 64)
                    Qg = QTs[ps, :, 8 * sq_:8 * sq_ + 8]
                    QPg = W.qp[:, g, :, 0:8]
                    qk = QTs.k() + QPk
                    oacc_g = W.oacc[0:8, g]
                    terms = [(KcT[ps, 0:128], Qg, KcT.k() + qk),
                             (A.kposc[:, 0:128], QPg, A.kposc.k() + qk),
                             (A.idb[:], bcm(cmks[:], 4), A.idb.k() + cmks.k())]
                    att_tile(A, 128, 8, terms, [(pb[2], 65, Vc[:, 0, g, :], Vc.k()),
                                                (pb[3], 64, A.mcs[:, 0, :], A.mcs.k())], True, True)
                    combine(W, 8, pb[2], g3[:, 4 * g:4 * g + 4, 0], oacc_g, True, W.oacc.k())
                    topk_negsel(A, W, 8, pb[3], W.rden, W.tc, nsT[:], nsT.k())
                    for j in range(NPG + 1):
                        nk = 128 if j < NPG else 8
                        c0 = j * 128
                        terms = [(KsT[ps, c0:c0 + nk], Qg, KsT.k() + qk),
                                 (A.kpos[:, c0:c0 + nk], QPg, A.kpos.k() + qk),
                                 (gxs[:, c0:c0 + nk], bcm(nsT[:], 4), gxs.k() + nsT.k())]
                        if j == NPG:
                            terms.append((A.idb[0:8, 0:8], bcm(A.am[0:8, 0, 0:8], 4), A.idb.k() + A.am.k()))
                        att_tile(A, nk, 8, terms, [(pb[4], 65, Vs[:nk, j, g, :], Vs.k())], j == 0, j == NPG)
                    combine(W, 8, pb[4], g3[:, 4 * g:4 * g + 4, 1], oacc_g, False, W.oacc.k())
                    for j in range(5):
                        nk = 128 if j < 4 else 8
                        c0 = j * 128
                        p0 = PAST - 512 + c0
                        terms = [(KwT[ps, c0:c0 + nk], Qg, KwT.k() + qk),
                                 (A.kpos[:, p0:p0 + nk], QPg, A.kpos.k() + qk)]
                        if j == 0:
                            terms.append((A.idb[:], bcm(A.am[:, 1, 0:8], 4), A.idb.k() + A.am.k()))
                        if j == 4:
                            terms.append((A.idb[0:8, 0:8], bcm(A.am[0:8, 0, 0:8], 4), A.idb.k() + A.am.k()))
                        att_tile(A, nk, 8, terms, [(pb[5], 65, Vw[:nk, j, g, :], Vw.k())], j == 0, j == 4)
                    combine(W, 8, pb[5], g3[:, 4 * g:4 * g + 4, 2], oacc_g, False, W.oacc.k())
                k.dma("sp", OTok[r0:r0 + 8, 0:512], W.oacc[0:8].rearrange("p g r d -> p (g r d)"), W.oacc.k(),
                      OTok.k(2 * (Ttot // 128 - 1)))

    finals = []
    if "s1" in stages or "all" in stages:
        ffn_stage(0, 0, 1, load_x_from_input, store_xT)
        finals += xT_d.k()
    if "s2" in stages or "all" in stages:
        win0_stage()
        finals += PT0.k() + ba0.k()
    if "s3" in stages or "all" in stages:
        mix0_stage()
        finals += OT.k() + delta_p.k() + delta_s.k() + conv_a_p.k() + conv_a_s.k() + conv_b_p.k() + conv_b_s.k()
    if "s5" in stages or "all" in stages:
        wout_stage(ev_w_out, 3)
        ffn_stage(1, 4, 5, load_xT, store_xT)
        ffn_stage(2, 6, 7, load_xT, store_xT)
        finals += xT_d.k()
    if "s7" in stages or "all" in stages:
        win1_stage()
        finals += (FT.k() + UTok.k() + KVtok.k() + GTok.k() + VTok.k() + cmp_p.k() + sel_p.k() + win_p.k()
                   + cmp_s.k() + sel_s.k() + win_s.k() + dv_s.k())
    if "s8p" in stages or "all" in stages:
        nsa_prompt_stage()
        finals += OTok.k()
    if "s8s" in stages or "all" in stages:
        nsa_sample_stage()
        finals += OTok.k()
    if "s8d" in stages or "all" in stages:
        cmlp_stage()
        finals += OTok.k()
    if "s9" in stages or "all" in stages:
        wout_stage(od_w_out, 9, o_tok=True)
        ffn_stage(3, 10, 11, load_xT, store_y_out)
        finals += y_all.k()
    k.finish(finals)
    return k


_CACHE = {}


def kernel(**inputs):
    f32 = lambda a: np.ascontiguousarray(np.asarray(a), dtype=np.float32)
    xp = f32(inputs["x_prompt"])
    xs = f32(inputs["x_sample"])
    B, TP, _ = xp.shape
    NSEQ = xs.shape[0]
    ncore = 8
    NS = NSEQ // ncore
    cc = f32(inputs["cache_cmp_kv"])
    NPOOL = cc.shape[0]
    if "k" not in _CACHE:
        _CACHE["k"] = build(TP=TP, NS=NS, NPOOL=NPOOL)
    k = _CACHE["k"]
    shared = {
        "consts": make_consts(),
        "norm_g": f32(inputs["norm_g"]).reshape(12, D),
        "ffn_gate": f32(inputs["ffn_gate"]).reshape(4, D, DFF),
        "ffn_up": f32(inputs["ffn_up"]).reshape(4, D, DFF),
        "ffn_down": f32(inputs["ffn_down"]).reshape(4, DFF, D),
        "ev_w_in": f32(inputs["ev_w_in"])[0],
        "ev_w_out": f32(inputs["ev_w_out"])[0],
        "ev_a_conv": f32(inputs["ev_a_conv"])[0],
        "ev_a_log": f32(inputs["ev_a_log"]),
        "ev_dt_bias": f32(inputs["ev_dt_bias"]),
        "ev_a_norm": f32(inputs["ev_a_norm"]),
        "ev_b_conv": f32(inputs["ev_b_conv"])[0],
        "od_w_in": f32(inputs["od_w_in"])[0],
        "od_w_out": f32(inputs["od_w_out"])[0],
        "od_cmp_pe": f32(inputs["od_cmp_pe"])[0],
        "od_cmp_w1": f32(inputs["od_cmp_w1"])[0],
        "od_cmp_w2": f32(inputs["od_cmp_w2"])[0],
        "od_d_ws": f32(inputs["od_d_ws"])[0],
        "od_d_bs": f32(inputs["od_d_bs"])[0],
        "od_d_ln_g": f32(inputs["od_d_ln_g"]),
        "od_d_ln_b": f32(inputs["od_d_ln_b"]),
        "cache_cmp": cc.reshape(NPOOL * 128, 256),
        "cache_sel": f32(inputs["cache_sel_kv"]).reshape(NPOOL * 128, 256),
    }
    shared.update(make_consts2())
    sd = f32(inputs["state_delta"])
    sca = f32(inputs["state_conv_a"])
    scb = f32(inputs["state_conv_b"])
    cw = f32(inputs["cache_win_kv"])
    pt = np.ascontiguousarray(np.asarray(inputs["page_table"]), dtype=np.int32)
    in_maps = []
    for c in range(ncore):
        sl = slice(c * NS, (c + 1) * NS)
        m = dict(shared)
        m["x_all"] = np.concatenate([xp[c % B], xs[sl].reshape(NS * 8, D)], 0)
        m["st_delta"] = sd[sl, 0]
        m["st_conv_a"] = sca[sl, 0].reshape(NS * 3, 1536)
        m["st_conv_b"] = scb[sl, 0].reshape(NS * 2, 512)
        m["cache_win"] = cw[sl, 0].reshape(NS, 512, 256)
        m["page_tab"] = pt[sl]
        in_maps.append(m)
    res = run_bass_kernel_spmd(k.nc, in_maps, core_ids=list(range(ncore)))
    R = res.results
    cat = lambda name, shp: np.concatenate([R[c][name].reshape(shp) for c in range(ncore)], 0)
    stk = lambda name, shp: np.stack([R[c][name].reshape(shp) for c in range(B)], 0)
    y_prompt = np.stack([R[c]["y_all"][:TP] for c in range(B)], 0)
    y_sample = np.concatenate([R[c]["y_all"][TP:].reshape(NS, 8, D) for c in range(ncore)], 0)
    return (y_prompt, y_sample,
            stk("delta_p", (1, 4, 128, 128)), cat("delta_s", (NS, 1, 4, 128, 128)),
            stk("conv_a_p", (1, 3, 1536)), cat("conv_a_s", (NS, 1, 3, 1536)),
            stk("conv_b_p", (1, 2, 512)), cat("conv_b_s", (NS, 1, 2, 512)),
            stk("cmp_p", (1, TP, 2, 2, 64)), cat("cmp_s", (NS, 1, 8, 2, 2, 64)),
            stk("sel_p", (1, TP, 2, 2, 64)), cat("sel_s", (NS, 1, 8, 2, 2, 64)),
            stk("win_p", (1, 512, 2, 2, 64)), cat("win_s", (NS, 1, 512, 2, 2, 64)),
            cat("dv_s", (NS, 1, 8, 512)))
```
